# Optimizing a Trainium2 kernel written in Bass

```python
import jax, jax.numpy as jnp
from jax import lax
import numpy as np

D_MODEL = 1024
BATCH = 32
SEQ = 256
DEPTH = 1
DEC_BATCH = 8
DEC_SEQ = 2048
PAST_LEN = 512

GRID_W = 64
H_RET = 4
DK_RET = 64
DV_RET = 128
H_GLA = 4
DK_GLA = 64
DV_GLA = 128
GLA_RANK = 16
GLA_TAU = 16.0
CHUNK = 64
ROPE_BASE = 10000.0
EPS = 1e-6
QK_RET = H_RET * DK_RET
W_RET = H_RET * DV_RET
QK_GLA = H_GLA * DK_GLA
W_GLA = H_GLA * DV_GLA
D_MIX = W_RET + W_GLA
D_IN_PROJ = 2 * QK_RET + 2 * W_RET + 2 * QK_GLA + 2 * W_GLA + 2 * GLA_RANK

kernel_name = "hybrid_retention_gla_diffusion_step"


def rms_norm(x):
    xf = x.astype(jnp.float32)
    return (xf * lax.rsqrt(jnp.mean(xf * xf, axis=-1, keepdims=True) + EPS)).astype(x.dtype)


def modulation(cond, w_mod, b_mod):
    m = jax.nn.silu(cond) @ w_mod + b_mod
    shift, scale, gate = jnp.split(m, 3, axis=-1)
    return shift, scale, gate


def axial_rope(n_tokens):
    rows = n_tokens // GRID_W
    rr, cc = jnp.meshgrid(jnp.arange(rows), jnp.arange(GRID_W), indexing="ij")
    rr = rr.reshape(-1).astype(jnp.float32)
    cc = cc.reshape(-1).astype(jnp.float32)
    n_freq = DK_RET // 4
    inv = ROPE_BASE ** (-jnp.arange(n_freq, dtype=jnp.float32) / n_freq)
    ang = jnp.concatenate([rr[:, None] * inv, cc[:, None] * inv], axis=-1)
    return jnp.cos(ang), jnp.sin(ang)


def apply_rope(x, cos, sin):
    x1, x2 = jnp.split(x, 2, axis=-1)
    c = cos.astype(x.dtype)
    s = sin.astype(x.dtype)
    return jnp.concatenate([x1 * c - x2 * s, x2 * c + x1 * s], axis=-1)


def _split_heads(a, n_heads):
    B, T, W = a.shape
    return a.reshape(B, T, n_heads, W // n_heads).transpose(0, 2, 1, 3)


def _merge_heads(a):
    B, H, T, d = a.shape
    return a.transpose(0, 2, 1, 3).reshape(B, T, H * d)


def chunked_gated_recurrence(q, k, v, log_g, s0):
    f32 = jnp.float32
    B, H, T, dk = q.shape
    dv = v.shape[-1]
    n = T // CHUNK
    qc = q.astype(f32).reshape(B, H, n, CHUNK, dk)
    kc = k.astype(f32).reshape(B, H, n, CHUNK, dk)
    vc = v.astype(f32).reshape(B, H, n, CHUNK, dv)
    b = jnp.cumsum(log_g.astype(f32).reshape(B, H, n, CHUNK, dk), axis=3)
    b_last = b[:, :, :, -1:, :]
    q_dec = qc * jnp.exp(b)
    k_inv = kc * jnp.exp(-b)
    k_dec = kc * jnp.exp(b_last - b)
    lower = jnp.tril(jnp.ones((CHUNK, CHUNK), dtype=bool))
    scores = jnp.where(lower, jnp.einsum("bhnid,bhnjd->bhnij", q_dec, k_inv), 0.0)
    o_intra = jnp.einsum("bhnij,bhnje->bhnie", scores, vc)
    kv = jnp.einsum("bhncd,bhnce->bhnde", k_dec, vc)
    decay = jnp.exp(b_last[:, :, :, 0, :])

    def step(s, inp):
        dec_n, kv_n = inp
        return dec_n[..., None] * s + kv_n, s

    s_final, s_starts = lax.scan(step, s0.astype(f32),
                                 (jnp.moveaxis(decay, 2, 0), jnp.moveaxis(kv, 2, 0)))
    s_starts = jnp.moveaxis(s_starts, 0, 2)
    o_inter = jnp.einsum("bhncd,bhnde->bhnce", q_dec, s_starts)
    return (o_intra + o_inter).reshape(B, H, T, dv), s_final


def bidirectional_recurrence(q, k, v, log_g_fwd, log_g_bwd, s0_fwd, s0_bwd):
    o_f, s_f = chunked_gated_recurrence(q, k, v, log_g_fwd, s0_fwd)
    flip = lambda a: jnp.flip(a, axis=2)
    o_b, s_b = chunked_gated_recurrence(flip(q), flip(k), flip(v), flip(log_g_bwd), s0_bwd)
    return o_f + flip(o_b), s_f, s_b


def mixer_layer(x, shift, scale, gate, s_ret0, s_gla0, rope,
                w_in, ret_log_decay, gla_w_alpha, gla_b_alpha, gla_norm_w, w_out):
    f32 = jnp.float32
    dt = x.dtype
    h = rms_norm(x) * (1.0 + scale) + shift
    proj = h @ w_in
    widths = (QK_RET, QK_RET, W_RET, W_RET, QK_GLA, QK_GLA, W_GLA, W_GLA)
    points = []
    acc = 0
    for w in widths:
        acc += w
        points.append(acc)
    q_r, k_r, v_r, z_r, q_g, k_g, v_g, z_g, lr = jnp.split(proj, points, axis=-1)

    q_r = _split_heads(q_r, H_RET)
    k_r = _split_heads(k_r, H_RET) * (DK_RET ** -0.5)
    if rope is not None:
        cos, sin = rope
        q_r = apply_rope(q_r, cos, sin)
        k_r = apply_rope(k_r, cos, sin)
    v_r = _split_heads(v_r, H_RET)
    B, _, T, _ = q_r.shape
    ld = ret_log_decay.astype(f32)
    g_rf = jnp.broadcast_to(ld[0][None, :, None, None], (B, H_RET, T, DK_RET))
    g_rb = jnp.broadcast_to(ld[1][None, :, None, None], (B, H_RET, T, DK_RET))
    o_r, sr_f, sr_b = bidirectional_recurrence(q_r, k_r, v_r, g_rf, g_rb, s_ret0[:, 0], s_ret0[:, 1])
    mu = jnp.mean(o_r, axis=-1, keepdims=True)
    var = jnp.mean(jnp.square(o_r - mu), axis=-1, keepdims=True)
    o_r = (o_r - mu) * lax.rsqrt(var + EPS)
    o_r = _merge_heads(o_r).astype(dt) * jax.nn.silu(z_r)

    q_g = _split_heads(q_g, H_GLA) * (DK_GLA ** -0.5)
    k_g = _split_heads(k_g, H_GLA)
    v_g = _split_heads(v_g, H_GLA)
    lr_f, lr_b = jnp.split(lr, 2, axis=-1)

    def gla_log_gate(lr_d, w_a, b_a):
        logit = (lr_d @ w_a + b_a).astype(f32)
        return _split_heads(jax.nn.log_sigmoid(logit) / GLA_TAU, H_GLA)

    o_g, sg_f, sg_b = bidirectional_recurrence(
        q_g, k_g, v_g,
        gla_log_gate(lr_f, gla_w_alpha[0], gla_b_alpha[0]),
        gla_log_gate(lr_b, gla_w_alpha[1], gla_b_alpha[1]),
        s_gla0[:, 0], s_gla0[:, 1])
    o_g = o_g * lax.rsqrt(jnp.mean(o_g * o_g, axis=-1, keepdims=True) + EPS) * gla_norm_w.astype(f32)
    o_g = _merge_heads(o_g).astype(dt) * jax.nn.silu(z_g)

    out = jnp.concatenate([o_r, o_g], axis=-1) @ w_out
    y = x + gate * out
    return y, jnp.stack([sr_f, sr_b], axis=1), jnp.stack([sg_f, sg_b], axis=1)


def setup_inputs(seed: int = 0) -> dict:
    key = jax.random.key(seed)
    ks = jax.random.split(key, 16)
    f32 = jnp.float32
    base_decay = np.log(1.0 - 2.0 ** (-5.0 - np.arange(H_RET))).astype(np.float32)
    ret_log_decay = jnp.asarray(base_decay)[None, None, :] * jnp.exp(
        0.1 * jax.random.normal(ks[9], (DEPTH, 2, H_RET), f32))
    return {
        "x_prompt": jax.random.normal(ks[0], (BATCH, SEQ, D_MODEL), f32),
        "x_sample": jax.random.normal(ks[1], (DEC_BATCH, DEC_SEQ, D_MODEL), f32),
        "c": jax.random.normal(ks[2], (DEC_BATCH, D_MODEL), f32),
        "state_ret": 0.5 * jax.random.normal(ks[3], (DEC_BATCH, DEPTH, 2, H_RET, DK_RET, DV_RET), f32),
        "state_gla": 0.5 * jax.random.normal(ks[4], (DEC_BATCH, DEPTH, 2, H_GLA, DK_GLA, DV_GLA), f32),
        "c_ctx": jax.random.normal(ks[5], (D_MODEL,), f32),
        "w_mod": jax.random.normal(ks[6], (DEPTH, D_MODEL, 3 * D_MODEL), f32) * (D_MODEL ** -0.5),
        "b_mod": 0.01 * jax.random.normal(ks[7], (DEPTH, 3 * D_MODEL), f32),
        "w_in": jax.random.normal(ks[8], (DEPTH, D_MODEL, D_IN_PROJ), f32) * (D_MODEL ** -0.5),
        "ret_log_decay": ret_log_decay,
        "gla_w_alpha": jax.random.normal(ks[10], (DEPTH, 2, GLA_RANK, QK_GLA), f32) * (GLA_RANK ** -0.5),
        "gla_b_alpha": 0.01 * jax.random.normal(ks[11], (DEPTH, 2, QK_GLA), f32),
        "gla_norm_w": 1.0 + 0.05 * jax.random.normal(ks[12], (DEPTH, DV_GLA), f32),
        "w_out": jax.random.normal(ks[13], (DEPTH, D_MIX, D_MODEL), f32) * (D_MIX ** -0.5),
        "final_norm_w": 1.0 + 0.05 * jax.random.normal(ks[14], (D_MODEL,), f32),
    }


def reference(x_prompt, x_sample, c, state_ret, state_gla, c_ctx, w_mod, b_mod, w_in,
              ret_log_decay, gla_w_alpha, gla_b_alpha, gla_norm_w, w_out, final_norm_w):
    x = x_prompt
    B_ctx = x_prompt.shape[0]
    new_ret, new_gla = [], []
    for l in range(DEPTH):
        shift, scale, gate = modulation(c_ctx[None, :], w_mod[l], b_mod[l])
        z_ret = jnp.zeros((B_ctx, 2, H_RET, DK_RET, DV_RET), jnp.float32)
        z_gla = jnp.zeros((B_ctx, 2, H_GLA, DK_GLA, DV_GLA), jnp.float32)
        x, s_r, s_g = mixer_layer(x, shift, scale, gate, z_ret, z_gla, None,
                                  w_in[l], ret_log_decay[l], gla_w_alpha[l], gla_b_alpha[l],
                                  gla_norm_w[l], w_out[l])
        new_ret.append(s_r.astype(x_prompt.dtype))
        new_gla.append(s_g.astype(x_prompt.dtype))
    y_prompt = rms_norm(x) * final_norm_w
    new_state_ret = jnp.stack(new_ret, axis=1)
    new_state_gla = jnp.stack(new_gla, axis=1)

    rope = axial_rope(x_sample.shape[1])
    xs = x_sample
    for l in range(DEPTH):
        shift, scale, gate = modulation(c[:, None, :], w_mod[l], b_mod[l])
        xs, _, _ = mixer_layer(xs, shift, scale, gate, state_ret[:, l], state_gla[:, l], rope,
                               w_in[l], ret_log_decay[l], gla_w_alpha[l], gla_b_alpha[l],
                               gla_norm_w[l], w_out[l])
    y_sample = rms_norm(xs) * final_norm_w
    return (y_prompt, y_sample, new_state_ret, new_state_gla)
```

```python
import math
from contextlib import ExitStack

import numpy as np
import ml_dtypes

import concourse.bass as bass
import concourse.mybir as mybir
from concourse.bass_utils import run_bass_kernel_spmd

F32 = mybir.dt.float32
BF16 = mybir.dt.bfloat16
ALU = mybir.AluOpType
AF = mybir.ActivationFunctionType

NCORES = 8
D = 1024
DIN = 3104
TP = 256
TS = 2048
NPS = 4
C = 128
SEG = 256
NT = SEG // C
EPS = 1e-6
LN8 = math.log(0.125)


class TT:
    __slots__ = ("name", "w", "r", "psum", "dsem", "dn")

    def __init__(self, name, psum=False):
        self.name = name
        self.w = None
        self.r = {}
        self.psum = psum
        self.dsem = None
        self.dn = 0


class Eng:
    def __init__(self, name, sem):
        self.name = name
        self.sem = sem
        self.n = 0
        self.seen = {}
        self.prog = []
        self.ninst = 0


class Trk:
    def __init__(self, sems):
        self.free_sems = list(sems)
        self.E = {nm: Eng(nm, self.free_sems.pop()) for nm in ("pe", "act", "dve", "pool", "sp")}
        self.out_events = []

    def new_sem(self):
        return self.free_sems.pop()

    def _deps(self, eng, R, W):
        need = {}

        def add(ev, raw):
            if ev is None:
                return
            sem, val, en = ev
            if en == eng.name:
                if eng.name in ("pe", "sp"):
                    return
            if need.get(id(sem), (None, 0))[1] < val:
                need[id(sem)] = (sem, val, en)

        for t in R:
            add(t.w, True)
            if t.psum:
                for ev in t.r.values():
                    add(ev, False)
        for t in W:
            add(t.w, False)
            for ev in t.r.values():
                add(ev, False)
        out = []
        for sem, val, en in need.values():
            if eng.seen.get(id(sem), 0) >= val:
                continue
            if en in self.E and val > self.E[en].n:
                raise RuntimeError(f"dependency on unsignalled instruction of {en} from {eng.name}")
            out.append((sem, val))
        return out

    def _wait(self, eng, deps):
        for sem, val in deps:
            eng.prog.append(("w", sem, val))
            eng.seen[id(sem)] = val

    def _record(self, ev, R, W):
        for t in R:
            t.r[id(ev[0])] = ev
        for t in W:
            t.w = ev
            t.r = {}

    def op(self, en, fn, R=(), W=(), sig=True):
        eng = self.E[en]
        self._wait(eng, self._deps(eng, R, W))
        eng.ninst += 1
        if sig:
            eng.prog.append(("i", fn, eng.sem, 1))
            eng.n += 1
            ev = (eng.sem, eng.n, en)
        else:
            eng.prog.append(("i", fn, None, 0))
            ev = (eng.sem, eng.n + 1, en)
        self._record(ev, R, W)

    def dma(self, qn, out, in_, R=(), W=(), st=None, is_out=False):
        eng = self.E[qn]
        self._wait(eng, self._deps(eng, R, W))
        if st is None:
            st = (list(W) + list(R))[0]
        if st.dsem is None:
            st.dsem = self.new_sem()
        eng.prog.append(("i", (mk("dma_start", out=out, in_=in_)), st.dsem, 16))
        st.dn += 1
        ev = (st.dsem, 16 * st.dn, "dma")
        self._record(ev, R, W)
        if is_out:
            self.out_events.append(ev)

    def barrier(self, dma_tiles=()):
        for eng in self.E.values():
            for t in dma_tiles:
                if t.dsem is not None and eng.seen.get(id(t.dsem), 0) < 16 * t.dn:
                    eng.prog.append(("w", t.dsem, 16 * t.dn))
                    eng.seen[id(t.dsem)] = 16 * t.dn
            for o in self.E.values():
                if o is eng or o.n == 0:
                    continue
                if eng.seen.get(id(o.sem), 0) < o.n:
                    eng.prog.append(("w", o.sem, o.n))
                    eng.seen[id(o.sem)] = o.n

    def finish(self, en="sp"):
        eng = self.E[en]
        best = {}
        for sem, val, _ in self.out_events:
            if best.get(id(sem), (None, 0))[1] < val:
                best[id(sem)] = (sem, val)
        for sem, val in best.values():
            eng.prog.append(("w", sem, val))

    def replay(self, block):
        def run(eng):
            def f(q):
                for e in eng.prog:
                    if e[0] == "w":
                        q.wait_ge(e[1], e[2])
                    else:
                        inst = e[1](q)
                        if e[2] is not None:
                            inst.then_inc(e[2], e[3])
            return f
        block.tensor(run(self.E["pe"]))
        block.scalar(run(self.E["act"]))
        block.vector(run(self.E["dve"]))
        block.gpsimd(run(self.E["pool"]))
        block.sync(run(self.E["sp"]))


def mk(name, *args, **kwargs):
    return lambda q: getattr(q, name)(*args, **kwargs)


class _Stop(Exception):
    pass


STOP = None


def build_nc(dbg=None):
    nc = bass.Bass("TRN2", target_bir_lowering=False)

    stage_cnt = {}

    def stage(name):
        if STOP is None:
            return
        stage_cnt[name] = stage_cnt.get(name, 0) + 1
        nm, _, cnt = STOP.partition(":")
        if nm == name and stage_cnt[name] >= int(cnt or 1):
            raise _Stop()

    def din(name, shape, dt=F32):
        return nc.dram_tensor(name, list(shape), dt, kind="ExternalInput").ap()

    def dout(name, shape, dt=F32):
        return nc.dram_tensor(name, list(shape), dt, kind="ExternalOutput").ap()

    xp = din("xp", [NPS * TP, D])
    xs = din("xs", [TS, D])
    cvecT = din("cvecT", [128, 16])
    st_ret = din("st_ret", [2, 4, 64, 128])
    st_gla = din("st_gla", [2, 4, 64, 128])
    w_mod = din("w_mod", [D, 3 * D])
    b_mod2 = din("b_mod2", [2, 3 * D])
    w_in = din("w_in", [D, DIN])
    ldcols = din("ldcols", [128, 4])
    wa_aug = din("wa_aug", [33, 512])
    rowsc = din("rowsc", [128, 8])
    w_out = din("w_out", [D, D])
    fnw_bc = din("fnw_bc", [128, D])
    c_idn = din("c_idn", [128, 128], BF16)
    c_tri = din("c_tri", [128, 2, 128], BF16)
    c_mask = din("c_mask", [128, 4, 128], BF16)
    c_pos = din("c_pos", [128, 2, 128])
    c_cos = din("c_cos", [128, TS])
    c_sin = din("c_sin", [128, TS])
    c_perm = din("c_perm", [128, 128], BF16)
    c_i2 = din("c_i2", [2, 2])
    c_sel = din("c_sel", [2, 2, 128])

    yp = dout("yp", [NPS * TP, D])
    ys = dout("ys", [TS, D])
    nsr = dout("nsr", [NPS, 2, 4, 64, 128])
    nsg = dout("nsg", [NPS, 2, 4, 64, 128])

    NCH = (NPS * TP + TS) // C
    sz_scr = nc.dram_tensor("sz_scr", [NCH, 128, D], BF16, kind="Internal").ap()
    op_scr = nc.dram_tensor("op_scr", [NCH, 128, D], BF16, kind="Internal").ap()
    wo1_scr = nc.dram_tensor("wo1_scr", [128, 8, D], BF16, kind="Internal").ap()
    t_szscr = [TT(f"szscr{i}") for i in range(NCH)]
    t_opscr = [TT(f"opscr{i}") for i in range(NCH)]
    t_wo1scr = TT("wo1scr")

    dbg_outs = {}
    dbg_tiles = []

    with ExitStack() as es:
        sems = [es.enter_context(nc.semaphore(f"s{i}")) for i in range(100)]
        tk = Trk(sems)

        def sb(name, shape, dt, scope=es):
            return scope.enter_context(nc.sbuf_tensor(name, list(shape), dt))

        PB_ = [es.enter_context(nc.psum_tensor(f"ps{i}", [128, 512], F32)) for i in range(8)]
        tP = [TT(f"ps{i}", True) for i in range(8)]
        PA, PB, PT, PK, PL, PS, PO0, PO1 = range(8)

        def pf(i):
            return PB_[i][:]

        def pb16(i):
            return PB_[i][:].bitcast(BF16)

        WIN = sb("WIN", [128, 8, DIN], BF16); tWIN = TT("WIN")
        PERM = sb("PERM", [128, 128], BF16)
        WOUT = sb("WOUT", [128, 8, D], BF16); tWOUT = TT("WOUT")
        COS = sb("COS", [128, TS], F32); SIN = sb("SIN", [128, TS], F32); tROPE = TT("ROPE")
        IDN = sb("IDN", [128, 128], BF16); TRI = sb("TRI", [128, 2, 128], BF16)
        MASK = sb("MASK", [128, 4, 128], BF16); tCONST = TT("CONST")
        EQR = sb("EQR", [128, 2, 2, 128], BF16); EKR = sb("EKR", [128, 2, 2, 128], BF16)
        DECR = sb("DECR", [128, 4], F32); tRETT = TT("RETT")
        WA = sb("WA", [33, 512], BF16); tWA = TT("WA")
        SHT = sb("SHT", [128, 8, 2], F32); SC1T = sb("SC1T", [128, 8, 2], F32); tMOD = TT("MOD")
        FNW = sb("FNW", [128, D], F32); tFNW = TT("FNW")
        ROWSC = sb("ROWSC", [128, 8], F32)

        def dbg_dump(name, ap, shape, dt, R):
            if dbg is None or name not in dbg:
                return
            o = nc.dram_tensor("dbg_" + name, list(shape), dt, kind="ExternalOutput").ap()
            dbg_outs[name] = "dbg_" + name
            t = TT("dbg_" + name)
            dbg_tiles.append(t)
            tk.dma("sp", o, ap, R=R, W=[t], st=t, is_out=True)

        with ExitStack() as ss:
            WM = [sb(f"WM{i}", [128, 3 * D], F32, ss) for i in range(2)]; tWM = [TT("WM0"), TT("WM1")]
            WI = [sb(f"WI{i}", [128, DIN], F32, ss) for i in range(2)]; tWI = [TT("WI0"), TT("WI1")]
            MSB = sb("MSB", [2, 3 * D], F32, ss); tMSB = TT("MSB")
            BM2 = sb("BM2", [2, 3 * D], F32, ss); tBM2 = TT("BM2")
            GATE = sb("GATE", [128, 2, D], F32, ss); tGATE = TT("GATE")
            WOS = [sb(f"WOS{i}", [128, D], F32, ss) for i in range(2)]; tWOS = [TT("WOS0"), TT("WOS1")]
            WO1 = sb("WO1", [128, 8, D], BF16, ss); tWO1 = TT("WO1"); tWO1h = [TT("WO1a"), TT("WO1b")]
            CV = sb("CV", [128, 16], F32, ss); SCT0 = sb("SILC", [128, 16], F32, ss); tCV = TT("CV")
            LD = sb("LD", [128, 4], F32, ss); NLD = sb("NLD", [128, 4], F32, ss); tLD = TT("LD")
            POS = sb("POS", [128, 2, 128], F32, ss)
            WAS = sb("WAS", [33, 512], F32, ss); tWAS = TT("WAS")
            I2 = sb("I2", [2, 2], F32, ss); SEL = sb("SEL", [2, 2, 128], F32, ss)

            tk.dma("sp", CV[:], cvecT, W=[tCV])
            tk.dma("sp", WM[0][:, 0:3 * D], w_mod[0:128, :], W=[tWM[0]])
            tk.dma("sp", WM[1][:, 0:3 * D], w_mod[128:256, :], W=[tWM[1]])
            tk.dma("sp", IDN[:], c_idn, W=[tCONST]); tk.dma("sp", TRI[:], c_tri, W=[tCONST], st=tCONST)
            tk.dma("sp", MASK[:], c_mask, W=[tCONST], st=tCONST)
            tk.dma("sp", PERM[:], c_perm, W=[tCONST], st=tCONST)
            tk.dma("sp", POS[:], c_pos, W=[tCONST], st=tCONST)
            tk.dma("sp", I2[:], c_i2, W=[tCONST], st=tCONST); tk.dma("sp", SEL[:], c_sel, W=[tCONST], st=tCONST)
            tk.dma("sp", ROWSC[:], rowsc, W=[tCONST], st=tCONST)
            tk.dma("sp", LD[:], ldcols, W=[tLD])
            tk.dma("sp", WAS[:], wa_aug, W=[tWAS])
            tk.dma("sp", BM2[:], b_mod2, W=[tBM2])
            tCONST.w = (tCONST.dsem, 16 * tCONST.dn, "dma")

            tk.op("act", mk("activation", out=SCT0[:], in_=CV[:], func=AF.Silu), R=[tCV], W=[tCV])
            tk.op("act", mk("copy", WA[:], WAS[:]), R=[tWAS], W=[tWA])
            tk.op("dve", mk("tensor_scalar", out=NLD[:], in0=LD[:], scalar1=-1.0, scalar2=None, op0=ALU.mult), R=[tLD], W=[tLD])
            for d in range(2):
                for p in range(2):
                    c_ = d * 2 + p
                    tk.op("act", mk("activation", out=EQR[:, d, p, :], in_=POS[:, d, :], func=AF.Exp,
                                                                          scale=LD[:, c_:c_ + 1]), R=[tLD, tCONST], W=[tRETT])
                    tk.op("act", mk("activation", out=EKR[:, d, p, :], in_=POS[:, d, :], func=AF.Exp,
                                                                          scale=NLD[:, c_:c_ + 1], bias=LN8), R=[tLD, tCONST], W=[tRETT])
            tk.op("act", mk("activation", out=DECR[:], in_=LD[:], func=AF.Exp, scale=float(C)), R=[tLD], W=[tRETT])
            for kk in range(2):
                tk.dma("act", WI[kk][:], w_in[kk * 128:(kk + 1) * 128, :], W=[tWI[kk]])
            for kk in range(8):
                b_ = kk % 2
                tk.op("act", mk("copy", WIN[:, kk, :], WI[b_][:]), R=[tWI[b_]], W=[tWIN])
                if kk + 2 < 8:
                    tk.dma("act", WI[b_][:], w_in[(kk + 2) * 128:(kk + 3) * 128, :], W=[tWI[b_]])
            mbanks = [PA, PB, PL, PS, PO0, PO1]
            for k in range(8):
                for n in range(6):
                    tk.op("pe", mk("matmul", pf(mbanks[n])[0:2, :], lhsT=SCT0[:, 2 * k:2 * k + 2],
                                   rhs=WM[k % 2][:, n * 512:(n + 1) * 512], start=(k == 0), stop=(k == 7)),
                          R=[tCV, tWM[k % 2]], W=[tP[mbanks[n]]], sig=(n == 5))
                if k + 2 < 8:
                    tk.dma("sp", WM[k % 2][:, 0:3 * D], w_mod[(k + 2) * 128:(k + 3) * 128, :], W=[tWM[k % 2]])
                if k == 5:
                    tk.dma("sp", WOS[0][:], w_out[0:128, :], W=[tWOS[0]])
                    tk.dma("sp", WOS[1][:], w_out[128:256, :], W=[tWOS[1]])
            for n in range(6):
                tk.op("dve", mk("tensor_tensor", out=MSB[:, n * 512:(n + 1) * 512], in0=pf(mbanks[n])[0:2, :],
                                                             in1=BM2[:, n * 512:(n + 1) * 512], op=ALU.add),
                      R=[tP[mbanks[n]], tBM2], W=[tMSB])
            for kind in range(2):
                for n in range(2):
                    bank = PB if n == 0 else PL
                    tk.op("pe", mk("matmul",
                        pf(bank), lhsT=SEL[0:2, kind, :], rhs=MSB[0:2, 2048 + n * 512:2048 + (n + 1) * 512],
                        start=True, stop=True), R=[tMSB, tCONST], W=[tP[bank]])
                    tk.op("act", mk("copy", GATE[:, kind, n * 512:(n + 1) * 512], pf(bank)),
                          R=[tP[bank]], W=[tGATE])

            for j in range(16):
                tk.op("pe", mk("matmul", pf(PA)[:, 2 * j:2 * j + 2], lhsT=MSB[0:2, j * 128:(j + 1) * 128],
                                                     rhs=I2[:], start=True, stop=True),
                      R=[tMSB, tCONST], W=[tP[PA]], sig=(j == 15))
            tk.op("dve", mk("tensor_copy", SHT[:].rearrange("p k c -> p (k c)"), pf(PA)[:, 0:16]), R=[tP[PA]], W=[tMOD])
            tk.op("dve", mk("tensor_scalar", out=SC1T[:].rearrange("p k c -> p (k c)"), in0=pf(PA)[:, 16:32],
                                                    scalar1=1.0, scalar2=None, op0=ALU.add), R=[tP[PA]], W=[tMOD])
            tk.dma("sp", COS[:], c_cos, W=[tROPE]); tk.dma("sp", SIN[:], c_sin, W=[tROPE], st=tROPE)
            tk.dma("sp", FNW[:], fnw_bc, W=[tFNW])
            for kk in range(8):
                k = kk
                tk.op("dve", mk("scalar_tensor_tensor", out=WOUT[:, k, :], in0=WOS[k % 2][:], scalar=ROWSC[:, k:k + 1],
                                in1=GATE[:, 0, :], op0=ALU.mult, op1=ALU.mult), R=[tWOS[k % 2], tGATE, tCONST], W=[tWOUT])
                tk.op("dve", mk("scalar_tensor_tensor", out=WO1[:, k, :], in0=WOS[k % 2][:], scalar=ROWSC[:, k:k + 1],
                                in1=GATE[:, 1, :], op0=ALU.mult, op1=ALU.mult), R=[tWOS[k % 2], tGATE, tCONST], W=[tWO1h[k // 4]])
                if k + 2 < 8:
                    tk.dma("sp", WOS[k % 2][:], w_out[(k + 2) * 128:(k + 3) * 128, :], W=[tWOS[k % 2]])
                if k == 3:
                    tk.dma("sp", wo1_scr[:, 0:4, :], WO1[:, 0:4, :], R=[tWO1h[0]], W=[t_wo1scr], st=tWO1h[0])
            tk.dma("sp", wo1_scr[:, 4:8, :], WO1[:, 4:8, :], R=[tWO1h[1]], W=[t_wo1scr], st=tWO1h[1])

            dbg_dump("SHT", SHT[:], [128, 8, 2], F32, [tMOD])
            dbg_dump("SC1T", SC1T[:], [128, 8, 2], F32, [tMOD])
            dbg_dump("GATE", GATE[:], [128, 2, D], F32, [tGATE])
            dbg_dump("EQR", EQR[:], [128, 2, 2, 128], BF16, [tRETT])
            tk.barrier(tWO1h + dbg_tiles)
        def _main_scope():
            KVF = sb("KVF", [128, 16, 4, 128], F32); tKVF = [TT(f"KVF{i}") for i in range(16)]
            QF = sb("QF", [128, 4, TS], BF16); tQF = [TT(f"QF{i}") for i in range(16)]
            DECF = sb("DECF", [128, 16, 4], F32); tDECF = [TT(f"DECF{i}") for i in range(16)]
            S = sb("S", [128, 4, 2, 128], F32); tS = [TT("Sf"), TT("Sb")]
            SNAP = sb("SNAP", [128, 4, 2, 128], BF16); tSNAP = [TT("SNf"), TT("SNb")]
            XT = [sb(f"XT{i}", [128, D], F32) for i in range(2)]; tXT = [TT(f"XT{i}") for i in range(2)]
            XN2 = [sb(f"XN{i}", [128, D], BF16) for i in range(NT)]; tXN2 = [TT(f"XN{i}") for i in range(NT)]
            STX = sb("STX", [128, 8], F32); tSTX = [TT("STX0"), TT("STX1")]
            HT2 = [sb(f"HT{i}", [128, 8, SEG], BF16) for i in range(2)]; tHT2 = [[TT(f"HT{i}_{t}") for t in range(NT)] for i in range(2)]
            hcur = [0]
            LRT = sb("LRT", [33, SEG], BF16); tLRT = TT("LRT")
            SPB = sb("SPB", [128, 512], BF16); tSPB = TT("SPB")
            EQG = sb("EQG", [128, 2, 2, SEG], BF16); EKG = sb("EKG", [128, 2, 2, SEG], BF16); tEG = [TT(f"EG{i}") for i in range(NT)]
            DECB = sb("DECB", [128, NT, 4], F32); tDECB = [TT(f"DECB{i}") for i in range(NT)]
            T1s = [sb(f"T1_{i}", [128, SEG], F32) for i in range(1)] * 2; T2s = [sb(f"T2_{i}", [128, SEG], F32) for i in range(1)] * 2
            tT1s = [TT("T1_0")] * 2; tT2s = [TT("T2_0")] * 2
            QRAW2 = [sb(f"QRAW{i}", [128, SEG], BF16) for i in range(2)]; tQRAW2 = [TT("QRAW0"), TT("QRAW1")]
            tT1 = TT("T1"); tT2 = TT("T2"); tRQ = TT("RQ")
            QB = sb("QB", [128, 4, SEG], BF16); KF = sb("KF", [128, 4, SEG], BF16); KB = sb("KB", [128, 4, SEG], BF16)
            tQB = [TT(f"QB{i}") for i in range(4)]; tKF = [TT(f"KF{i}") for i in range(4)]; tKB = [TT(f"KB{i}") for i in range(4)]
            VY = sb("VY", [128, NT * D], BF16)
            V = [VY[:, i * D:(i + 1) * D] for i in range(NT)]; tV = [TT(f"V{i}") for i in range(NT)]
            SZB = [sb(f"SZB{i}", [128, D], BF16) for i in range(2)]; tSZB = [TT("SZB0"), TT("SZB1")]
            OPB = [sb(f"OPB{i}", [128, D], BF16) for i in range(2)]; tOPB = [TT("OPB0"), TT("OPB1")]
            KTOK = sb("KTOK", [128, 8, 128], BF16); tKTOK = TT("KTOK")
            SCT = [sb(f"SCT{i}", [128, 4, 128], BF16) for i in range(2)]; tSCT = [TT("SCT0"), TT("SCT1")]
            ON = SPB; tON = tSPB
            OG2 = [sb("OG0", [128, D], BF16)] * 2; tOG2 = [TT("OG0")] * 2
            OGT = KTOK; tOGT = tKTOK
            Y = sb("Y", [128, D], F32); tY = TT("Y")
            YOap = VY[:].bitcast(F32)
            ETMPap = Y[:, 0:512]; tETMP = tY
            ST = sb("ST", [128, 64], F32); tST = TT("ST"); tSTr = TT("STr"); tSTg = TT("STg"); tSTy = TT("STy")
            tk.op("pool", mk("memset", LRT[32:33, :], 1.0), W=[tLRT])
            stage("setup")

            jobs = []
            seqsA = []
            for s in range(NPS):
                seqsA.append(dict(x=xp[s * TP:(s + 1) * TP, :], y=yp[s * TP:(s + 1) * TP, :], T=TP, ch0=2 * s, s0=False, so=s))
            jobs.append(dict(kind=0, rope=False, seqs=seqsA, scr0=0))
            jobs.append(dict(kind=1, rope=True, seqs=[dict(x=xs, y=ys, T=TS, ch0=0, s0=True, so=None)], scr0=8))

            def col_q(mt):
                if mt < 2:
                    return mt * 128, 256 + mt * 128
                return 1536 + (mt - 2) * 128, 1792 + (mt - 2) * 128

            def state_dram(base, typ, d, p):
                return base[d, 2 * p:2 * p + 2].rearrange("h k e -> (h k) e")

            xcount = [0]

            def load_x(xap, row0):
                i = xcount[0] % 2
                xcount[0] += 1
                tk.dma("sp", XT[i][:], xap[row0:row0 + 128, :], W=[tXT[i]])
                return i

            for job in jobs:
                kind = job["kind"]
                rope = job["rope"]
                if kind == 1:
                    tk.dma("sp", WOUT[:], wo1_scr, R=[t_wo1scr], W=[tWOUT], st=tWOUT)
                segs = []
                for sq in job["seqs"]:
                    for seg in reversed(range(sq["T"] // SEG)):
                        segs.append((sq, seg))
                rot = [PA, PB, PO0, PO1]
                rcnt = [0]

                def nbank():
                    b = rot[rcnt[0] % 4]
                    rcnt[0] += 1
                    return b

                def seg_loads(si):
                    sq_, seg_ = segs[si]
                    return [load_x(sq_["x"], (seg_ * NT + t_) * C) for t_ in range(NT)]

                def a_norm(xbs):
                    for t_ in range(NT):
                        xb = xbs[t_]
                        c0 = 4 * t_
                        tk.op("act", mk("activation", out=XN2[t_][:], in_=XT[xb][:], func=AF.Square, scale=1.0 / 32.0,
                                        accum_out=STX[:, c0:c0 + 1]), R=[tXT[xb]], W=[tXN2[t_], tSTX[t_]])
                        tk.op("act", mk("activation", out=STX[:, c0 + 1:c0 + 2], in_=STX[:, c0:c0 + 1], func=AF.Ln, bias=EPS),
                              R=[tSTX[t_]], W=[tSTX[t_]])
                        tk.op("act", mk("activation", out=STX[:, c0 + 2:c0 + 3], in_=STX[:, c0 + 1:c0 + 2], func=AF.Exp, scale=-0.5),
                              R=[tSTX[t_]], W=[tSTX[t_]])
                        tk.op("dve", mk("tensor_scalar", out=XN2[t_][:], in0=XT[xb][:], scalar1=STX[:, c0 + 2:c0 + 3], scalar2=None,
                                        op0=ALU.mult), R=[tXT[xb], tSTX[t_]], W=[tXN2[t_]])

                def a_trans(t_, hti):
                    HT = HT2[hti]; tHT = tHT2[hti]
                    for k in range(8):
                        tk.op("pe", mk("transpose", pb16(PT)[:, k * 128:(k + 1) * 128], XN2[t_][:, k * 128:(k + 1) * 128], IDN[:]),
                              R=[tXN2[t_], tCONST], W=[tP[PT]], sig=(k == 7))
                    for k in range(8):
                        if True:
                            tk.op("act", mk("activation", out=HT[:, k, t_ * C:(t_ + 1) * C], in_=pb16(PT)[:, k * 128:(k + 1) * 128],
                                            func=AF.Identity, scale=SC1T[:, k, kind:kind + 1], bias=SHT[:, k, kind:kind + 1]),
                                  R=[tP[PT], tMOD], W=[tHT[t_]])
                        else:
                            tk.op("dve", mk("tensor_scalar", out=HT[:, k, t_ * C:(t_ + 1) * C], in0=pb16(PT)[:, k * 128:(k + 1) * 128],
                                            scalar1=SC1T[:, k, kind:kind + 1], scalar2=SHT[:, k, kind:kind + 1],
                                            op0=ALU.mult, op1=ALU.add), R=[tP[PT], tMOD], W=[tHT[t_]])

                def proj_group(cols, ncols, rows_t=None, wsrc=None, tw=None):
                    b = nbank()
                    HT = HT2[hcur[0]]; tHT = tHT2[hcur[0]]
                    wsrc_ = WIN if wsrc is None else wsrc
                    tw_ = tWIN if tw is None else tw
                    for k in range(8):
                        if rows_t is None:
                            tk.op("pe", mk("matmul", pf(b)[0:ncols, 0:SEG], lhsT=wsrc_[:, k, cols:cols + ncols], rhs=HT[:, k, :],
                                           start=(k == 0), stop=(k == 7)), R=[tw_] + tHT, W=[tP[b]], sig=(k == 7))
                        else:
                            tk.op("pe", mk("matmul", pf(b)[:, 0:ncols], lhsT=HT[:, k, rows_t * C:(rows_t + 1) * C],
                                           rhs=wsrc_[:, k, cols:cols + ncols], start=(k == 0), stop=(k == 7)),
                                  R=[tw_, tHT[rows_t]], W=[tP[b]], sig=(k == 7))
                    return b

                def b_gates_vz(sq, seg, hooks):
                    b = proj_group(3072, 32)
                    tk.op("act", mk("copy", LRT[0:32, :], pf(b)[0:32, 0:SEG]), R=[tP[b]], W=[tLRT])

                    def g1(t_):
                        tk.op("pe", mk("matmul", pf(PL), lhsT=LRT[0:33, t_ * C:(t_ + 1) * C], rhs=WA[0:33, :], start=True, stop=True),
                              R=[tLRT, tWA], W=[tP[PL]])
                        tk.op("act", mk("activation", out=ETMPap, in_=pf(PL), func=AF.Exp, scale=-1.0), R=[tP[PL]], W=[tETMP])
                        tk.op("act", mk("activation", out=SPB[:], in_=ETMPap, func=AF.Ln, bias=1.0), R=[tETMP], W=[tSPB])

                    def g2(t_):
                        chl = sq["ch0"] + seg * NT + t_
                        for d in range(2):
                            for p in range(2):
                                sl = d * 2 + p
                                tk.op("pe", mk("matmul", pf(PS)[:, sl * 128:(sl + 1) * 128], lhsT=SPB[:, d * 256 + p * 128:d * 256 + (p + 1) * 128],
                                               rhs=TRI[:, d, :], start=True, stop=True), R=[tSPB, tCONST], W=[tP[PS]], sig=(sl == 3))
                        bview = pf(PS).rearrange("p (s c) -> p s c", s=4)
                        tk.op("act", mk("activation", out=EQG[:].rearrange("p d a s -> p (d a) s")[:, :, t_ * C:(t_ + 1) * C], in_=bview,
                                        func=AF.Exp, bias=LN8), R=[tP[PS]], W=[tEG[t_]])
                        tk.op("act", mk("activation", out=EKG[:].rearrange("p d a s -> p (d a) s")[:, :, t_ * C:(t_ + 1) * C], in_=bview,
                                        func=AF.Exp, scale=-1.0), R=[tP[PS]], W=[tEG[t_]])
                        tk.op("act", mk("activation", out=DECF[:, chl, 2:4].unsqueeze(2), in_=bview[:, 0:2, C - 1:C], func=AF.Exp),
                              R=[tP[PS]], W=[tDECF[chl]])
                        tk.op("act", mk("activation", out=DECB[:, t_, 2:4].unsqueeze(2), in_=bview[:, 2:4, 0:1], func=AF.Exp),
                              R=[tP[PS]], W=[tDECB[t_]])
                        tk.op("pool", mk("tensor_copy", DECF[:, chl, 0:2], DECR[:, 0:2]), R=[tRETT], W=[tDECF[chl]])
                        tk.op("pool", mk("tensor_copy", DECB[:, t_, 0:2], DECR[:, 2:4]), R=[tRETT], W=[tDECB[t_]])

                    def dgrp(t_, gi):
                        chl = sq["ch0"] + seg * NT + t_
                        sbi = chl % 2
                        cb, typ, oc = ((512, "v", 0), (2048, "v", 512), (1024, "z", 0), (2560, "z", 512))[gi]
                        bank = proj_group(cb, 512, rows_t=t_)
                        if typ == "v":
                            tk.op("act", mk("copy", V[t_][:, oc:oc + 512], pf(bank)), R=[tP[bank]], W=[tV[t_]])
                        else:
                            tk.op("act", mk("activation", out=SZB[sbi][:, oc:oc + 512], in_=pf(bank), func=AF.Silu),
                                  R=[tP[bank]], W=[tSZB[sbi]])
                        if gi == 3:
                            gch = job["scr0"] + chl
                            tk.dma("sp", sz_scr[gch], SZB[sbi][:], R=[tSZB[sbi]], W=[t_szscr[gch]], st=tSZB[sbi])

                    dgrp(0, 2); dgrp(0, 3); dgrp(1, 2); dgrp(1, 3)
                    if "norm" in hooks:
                        hooks["norm"]()
                    dgrp(0, 0); g1(0); dgrp(0, 1); dgrp(1, 0); g2(0); g1(1); dgrp(1, 1)
                    g2(1)

                def c_qk(sq, seg, hooks):
                    tok0 = seg * SEG
                    pend = []

                    def group(mt, which, cbase):
                        p = mt % 2
                        QRAW = QRAW2[0 if which == "k" else 1]; tQRAW = tQRAW2[0 if which == "k" else 1]
                        T1 = T1s[0 if which == "k" else 1]; tT1 = tT1s[0 if which == "k" else 1]
                        T2 = T2s[0 if which == "k" else 1]; tT2 = tT2s[0 if which == "k" else 1]
                        if mt < 2:
                            tabf = (EQR if which == "q" else EKR)[:, 0, p, :].unsqueeze(1).to_broadcast([128, NT, C])
                            tabb = (EQR if which == "q" else EKR)[:, 1, p, :].unsqueeze(1).to_broadcast([128, NT, C])
                            tabR = [tRETT]
                        else:
                            tabf = (EQG if which == "q" else EKG)[:, 0, p, :].rearrange("p (t c) -> p t c", t=NT)
                            tabb = (EQG if which == "q" else EKG)[:, 1, p, :].rearrange("p (t c) -> p t c", t=NT)
                            tabR = list(tEG)
                        chs = [sq["ch0"] + seg * NT + t_ for t_ in range(NT)]
                        if which == "q":
                            outf = QF[:, mt, (sq["ch0"] * C + tok0):(sq["ch0"] * C + tok0 + SEG)].rearrange("p (t c) -> p t c", t=NT)
                            outb = QB[:, mt, :].rearrange("p (t c) -> p t c", t=NT)
                            Wf = [tQF[c_] for c_ in chs]; Wb = [tQB[mt]]
                        else:
                            outf = KF[:, mt, :].rearrange("p (t c) -> p t c", t=NT)
                            outb = KB[:, mt, :].rearrange("p (t c) -> p t c", t=NT)
                            Wf = [tKF[mt]]; Wb = [tKB[mt]]
                        bq = proj_group(cbase, 128)
                        use_rope = rope and mt < 2
                        if use_rope:
                            tk.op("act", mk("copy", QRAW[:], pf(bq)[:, 0:SEG]), R=[tP[bq]], W=[tQRAW])

                        def stage2():
                            if use_rope:
                                br = nbank()
                                tk.op("pe", mk("matmul", pf(br)[:, 0:SEG], lhsT=PERM[:], rhs=QRAW[:], start=True, stop=True),
                                      R=[tCONST, tQRAW], W=[tP[br]])
                                tk.op("dve", mk("tensor_tensor", out=T1[:], in0=pf(br)[:, 0:SEG], in1=SIN[:, tok0:tok0 + SEG], op=ALU.mult),
                                      R=[tP[br], tROPE], W=[tT1])
                                tk.op("dve", mk("tensor_tensor", out=T2[:], in0=pf(bq)[:, 0:SEG], in1=COS[:, tok0:tok0 + SEG], op=ALU.mult),
                                      R=[tP[bq], tROPE], W=[tT2])
                                tk.op("dve", mk("tensor_tensor", out=T2[:], in0=T1[:], in1=T2[:], op=ALU.add), R=[tT1, tT2], W=[tT2])
                                src = T2[:].rearrange("p (t c) -> p t c", t=NT)
                                tk.op("dve", mk("tensor_tensor", out=outf, in0=src, in1=tabf, op=ALU.mult), R=[tT2] + tabR, W=Wf)
                                tk.op("dve", mk("tensor_tensor", out=outb, in0=src, in1=tabb, op=ALU.mult), R=[tT2] + tabR, W=Wb)
                            else:
                                src = pf(bq)[:, 0:SEG].rearrange("p (t c) -> p t c", t=NT)
                                tk.op("dve", mk("tensor_tensor", out=outf, in0=src, in1=tabf, op=ALU.mult), R=[tP[bq]] + tabR, W=Wf)
                                tk.op("dve", mk("tensor_tensor", out=outb, in0=src, in1=tabb, op=ALU.mult), R=[tP[bq]] + tabR, W=Wb)
                        return stage2

                    gi = 0
                    for mt in range(4):
                        cq, ck = col_q(mt)
                        for which, cbase in (("k", ck), ("q", cq)):
                            st2 = group(mt, which, cbase)
                            if pend:
                                pend.pop(0)()
                            pend.append(st2)
                            gi += 1
                            if gi == 2 and "t0" in hooks:
                                hooks["t0"]()
                            if gi == 5 and "t1" in hooks:
                                hooks["t1"]()
                    while pend:
                        pend.pop(0)()

                def e_pre(sq, seg, t):
                    cs = slice(t * C, (t + 1) * C)
                    for mt in range(4):
                        for d in range(2):
                            src = (KF if d == 0 else KB)[:, mt, cs]
                            tk.op("pe", mk("transpose", pb16(PK)[:, (mt * 2 + d) * 128:(mt * 2 + d + 1) * 128], src, IDN[:]),
                                  R=[tKF[mt], tKB[mt], tCONST], W=[tP[PK]], sig=(mt == 3 and d == 1))
                    tk.op("act", mk("copy", KTOK[:].rearrange("p s c -> p (s c)"), pb16(PK)), R=[tP[PK]], W=[tKTOK])

                def e_chunk(sq, seg, t):
                    chl = sq["ch0"] + seg * NT + t
                    obi = chl % 2
                    cs = slice(t * C, (t + 1) * C)
                    sbanks = [(PS, PK), (PA, PT)]

                    def scores(mt):
                        sct = SCT[mt % 2]; tsct = tSCT[mt % 2]
                        qf_ap = QF[:, mt, (chl * C):(chl + 1) * C]
                        for h in range(2):
                            hs = slice(h * 64, (h + 1) * 64)
                            sbank = sbanks[mt % 2][h]
                            for d in range(2):
                                kk = (KF if d == 0 else KB)[hs, mt, cs]
                                qq = qf_ap[hs, :] if d == 0 else QB[hs, mt, cs]
                                tk.op("pe", mk("matmul", pf(sbank)[:, d * 128:(d + 1) * 128], lhsT=kk, rhs=qq, start=True, stop=True),
                                      R=[tKF[mt], tKB[mt], tQB[mt], tQF[chl]], W=[tP[sbank]], sig=(d == 1))
                        for h in range(2):
                            sbank = sbanks[mt % 2][h]
                            tk.op("dve", mk("tensor_tensor", out=sct[:, 2 * h:2 * h + 2, :],
                                            in0=pf(sbank)[:, 0:256].rearrange("p (s c) -> p s c", s=2), in1=MASK[:, 2 * h:2 * h + 2, :],
                                            op=ALU.mult), R=[tP[sbank], tCONST], W=[tsct])

                    def kvp(mt):
                        for h in range(2):
                            hs = slice(h * 64, (h + 1) * 64)
                            hg = mt * 2 + h
                            vv = V[t][:, hg * 128:(hg + 1) * 128]
                            for d in range(2):
                                kvbank = PL if d == 0 else PB
                                tk.op("pe", mk("matmul", pf(kvbank)[hs, mt * 128:(mt + 1) * 128], lhsT=KTOK[:, mt * 2 + d, hs], rhs=vv,
                                               start=True, stop=True), R=[tKTOK, tV[t]], W=[tP[kvbank]], sig=(h == 1 and d == 1))

                    def omm(mt):
                        sct = SCT[mt % 2]; tsct = tSCT[mt % 2]
                        obank = PO0 if mt < 2 else PO1
                        for h in range(2):
                            hs = slice(h * 64, (h + 1) * 64)
                            hg = mt * 2 + h
                            oc = (hg % 4) * 128
                            vv = V[t][:, hg * 128:(hg + 1) * 128]
                            tk.op("pe", mk("matmul", pf(obank)[:, oc:oc + 128], lhsT=sct[:, h * 2, :], rhs=vv, start=True, stop=False),
                                  R=[tsct, tV[t]], W=[tP[obank]], sig=False)
                            tk.op("pe", mk("matmul", pf(obank)[:, oc:oc + 128], lhsT=sct[:, h * 2 + 1, :], rhs=vv, start=False, stop=False),
                                  R=[tsct, tV[t]], W=[tP[obank]], sig=False)
                            tk.op("pe", mk("matmul", pf(obank)[:, oc:oc + 128], lhsT=QB[hs, mt, cs], rhs=SNAP[hs, mt, 1, :], start=False, stop=True),
                                  R=[tQB[mt], tSNAP[1]], W=[tP[obank]], sig=True)

                    scores(0)
                    kvp(0)
                    for mt in range(4):
                        if mt + 1 < 4:
                            scores(mt + 1)
                            kvp(mt + 1)
                        omm(mt)
                    tk.op("act", mk("copy", OPB[obi][:, 0:512], pf(PO0)), R=[tP[PO0]], W=[tOPB[obi]])
                    tk.op("act", mk("copy", OPB[obi][:, 512:1024], pf(PO1)), R=[tP[PO1]], W=[tOPB[obi]])
                    gch = job["scr0"] + chl
                    tk.dma("sp", op_scr[gch], OPB[obi][:], R=[tOPB[obi]], W=[t_opscr[gch]], st=tOPB[obi])
                    tk.op("act", mk("copy", KVF[:, chl, :, :], pf(PL).rearrange("p (m e) -> p m e", m=4)), R=[tP[PL]], W=[tKVF[chl]])
                    tk.op("dve", mk("tensor_tensor", out=S[:, :, 1, :], in0=pf(PB).rearrange("p (m e) -> p m e", m=4), in1=S[:, :, 1, :], op=ALU.add),
                          R=[tP[PB], tS[1]], W=[tS[1]])
                    tk.op("pool", mk("tensor_tensor", out=S[:, :, 1, :], in0=S[:, :, 1, :], in1=DECB[:, t, :].unsqueeze(2).to_broadcast([128, 4, 128]),
                                     op=ALU.mult), R=[tS[1], tDECB[t]], W=[tS[1]])
                    tk.op("act", mk("copy", SNAP[:, :, 1, :], S[:, :, 1, :]), R=[tS[1]], W=[tSNAP[1]])

                xbs_next = seg_loads(0)
                a_norm(xbs_next)
                if len(segs) > 1:
                    xbs_next = seg_loads(1)
                for t_ in range(NT):
                    a_trans(t_, 0)
                for si, (sq, seg) in enumerate(segs):
                    nseg = sq["T"] // SEG
                    hcur[0] = si % 2
                    if seg == nseg - 1:
                        for mt in range(4):
                            if sq["s0"]:
                                base = st_ret if mt < 2 else st_gla
                                tk.dma("sp", S[:, mt, 1, :], state_dram(base, mt // 2, 1, mt % 2), W=[tS[1]])
                            else:
                                tk.op("pool", mk("memset", S[:, mt, 1, :], 0.0), W=[tS[1]])
                        tk.op("pool", mk("tensor_copy", SNAP[:, :, 1, :], S[:, :, 1, :]), R=[tS[1]], W=[tSNAP[1]])
                    has_next = si + 1 < len(segs)
                    hooks = {}
                    if has_next:
                        def hk_norm():
                            global_xbs = hooks["xbs"]
                            a_norm(global_xbs)
                        hooks["xbs"] = xbs_next
                        hooks["norm"] = hk_norm
                        hooks["t0"] = lambda nh=(si + 1) % 2: a_trans(0, nh)
                        hooks["t1"] = lambda nh=(si + 1) % 2: a_trans(1, nh)
                    b_gates_vz(sq, seg, hooks)
                    if has_next and si + 2 < len(segs):
                        xbs_next = seg_loads(si + 2)
                    c_qk(sq, seg, hooks)
                    e_pre(sq, seg, 1)
                    e_chunk(sq, seg, 1)
                    e_pre(sq, seg, 0)
                    e_chunk(sq, seg, 0)
                    if seg == 0 and sq["so"] is not None:
                        for mt in range(4):
                            dst = (nsr if mt < 2 else nsg)[sq["so"]]
                            tk.dma("sp", state_dram(dst, mt // 2, 1, mt % 2), S[:, mt, 1, :], R=[tS[1]], st=tS[1], is_out=True)


                stage("p1")
                ftiles = []
                for sq in job["seqs"]:
                    for cis in range(sq["T"] // C):
                        ftiles.append((sq, cis))
                nft = len(ftiles)
                PObanks = [(PO0, PO1), (PT, PL)]
                xb3 = {}

                def ld_x(i):
                    sq_, cis_ = ftiles[i]
                    xb3[i] = load_x(sq_["x"], cis_ * C)

                def ld_sz(i):
                    sq_, cis_ = ftiles[i]
                    chl_ = sq_["ch0"] + cis_
                    gch_ = job["scr0"] + chl_
                    bi_ = chl_ % 2
                    tk.dma("sp", SZB[bi_][:], sz_scr[gch_], R=[t_szscr[gch_]], W=[tSZB[bi_]], st=tSZB[bi_])

                def ld_op(i):
                    sq_, cis_ = ftiles[i]
                    chl_ = sq_["ch0"] + cis_
                    gch_ = job["scr0"] + chl_
                    bi_ = chl_ % 2
                    tk.dma("sp", OPB[bi_][:], op_scr[gch_], R=[t_opscr[gch_]], W=[tOPB[bi_]], st=tOPB[bi_])

                def s1(i):
                    sq, cis = ftiles[i]
                    chl = sq["ch0"] + cis
                    bi = chl % 2
                    pob = PObanks[i % 2]
                    if cis == 0:
                        for mt in range(4):
                            if sq["s0"]:
                                base = st_ret if mt < 2 else st_gla
                                tk.dma("sp", S[:, mt, 0, :], state_dram(base, mt // 2, 0, mt % 2), W=[tS[0]])
                            else:
                                tk.op("pool", mk("memset", S[:, mt, 0, :], 0.0), W=[tS[0]])
                    tk.op("act", mk("copy", SNAP[:, :, 0, :], S[:, :, 0, :]), R=[tS[0]], W=[tSNAP[0]])
                    for hg in range(8):
                        obank = pob[hg // 4]
                        hh = hg % 4
                        mt = hg // 2
                        hs = slice((hg % 2) * 64, (hg % 2 + 1) * 64)
                        tk.op("pe", mk("matmul", pf(obank)[:, hh * 128:(hh + 1) * 128], lhsT=IDN[:], rhs=OPB[bi][:, hg * 128:(hg + 1) * 128],
                                       start=True, stop=False), R=[tCONST, tOPB[bi]], W=[tP[obank]], sig=False)
                        tk.op("pe", mk("matmul", pf(obank)[:, hh * 128:(hh + 1) * 128], lhsT=QF[hs, mt, chl * C:(chl + 1) * C],
                                       rhs=SNAP[hs, mt, 0, :], start=False, stop=True),
                              R=[tQF[chl], tSNAP[0]], W=[tP[obank]], sig=(hh == 3))
                    tk.op("pool", mk("tensor_tensor", out=S[:, :, 0, :], in0=KVF[:, chl, :, :], in1=S[:, :, 0, :], op=ALU.add),
                          R=[tKVF[chl], tS[0]], W=[tS[0]])
                    tk.op("pool", mk("tensor_tensor", out=S[:, :, 0, :], in0=S[:, :, 0, :], in1=DECF[:, chl, :].unsqueeze(2).to_broadcast([128, 4, 128]),
                                     op=ALU.mult), R=[tS[0], tDECF[chl]], W=[tS[0]])
                    if cis == sq["T"] // C - 1 and sq["so"] is not None:
                        for mt in range(4):
                            dst = (nsr if mt < 2 else nsg)[sq["so"]]
                            tk.dma("sp", state_dram(dst, mt // 2, 0, mt % 2), S[:, mt, 0, :], R=[tS[0]], st=tS[0], is_out=True)

                def s2(i):
                    sq, cis = ftiles[i]
                    chl = sq["ch0"] + cis
                    bi = chl % 2
                    po_r, po_g = PObanks[i % 2]
                    OG = OG2[i % 2]; tOG = tOG2[i % 2]
                    for hh in range(4):
                        tk.op("dve", mk("bn_stats", out=ST[:, 4 + hh * 6:10 + hh * 6], in_=pf(po_r)[:, hh * 128:(hh + 1) * 128]),
                              R=[tP[po_r]], W=[tSTr])
                    for hh in range(4):
                        tk.op("act", mk("activation", out=ON[:, hh * 128:(hh + 1) * 128], in_=pf(po_g)[:, hh * 128:(hh + 1) * 128], func=AF.Square,
                                        scale=1.0 / math.sqrt(128.0), accum_out=ST[:, 48 + hh:49 + hh]), R=[tP[po_g]], W=[tON, tSTg])
                    for hh in range(4):
                        tk.op("dve", mk("bn_aggr", out=ST[:, 28 + hh * 2:30 + hh * 2], in_=ST[:, 4 + hh * 6:10 + hh * 6]), R=[tSTr], W=[tSTr])

                def s2b(i):
                    sq, cis = ftiles[i]
                    chl = sq["ch0"] + cis
                    bi = chl % 2
                    po_r, po_g = PObanks[i % 2]
                    OG = OG2[i % 2]; tOG = tOG2[i % 2]
                    mvv = ST[:, 28:36].rearrange("p (h t) -> p h t", t=2)
                    tk.op("act", mk("activation", out=ST[:, 36:40].unsqueeze(2), in_=mvv[:, :, 1:2], func=AF.Ln, bias=EPS), R=[tSTr], W=[tSTr])
                    tk.op("act", mk("activation", out=ST[:, 40:44], in_=ST[:, 36:40], func=AF.Exp, scale=-0.5), R=[tSTr], W=[tSTr])
                    tk.op("dve", mk("scalar_tensor_tensor", out=ST[:, 44:48].unsqueeze(2), in0=mvv[:, :, 0:1], scalar=-1.0, in1=ST[:, 40:44].unsqueeze(2),
                                    op0=ALU.mult, op1=ALU.mult), R=[tSTr], W=[tSTr])
                    tk.op("act", mk("activation", out=ST[:, 52:56], in_=ST[:, 48:52], func=AF.Ln, bias=EPS), R=[tSTg], W=[tSTg])
                    tk.op("act", mk("activation", out=ST[:, 56:60], in_=ST[:, 52:56], func=AF.Exp, scale=-0.5), R=[tSTg], W=[tSTg])
                    for hh in range(4):
                        tk.op("act", mk("activation", out=ON[:, hh * 128:(hh + 1) * 128], in_=pf(po_r)[:, hh * 128:(hh + 1) * 128], func=AF.Identity,
                                        scale=ST[:, 40 + hh:41 + hh], bias=ST[:, 44 + hh:45 + hh]), R=[tP[po_r], tSTr], W=[tON])
                    for hh in range(4):
                        tk.op("dve", mk("scalar_tensor_tensor", out=OG[:, 512 + hh * 128:512 + (hh + 1) * 128], in0=pf(po_g)[:, hh * 128:(hh + 1) * 128],
                                        scalar=ST[:, 56 + hh:57 + hh], in1=SZB[bi][:, 512 + hh * 128:512 + (hh + 1) * 128],
                                        op0=ALU.mult, op1=ALU.mult), R=[tP[po_g], tSTg, tSZB[bi]], W=[tOG])
                    tk.op("pool", mk("tensor_tensor", out=OG[:, 0:512], in0=ON[:], in1=SZB[bi][:, 0:512], op=ALU.mult),
                          R=[tON, tSZB[bi]], W=[tOG])

                def s3(i):
                    OG = OG2[i % 2]; tOG = tOG2[i % 2]
                    for k in range(8):
                        tk.op("pe", mk("transpose", pb16(PK)[:, k * 128:(k + 1) * 128], OG[:, k * 128:(k + 1) * 128], IDN[:]),
                              R=[tOG, tCONST], W=[tP[PK]], sig=(k == 7))
                    tk.op("act", mk("copy", OGT[:].rearrange("p k c -> p (k c)"), pb16(PK)), R=[tP[PK]], W=[tOGT])

                def s4mm(i):
                    for half in range(2):
                        bank = PA if half == 0 else PB
                        for k in range(8):
                            tk.op("pe", mk("matmul", pf(bank), lhsT=OGT[:, k, :], rhs=WOUT[:, k, half * 512:(half + 1) * 512],
                                           start=(k == 0), stop=(k == 7)), R=[tOGT, tWOUT], W=[tP[bank]], sig=(k == 7))

                def s4add(i):
                    xb = xb3[i]
                    for half in range(2):
                        bank = PA if half == 0 else PB
                        tk.op("dve", mk("tensor_tensor", out=Y[:, half * 512:(half + 1) * 512], in0=pf(bank),
                                        in1=XT[xb][:, half * 512:(half + 1) * 512], op=ALU.add), R=[tP[bank], tXT[xb]], W=[tY])

                def s4y_act(i):
                    tk.op("act", mk("activation", out=XN2[0][:], in_=Y[:], func=AF.Square, scale=1.0 / 32.0, accum_out=ST[:, 60:61]), R=[tY], W=[tXN2[0], tSTy])
                    tk.op("act", mk("activation", out=ST[:, 61:62], in_=ST[:, 60:61], func=AF.Ln, bias=EPS), R=[tSTy], W=[tSTy])
                    tk.op("act", mk("activation", out=ST[:, 62:63], in_=ST[:, 61:62], func=AF.Exp, scale=-0.5), R=[tSTy], W=[tSTy])

                def s4y_dve(i):
                    sq, cis = ftiles[i]
                    tk.op("dve", mk("scalar_tensor_tensor", out=YOap, in0=Y[:], scalar=ST[:, 62:63], in1=FNW[:], op0=ALU.mult, op1=ALU.mult),
                          R=[tY, tSTy, tFNW], W=[tV[0], tV[1]])
                    tk.dma("sp", sq["y"][cis * C:(cis + 1) * C, :], YOap, R=[tV[0], tV[1]], st=tV[0], is_out=True)
                    stage("p3")

                ld_op(0); ld_sz(0)
                if nft > 1:
                    ld_op(1)
                s1(0)
                for j in range(nft + 2):
                    if j + 2 < nft:
                        ld_op(j + 2)
                    if j + 1 < nft:
                        ld_sz(j + 1)
                    if j < nft:
                        ld_x(j)
                    if 0 <= j - 1 < nft:
                        s4mm(j - 1)
                    if j + 1 < nft:
                        s1(j + 1)
                    if j < nft:
                        s2(j)
                    if 0 <= j - 2 < nft:
                        s4y_dve(j - 2)
                    if j < nft:
                        s2b(j)
                    if 0 <= j - 1 < nft:
                        s4add(j - 1)
                    if j < nft:
                        s3(j)
                    if 0 <= j - 1 < nft:
                        s4y_act(j - 1)

        try:
            _main_scope()
        except _Stop:
            pass
        tk.finish("sp")
        with nc.Block() as block:
            tk.replay(block)
    nc._dbg_outs = dbg_outs
    nc._ninst = {k: v.ninst for k, v in tk.E.items()}
    return nc


def _host_consts():
    bf = ml_dtypes.bfloat16
    ii = np.arange(C)
    idn = np.eye(128, dtype=np.float32).astype(bf)
    tri = np.zeros((128, 2, 128), np.float32)
    tri[:, 0, :] = np.where(ii[:, None] <= ii[None, :], -1.0 / 16.0, 0.0)
    tri[:, 1, :] = np.where(ii[:, None] >= ii[None, :], -1.0 / 16.0, 0.0)
    mask = np.zeros((128, 4, 128), np.float32)
    mf = (ii[:, None] <= ii[None, :]).astype(np.float32)
    mb = (ii[:, None] >= ii[None, :]).astype(np.float32)
    for h in range(2):
        mask[:, h * 2 + 0, :] = mf
        mask[:, h * 2 + 1, :] = mb
    pos = np.zeros((128, 2, 128), np.float32)
    pos[:, 0, :] = (ii + 1)[None, :]
    pos[:, 1, :] = (C - ii)[None, :]
    t = np.arange(TS)
    rr = (t // 64).astype(np.float32)
    cc = (t % 64).astype(np.float32)
    inv = (10000.0 ** (-np.arange(16, dtype=np.float32) / 16)).astype(np.float32)
    ang = np.concatenate([rr[:, None] * inv, cc[:, None] * inv], axis=-1).astype(np.float32)
    cosT = np.cos(ang).T.astype(np.float32)
    sinT = np.sin(ang).T.astype(np.float32)
    cos128 = np.ascontiguousarray(np.tile(cosT, (4, 1)).astype(np.float32))
    sin128 = np.ascontiguousarray(np.tile(sinT, (4, 1)).astype(np.float32))
    i2 = np.eye(2, dtype=np.float32)
    perm = np.zeros((128, 128), np.float32)
    for m in range(128):
        if m % 64 < 32:
            perm[m + 32, m] = -1.0
        else:
            perm[m - 32, m] = 1.0
    sel = np.zeros((2, 2, 128), np.float32)
    sel[0, 0, :] = 1.0
    sel[1, 1, :] = 1.0
    return dict(c_idn=idn, c_tri=tri.astype(bf), c_mask=mask.astype(bf), c_pos=pos, c_cos=cos128, c_sin=sin128, c_perm=perm.astype(bf), c_i2=i2, c_sel=sel)


_NC_CACHE = {}


def kernel(x_prompt, x_sample, c, state_ret, state_gla, c_ctx, w_mod, b_mod, w_in,
           ret_log_decay, gla_w_alpha, gla_b_alpha, gla_norm_w, w_out, final_norm_w, _dbg=None):
    f = lambda a: np.ascontiguousarray(np.asarray(a, dtype=np.float32))
    x_prompt, x_sample, c, state_ret, state_gla, c_ctx = map(f, (x_prompt, x_sample, c, state_ret, state_gla, c_ctx))
    w_mod, b_mod, w_in, ret_log_decay = map(f, (w_mod, b_mod, w_in, ret_log_decay))
    gla_w_alpha, gla_b_alpha, gla_norm_w, w_out, final_norm_w = map(f, (gla_w_alpha, gla_b_alpha, gla_norm_w, w_out, final_norm_w))

    key = None if _dbg is None else tuple(sorted(_dbg))
    if key not in _NC_CACHE:
        _NC_CACHE[key] = build_nc(_dbg)
    nc = _NC_CACHE[key]
    consts = _host_consts()

    ld = ret_log_decay[0]
    ldcols = np.zeros((128, 4), np.float32)
    for d in range(2):
        for p in range(2):
            ldcols[0:64, d * 2 + p] = ld[d, 2 * p]
            ldcols[64:128, d * 2 + p] = ld[d, 2 * p + 1]
    wa_aug = np.zeros((33, 512), np.float32)
    wa_aug[0:16, 0:256] = gla_w_alpha[0, 0]
    wa_aug[16:32, 256:512] = gla_w_alpha[0, 1]
    wa_aug[32, 0:256] = gla_b_alpha[0, 0]
    wa_aug[32, 256:512] = gla_b_alpha[0, 1]
    rowsc = np.ones((128, 8), np.float32)
    rowsc[:, 4:8] = gla_norm_w[0][:, None]
    fnw_bc = np.ascontiguousarray(np.broadcast_to(final_norm_w[None, :], (128, D)))
    b_mod2 = np.ascontiguousarray(np.broadcast_to(b_mod[0][None, :], (2, 3 * D)))
    shared = dict(w_mod=w_mod[0], b_mod2=b_mod2, w_in=w_in[0], ldcols=ldcols, wa_aug=wa_aug, rowsc=rowsc,
                  w_out=w_out[0], fnw_bc=fnw_bc, **consts)

    in_maps = []
    for core in range(NCORES):
        cv = np.stack([c_ctx, c[core]], axis=0)
        cvecT = np.ascontiguousarray(cv.reshape(2, 8, 128).transpose(2, 1, 0).reshape(128, 16))
        m = dict(shared)
        m.update(xp=np.ascontiguousarray(x_prompt[core * NPS:(core + 1) * NPS].reshape(NPS * TP, D)),
                 xs=np.ascontiguousarray(x_sample[core]),
                 cvecT=cvecT,
                 st_ret=np.ascontiguousarray(state_ret[core, 0]),
                 st_gla=np.ascontiguousarray(state_gla[core, 0]))
        in_maps.append(m)

    res = run_bass_kernel_spmd(nc, in_maps, core_ids=list(range(NCORES)))
    rs = res.results
    y_prompt = np.concatenate([r["yp"].reshape(NPS, TP, D) for r in rs], axis=0).astype(np.float32)
    y_sample = np.stack([r["ys"] for r in rs], axis=0).astype(np.float32)
    new_ret = np.concatenate([r["nsr"].reshape(NPS, 1, 2, 4, 64, 128) for r in rs], axis=0).astype(np.float32)
    new_gla = np.concatenate([r["nsg"].reshape(NPS, 1, 2, 4, 64, 128) for r in rs], axis=0).astype(np.float32)
    if _dbg is not None:
        kernel._dbg_results = [{k: r[v] for k, v in nc._dbg_outs.items()} for r in rs]
    return (y_prompt, y_sample, new_ret, new_gla)
```

```python
import math
from contextlib import ExitStack

import numpy as np
import ml_dtypes

import concourse.bass as bass
import concourse.mybir as mybir
from concourse.bass_utils import run_bass_kernel_spmd

F32 = mybir.dt.float32
BF16 = mybir.dt.bfloat16
ALU = mybir.AluOpType
AF = mybir.ActivationFunctionType

NCORES = 8
D = 1024
DIN = 3104
TP = 256
TS = 2048
NPS = 4
C = 128
SEG = 256
NT = SEG // C
EPS = 1e-6
LN8 = math.log(0.125)


class TT:
    __slots__ = ("name", "w", "r", "psum", "dsem", "dn")

    def __init__(self, name, psum=False):
        self.name = name
        self.w = None
        self.r = {}
        self.psum = psum
        self.dsem = None
        self.dn = 0


class Eng:
    def __init__(self, name, sem):
        self.name = name
        self.sem = sem
        self.n = 0
        self.seen = {}
        self.prog = []
        self.ninst = 0


class Trk:
    def __init__(self, sems):
        self.free_sems = list(sems)
        self.E = {nm: Eng(nm, self.free_sems.pop()) for nm in ("pe", "act", "dve", "pool", "sp")}
        self.out_events = []

    def new_sem(self):
        return self.free_sems.pop()

    def _deps(self, eng, R, W):
        need = {}

        def add(ev, raw):
            if ev is None:
                return
            sem, val, en = ev
            if en == eng.name:
                if eng.name in ("pe", "sp"):
                    return
            if need.get(id(sem), (None, 0))[1] < val:
                need[id(sem)] = (sem, val, en)

        for t in R:
            add(t.w, True)
            if t.psum:
                for ev in t.r.values():
                    add(ev, False)
        for t in W:
            add(t.w, False)
            for ev in t.r.values():
                add(ev, False)
        out = []
        for sem, val, en in need.values():
            if eng.seen.get(id(sem), 0) >= val:
                continue
            if en in self.E and val > self.E[en].n:
                raise RuntimeError(f"dependency on unsignalled instruction of {en} from {eng.name}")
            out.append((sem, val))
        return out

    def _wait(self, eng, deps):
        for sem, val in deps:
            eng.prog.append(("w", sem, val))
            eng.seen[id(sem)] = val

    def _record(self, ev, R, W):
        for t in R:
            t.r[id(ev[0])] = ev
        for t in W:
            t.w = ev
            t.r = {}

    def op(self, en, fn, R=(), W=(), sig=True):
        eng = self.E[en]
        self._wait(eng, self._deps(eng, R, W))
        eng.ninst += 1
        if sig:
            eng.prog.append(("i", fn, eng.sem, 1))
            eng.n += 1
            ev = (eng.sem, eng.n, en)
        else:
            eng.prog.append(("i", fn, None, 0))
            ev = (eng.sem, eng.n + 1, en)
        self._record(ev, R, W)

    def dma(self, qn, out, in_, R=(), W=(), st=None, is_out=False):
        eng = self.E[qn]
        self._wait(eng, self._deps(eng, R, W))
        if st is None:
            st = (list(W) + list(R))[0]
        if st.dsem is None:
            st.dsem = self.new_sem()
        eng.prog.append(("i", (mk("dma_start", out=out, in_=in_)), st.dsem, 16))
        st.dn += 1
        ev = (st.dsem, 16 * st.dn, "dma")
        self._record(ev, R, W)
        if is_out:
            self.out_events.append(ev)

    def barrier(self, dma_tiles=()):
        for eng in self.E.values():
            for t in dma_tiles:
                if t.dsem is not None and eng.seen.get(id(t.dsem), 0) < 16 * t.dn:
                    eng.prog.append(("w", t.dsem, 16 * t.dn))
                    eng.seen[id(t.dsem)] = 16 * t.dn
            for o in self.E.values():
                if o is eng or o.n == 0:
                    continue
                if eng.seen.get(id(o.sem), 0) < o.n:
                    eng.prog.append(("w", o.sem, o.n))
                    eng.seen[id(o.sem)] = o.n

    def finish(self, en="sp"):
        eng = self.E[en]
        best = {}
        for sem, val, _ in self.out_events:
            if best.get(id(sem), (None, 0))[1] < val:
                best[id(sem)] = (sem, val)
        for sem, val in best.values():
            eng.prog.append(("w", sem, val))

    def replay(self, block):
        def run(eng):
            def f(q):
                for e in eng.prog:
                    if e[0] == "w":
                        q.wait_ge(e[1], e[2])
                    else:
                        inst = e[1](q)
                        if e[2] is not None:
                            inst.then_inc(e[2], e[3])
            return f
        block.tensor(run(self.E["pe"]))
        block.scalar(run(self.E["act"]))
        block.vector(run(self.E["dve"]))
        block.gpsimd(run(self.E["pool"]))
        block.sync(run(self.E["sp"]))


def mk(name, *args, **kwargs):
    return lambda q: getattr(q, name)(*args, **kwargs)


class _Stop(Exception):
    pass


STOP = None


def build_nc(dbg=None):
    nc = bass.Bass("TRN2", target_bir_lowering=False)

    stage_cnt = {}

    def stage(name):
        if STOP is None:
            return
        stage_cnt[name] = stage_cnt.get(name, 0) + 1
        nm, _, cnt = STOP.partition(":")
        if nm == name and stage_cnt[name] >= int(cnt or 1):
            raise _Stop()

    def din(name, shape, dt=F32):
        return nc.dram_tensor(name, list(shape), dt, kind="ExternalInput").ap()

    def dout(name, shape, dt=F32):
        return nc.dram_tensor(name, list(shape), dt, kind="ExternalOutput").ap()

    xp = din("xp", [NPS * TP, D])
    xs = din("xs", [TS, D])
    cvecT = din("cvecT", [128, 16])
    st_ret = din("st_ret", [2, 4, 64, 128])
    st_gla = din("st_gla", [2, 4, 64, 128])
    w_mod = din("w_mod", [D, 3 * D])
    b_mod2 = din("b_mod2", [2, 3 * D])
    w_in = din("w_in", [D, DIN])
    ldcols = din("ldcols", [128, 4])
    wa_aug = din("wa_aug", [33, 512])
    rowsc = din("rowsc", [128, 8])
    w_out = din("w_out", [D, D])
    fnw_bc = din("fnw_bc", [128, D])
    c_idn = din("c_idn", [128, 128], BF16)
    c_tri = din("c_tri", [128, 2, 128], BF16)
    c_mask = din("c_mask", [128, 4, 128], BF16)
    c_pos = din("c_pos", [128, 2, 128])
    c_cos = din("c_cos", [128, TS])
    c_sin = din("c_sin", [128, TS])
    c_perm = din("c_perm", [128, 128], BF16)
    c_i2 = din("c_i2", [2, 2])
    c_sel = din("c_sel", [2, 2, 128])

    yp = dout("yp", [NPS * TP, D])
    ys = dout("ys", [TS, D])
    nsr = dout("nsr", [NPS, 2, 4, 64, 128])
    nsg = dout("nsg", [NPS, 2, 4, 64, 128])

    NCH = (NPS * TP + TS) // C
    sz_scr = nc.dram_tensor("sz_scr", [NCH, 128, D], BF16, kind="Internal").ap()
    op_scr = nc.dram_tensor("op_scr", [NCH, 128, D], BF16, kind="Internal").ap()
    wo1_scr = nc.dram_tensor("wo1_scr", [128, 8, D], BF16, kind="Internal").ap()
    t_szscr = [TT(f"szscr{i}") for i in range(NCH)]
    t_opscr = [TT(f"opscr{i}") for i in range(NCH)]
    t_wo1scr = TT("wo1scr")

    dbg_outs = {}
    dbg_tiles = []

    with ExitStack() as es:
        sems = [es.enter_context(nc.semaphore(f"s{i}")) for i in range(100)]
        tk = Trk(sems)

        def sb(name, shape, dt, scope=es):
            return scope.enter_context(nc.sbuf_tensor(name, list(shape), dt))

        PB_ = [es.enter_context(nc.psum_tensor(f"ps{i}", [128, 512], F32)) for i in range(8)]
        tP = [TT(f"ps{i}", True) for i in range(8)]
        PA, PB, PT, PK, PL, PS, PO0, PO1 = range(8)

        def pf(i):
            return PB_[i][:]

        def pb16(i):
            return PB_[i][:].bitcast(BF16)

        WIN = sb("WIN", [128, 8, DIN], BF16); tWIN = TT("WIN")
        PERM = sb("PERM", [128, 128], BF16)
        WOUT = sb("WOUT", [128, 8, D], BF16); tWOUT = TT("WOUT")
        COS = sb("COS", [128, TS], F32); SIN = sb("SIN", [128, TS], F32); tROPE = TT("ROPE")
        IDN = sb("IDN", [128, 128], BF16); TRI = sb("TRI", [128, 2, 128], BF16)
        MASK = sb("MASK", [128, 4, 128], BF16); tCONST = TT("CONST")
        EQR = sb("EQR", [128, 2, 2, 128], BF16); EKR = sb("EKR", [128, 2, 2, 128], BF16)
        DECR = sb("DECR", [128, 4], F32); tRETT = TT("RETT")
        WA = sb("WA", [33, 512], BF16); tWA = TT("WA")
        SHT = sb("SHT", [128, 8, 2], F32); SC1T = sb("SC1T", [128, 8, 2], F32); tMOD = TT("MOD")
        FNW = sb("FNW", [128, D], F32); tFNW = TT("FNW")
        ROWSC = sb("ROWSC", [128, 8], F32)

        def dbg_dump(name, ap, shape, dt, R):
            if dbg is None or name not in dbg:
                return
            o = nc.dram_tensor("dbg_" + name, list(shape), dt, kind="ExternalOutput").ap()
            dbg_outs[name] = "dbg_" + name
            t = TT("dbg_" + name)
            dbg_tiles.append(t)
            tk.dma("sp", o, ap, R=R, W=[t], st=t, is_out=True)

        with ExitStack() as ss:
            WM = [sb(f"WM{i}", [128, DIN], F32, ss) for i in range(3)]; tWM = [TT("WM0"), TT("WM1"), TT("WM2")]
            MSB = sb("MSB", [2, 3 * D], F32, ss); tMSB = TT("MSB")
            BM2 = sb("BM2", [2, 3 * D], F32, ss); tBM2 = TT("BM2")
            GATE = sb("GATE", [128, 2, D], F32, ss); tGATE = TT("GATE")
            WOS = [sb(f"WOS{i}", [128, D], F32, ss) for i in range(2)]; tWOS = [TT("WOS0"), TT("WOS1")]
            WO1 = sb("WO1", [128, 8, D], BF16, ss); tWO1 = TT("WO1")
            CV = sb("CV", [128, 16], F32, ss); SCT0 = sb("SILC", [128, 16], F32, ss); tCV = TT("CV")
            LD = sb("LD", [128, 4], F32, ss); NLD = sb("NLD", [128, 4], F32, ss); tLD = TT("LD")
            POS = sb("POS", [128, 2, 128], F32, ss)
            WAS = sb("WAS", [33, 512], F32, ss); tWAS = TT("WAS")
            I2 = sb("I2", [2, 2], F32, ss); SEL = sb("SEL", [2, 2, 128], F32, ss)

            tk.dma("sp", CV[:], cvecT, W=[tCV])
            tk.dma("sp", WM[0][:, 0:3 * D], w_mod[0:128, :], W=[tWM[0]])
            tk.dma("sp", WM[1][:, 0:3 * D], w_mod[128:256, :], W=[tWM[1]])
            tk.dma("sp", WM[2][:, 0:3 * D], w_mod[256:384, :], W=[tWM[2]])
            tk.dma("sp", IDN[:], c_idn, W=[tCONST]); tk.dma("sp", TRI[:], c_tri, W=[tCONST], st=tCONST)
            tk.dma("sp", MASK[:], c_mask, W=[tCONST], st=tCONST)
            tk.dma("sp", PERM[:], c_perm, W=[tCONST], st=tCONST)
            tk.dma("sp", POS[:], c_pos, W=[tCONST], st=tCONST)
            tk.dma("sp", I2[:], c_i2, W=[tCONST], st=tCONST); tk.dma("sp", SEL[:], c_sel, W=[tCONST], st=tCONST)
            tk.dma("sp", ROWSC[:], rowsc, W=[tCONST], st=tCONST)
            tk.dma("sp", LD[:], ldcols, W=[tLD])
            tk.dma("sp", WAS[:], wa_aug, W=[tWAS])
            tk.dma("sp", BM2[:], b_mod2, W=[tBM2])
            tCONST.w = (tCONST.dsem, 16 * tCONST.dn, "dma")

            tk.op("act", mk("activation", out=SCT0[:], in_=CV[:], func=AF.Silu), R=[tCV], W=[tCV])
            tk.op("act", mk("copy", WA[:], WAS[:]), R=[tWAS], W=[tWA])
            tk.op("dve", mk("tensor_scalar", out=NLD[:], in0=LD[:], scalar1=-1.0, scalar2=None, op0=ALU.mult), R=[tLD], W=[tLD])
            for d in range(2):
                for p in range(2):
                    c_ = d * 2 + p
                    tk.op("act", mk("activation", out=EQR[:, d, p, :], in_=POS[:, d, :], func=AF.Exp,
                                                                          scale=LD[:, c_:c_ + 1]), R=[tLD, tCONST], W=[tRETT])
                    tk.op("act", mk("activation", out=EKR[:, d, p, :], in_=POS[:, d, :], func=AF.Exp,
                                                                          scale=NLD[:, c_:c_ + 1], bias=LN8), R=[tLD, tCONST], W=[tRETT])
            tk.op("act", mk("activation", out=DECR[:], in_=LD[:], func=AF.Exp, scale=float(C)), R=[tLD], W=[tRETT])
            mbanks = [PA, PB, PL, PS, PO0, PO1]
            for k in range(8):
                for n in range(6):
                    tk.op("pe", mk("matmul", pf(mbanks[n])[0:2, :], lhsT=SCT0[:, 2 * k:2 * k + 2],
                                   rhs=WM[k % 3][:, n * 512:(n + 1) * 512], start=(k == 0), stop=(k == 7)),
                          R=[tCV, tWM[k % 3]], W=[tP[mbanks[n]]], sig=(n == 5))
                if k + 3 < 8:
                    tk.dma("sp", WM[k % 3][:, 0:3 * D], w_mod[(k + 3) * 128:(k + 4) * 128, :], W=[tWM[k % 3]])
                else:
                    kk = k - 5
                    tk.dma("sp", WM[k % 3][:], w_in[kk * 128:(kk + 1) * 128, :], W=[tWM[k % 3]])
                if k == 5:
                    tk.dma("sp", WOS[0][:], w_out[0:128, :], W=[tWOS[0]])
                    tk.dma("sp", WOS[1][:], w_out[128:256, :], W=[tWOS[1]])
            for n in range(6):
                tk.op("dve", mk("tensor_tensor", out=MSB[:, n * 512:(n + 1) * 512], in0=pf(mbanks[n])[0:2, :],
                                                             in1=BM2[:, n * 512:(n + 1) * 512], op=ALU.add),
                      R=[tP[mbanks[n]], tBM2], W=[tMSB])
            for j in range(16):
                tk.op("pe", mk("matmul", pf(PA)[:, 2 * j:2 * j + 2], lhsT=MSB[0:2, j * 128:(j + 1) * 128],
                                                     rhs=I2[:], start=True, stop=True),
                      R=[tMSB, tCONST], W=[tP[PA]], sig=(j == 15))
            tk.op("dve", mk("tensor_copy", SHT[:].rearrange("p k c -> p (k c)"), pf(PA)[:, 0:16]), R=[tP[PA]], W=[tMOD])
            tk.op("dve", mk("tensor_scalar", out=SC1T[:].rearrange("p k c -> p (k c)"), in0=pf(PA)[:, 16:32],
                                                    scalar1=1.0, scalar2=None, op0=ALU.add), R=[tP[PA]], W=[tMOD])
            for kind in range(2):
                for n in range(2):
                    bank = PB if n == 0 else PL
                    tk.op("pe", mk("matmul",
                        pf(bank), lhsT=SEL[0:2, kind, :], rhs=MSB[0:2, 2048 + n * 512:2048 + (n + 1) * 512],
                        start=True, stop=True), R=[tMSB, tCONST], W=[tP[bank]])
                    tk.op("act", mk("copy", GATE[:, kind, n * 512:(n + 1) * 512], pf(bank)),
                          R=[tP[bank]], W=[tGATE])

            tk.dma("sp", COS[:], c_cos, W=[tROPE]); tk.dma("sp", SIN[:], c_sin, W=[tROPE], st=tROPE)
            tk.dma("sp", FNW[:], fnw_bc, W=[tFNW])
            for kk in range(8):
                b_ = (kk + 5) % 3
                tk.op("act", mk("copy", WIN[:, kk, :], WM[b_][:]), R=[tWM[b_]], W=[tWIN])
                if kk + 3 < 8:
                    tk.dma("sp", WM[b_][:], w_in[(kk + 3) * 128:(kk + 4) * 128, :], W=[tWM[b_]])
                k = kk
                tk.op("dve", mk("scalar_tensor_tensor", out=WOUT[:, k, :], in0=WOS[k % 2][:], scalar=ROWSC[:, k:k + 1],
                                in1=GATE[:, 0, :], op0=ALU.mult, op1=ALU.mult), R=[tWOS[k % 2], tGATE, tCONST], W=[tWOUT])
                tk.op("dve", mk("scalar_tensor_tensor", out=WO1[:, k, :], in0=WOS[k % 2][:], scalar=ROWSC[:, k:k + 1],
                                in1=GATE[:, 1, :], op0=ALU.mult, op1=ALU.mult), R=[tWOS[k % 2], tGATE, tCONST], W=[tWO1])
                if k + 2 < 8:
                    tk.dma("sp", WOS[k % 2][:], w_out[(k + 2) * 128:(k + 3) * 128, :], W=[tWOS[k % 2]])
            tk.dma("sp", wo1_scr, WO1[:], R=[tWO1], W=[t_wo1scr], st=tWO1)

            dbg_dump("SHT", SHT[:], [128, 8, 2], F32, [tMOD])
            dbg_dump("SC1T", SC1T[:], [128, 8, 2], F32, [tMOD])
            dbg_dump("GATE", GATE[:], [128, 2, D], F32, [tGATE])
            dbg_dump("EQR", EQR[:], [128, 2, 2, 128], BF16, [tRETT])
            tk.barrier([tWO1] + dbg_tiles)
        def _main_scope():
            KVF = sb("KVF", [128, 16, 4, 128], F32); tKVF = [TT(f"KVF{i}") for i in range(16)]
            QF = sb("QF", [128, 4, TS], BF16); tQF = [TT(f"QF{i}") for i in range(16)]
            DECF = sb("DECF", [128, 16, 4], F32); tDECF = [TT(f"DECF{i}") for i in range(16)]
            S = sb("S", [128, 4, 2, 128], F32); tS = [TT("Sf"), TT("Sb")]
            SNAP = sb("SNAP", [128, 4, 2, 128], BF16); tSNAP = [TT("SNf"), TT("SNb")]
            XT = [sb(f"XT{i}", [128, D], F32) for i in range(2)]; tXT = [TT(f"XT{i}") for i in range(2)]
            XN2 = [sb(f"XN{i}", [128, D], BF16) for i in range(NT)]; tXN2 = [TT(f"XN{i}") for i in range(NT)]
            STX = sb("STX", [128, 8], F32); tSTX = [TT("STX0"), TT("STX1")]
            HT2 = [sb(f"HT{i}", [128, 8, SEG], BF16) for i in range(2)]; tHT2 = [[TT(f"HT{i}_{t}") for t in range(NT)] for i in range(2)]
            hcur = [0]
            LRT = sb("LRT", [33, SEG], BF16); tLRT = TT("LRT")
            SPB = sb("SPB", [128, 512], BF16); tSPB = TT("SPB")
            EQG = sb("EQG", [128, 2, 2, SEG], BF16); EKG = sb("EKG", [128, 2, 2, SEG], BF16); tEG = [TT(f"EG{i}") for i in range(NT)]
            DECB = sb("DECB", [128, NT, 4], F32); tDECB = [TT(f"DECB{i}") for i in range(NT)]
            T1s = [sb(f"T1_{i}", [128, SEG], F32) for i in range(1)] * 2; T2s = [sb(f"T2_{i}", [128, SEG], F32) for i in range(1)] * 2
            tT1s = [TT("T1_0")] * 2; tT2s = [TT("T2_0")] * 2
            QRAW2 = [sb(f"QRAW{i}", [128, SEG], BF16) for i in range(2)]; tQRAW2 = [TT("QRAW0"), TT("QRAW1")]
            tT1 = TT("T1"); tT2 = TT("T2"); tRQ = TT("RQ")
            QB = sb("QB", [128, 4, SEG], BF16); KF = sb("KF", [128, 4, SEG], BF16); KB = sb("KB", [128, 4, SEG], BF16)
            tQB = [TT(f"QB{i}") for i in range(4)]; tKF = [TT(f"KF{i}") for i in range(4)]; tKB = [TT(f"KB{i}") for i in range(4)]
            VY = sb("VY", [128, NT * D], BF16)
            V = [VY[:, i * D:(i + 1) * D] for i in range(NT)]; tV = [TT(f"V{i}") for i in range(NT)]
            SZB = [sb(f"SZB{i}", [128, D], BF16) for i in range(2)]; tSZB = [TT("SZB0"), TT("SZB1")]
            OPB = [sb(f"OPB{i}", [128, D], BF16) for i in range(2)]; tOPB = [TT("OPB0"), TT("OPB1")]
            KTOK = sb("KTOK", [128, 8, 128], BF16); tKTOK = TT("KTOK")
            SCT = [sb(f"SCT{i}", [128, 4, 128], BF16) for i in range(2)]; tSCT = [TT("SCT0"), TT("SCT1")]
            ON = SPB; tON = tSPB
            OG2 = [sb("OG0", [128, D], BF16)] * 2; tOG2 = [TT("OG0")] * 2
            OGT = KTOK; tOGT = tKTOK
            Y = sb("Y", [128, D], F32); tY = TT("Y")
            YOap = VY[:].bitcast(F32)
            ETMPap = Y[:, 0:512]; tETMP = tY
            ST = sb("ST", [128, 64], F32); tST = TT("ST"); tSTr = TT("STr"); tSTg = TT("STg"); tSTy = TT("STy")
            tk.op("pool", mk("memset", LRT[32:33, :], 1.0), W=[tLRT])
            stage("setup")

            jobs = []
            seqsA = []
            for s in range(NPS):
                seqsA.append(dict(x=xp[s * TP:(s + 1) * TP, :], y=yp[s * TP:(s + 1) * TP, :], T=TP, ch0=2 * s, s0=False, so=s))
            jobs.append(dict(kind=0, rope=False, seqs=seqsA, scr0=0))
            jobs.append(dict(kind=1, rope=True, seqs=[dict(x=xs, y=ys, T=TS, ch0=0, s0=True, so=None)], scr0=8))

            def col_q(mt):
                if mt < 2:
                    return mt * 128, 256 + mt * 128
                return 1536 + (mt - 2) * 128, 1792 + (mt - 2) * 128

            def state_dram(base, typ, d, p):
                return base[d, 2 * p:2 * p + 2].rearrange("h k e -> (h k) e")

            xcount = [0]

            def load_x(xap, row0):
                i = xcount[0] % 2
                xcount[0] += 1
                tk.dma("sp", XT[i][:], xap[row0:row0 + 128, :], W=[tXT[i]])
                return i

            for job in jobs:
                kind = job["kind"]
                rope = job["rope"]
                if kind == 1:
                    tk.dma("sp", WOUT[:], wo1_scr, R=[t_wo1scr], W=[tWOUT], st=tWOUT)
                segs = []
                for sq in job["seqs"]:
                    for seg in reversed(range(sq["T"] // SEG)):
                        segs.append((sq, seg))
                rot = [PA, PB, PO0, PO1]
                rcnt = [0]

                def nbank():
                    b = rot[rcnt[0] % 4]
                    rcnt[0] += 1
                    return b

                def seg_loads(si):
                    sq_, seg_ = segs[si]
                    return [load_x(sq_["x"], (seg_ * NT + t_) * C) for t_ in range(NT)]

                def a_norm(xbs):
                    for t_ in range(NT):
                        xb = xbs[t_]
                        c0 = 4 * t_
                        tk.op("act", mk("activation", out=XN2[t_][:], in_=XT[xb][:], func=AF.Square, scale=1.0 / 32.0,
                                        accum_out=STX[:, c0:c0 + 1]), R=[tXT[xb]], W=[tXN2[t_], tSTX[t_]])
                        tk.op("act", mk("activation", out=STX[:, c0 + 1:c0 + 2], in_=STX[:, c0:c0 + 1], func=AF.Ln, bias=EPS),
                              R=[tSTX[t_]], W=[tSTX[t_]])
                        tk.op("act", mk("activation", out=STX[:, c0 + 2:c0 + 3], in_=STX[:, c0 + 1:c0 + 2], func=AF.Exp, scale=-0.5),
                              R=[tSTX[t_]], W=[tSTX[t_]])
                        tk.op("dve", mk("tensor_scalar", out=XN2[t_][:], in0=XT[xb][:], scalar1=STX[:, c0 + 2:c0 + 3], scalar2=None,
                                        op0=ALU.mult), R=[tXT[xb], tSTX[t_]], W=[tXN2[t_]])

                def a_trans(t_, hti):
                    HT = HT2[hti]; tHT = tHT2[hti]
                    for k in range(8):
                        tk.op("pe", mk("transpose", pb16(PT)[:, k * 128:(k + 1) * 128], XN2[t_][:, k * 128:(k + 1) * 128], IDN[:]),
                              R=[tXN2[t_], tCONST], W=[tP[PT]], sig=(k == 7))
                    for k in range(8):
                        if True:
                            tk.op("act", mk("activation", out=HT[:, k, t_ * C:(t_ + 1) * C], in_=pb16(PT)[:, k * 128:(k + 1) * 128],
                                            func=AF.Identity, scale=SC1T[:, k, kind:kind + 1], bias=SHT[:, k, kind:kind + 1]),
                                  R=[tP[PT], tMOD], W=[tHT[t_]])
                        else:
                            tk.op("dve", mk("tensor_scalar", out=HT[:, k, t_ * C:(t_ + 1) * C], in0=pb16(PT)[:, k * 128:(k + 1) * 128],
                                            scalar1=SC1T[:, k, kind:kind + 1], scalar2=SHT[:, k, kind:kind + 1],
                                            op0=ALU.mult, op1=ALU.add), R=[tP[PT], tMOD], W=[tHT[t_]])

                def proj_group(cols, ncols, rows_t=None, wsrc=None, tw=None):
                    b = nbank()
                    HT = HT2[hcur[0]]; tHT = tHT2[hcur[0]]
                    wsrc_ = WIN if wsrc is None else wsrc
                    tw_ = tWIN if tw is None else tw
                    for k in range(8):
                        if rows_t is None:
                            tk.op("pe", mk("matmul", pf(b)[0:ncols, 0:SEG], lhsT=wsrc_[:, k, cols:cols + ncols], rhs=HT[:, k, :],
                                           start=(k == 0), stop=(k == 7)), R=[tw_] + tHT, W=[tP[b]], sig=(k == 7))
                        else:
                            tk.op("pe", mk("matmul", pf(b)[:, 0:ncols], lhsT=HT[:, k, rows_t * C:(rows_t + 1) * C],
                                           rhs=wsrc_[:, k, cols:cols + ncols], start=(k == 0), stop=(k == 7)),
                                  R=[tw_, tHT[rows_t]], W=[tP[b]], sig=(k == 7))
                    return b

                def b_gates_vz(sq, seg, hooks):
                    b = proj_group(3072, 32)
                    tk.op("act", mk("copy", LRT[0:32, :], pf(b)[0:32, 0:SEG]), R=[tP[b]], W=[tLRT])

                    def g1(t_):
                        tk.op("pe", mk("matmul", pf(PL), lhsT=LRT[0:33, t_ * C:(t_ + 1) * C], rhs=WA[0:33, :], start=True, stop=True),
                              R=[tLRT, tWA], W=[tP[PL]])
                        tk.op("act", mk("activation", out=ETMPap, in_=pf(PL), func=AF.Exp, scale=-1.0), R=[tP[PL]], W=[tETMP])
                        tk.op("act", mk("activation", out=SPB[:], in_=ETMPap, func=AF.Ln, bias=1.0), R=[tETMP], W=[tSPB])

                    def g2(t_):
                        chl = sq["ch0"] + seg * NT + t_
                        for d in range(2):
                            for p in range(2):
                                sl = d * 2 + p
                                tk.op("pe", mk("matmul", pf(PS)[:, sl * 128:(sl + 1) * 128], lhsT=SPB[:, d * 256 + p * 128:d * 256 + (p + 1) * 128],
                                               rhs=TRI[:, d, :], start=True, stop=True), R=[tSPB, tCONST], W=[tP[PS]], sig=(sl == 3))
                        bview = pf(PS).rearrange("p (s c) -> p s c", s=4)
                        tk.op("act", mk("activation", out=EQG[:].rearrange("p d a s -> p (d a) s")[:, :, t_ * C:(t_ + 1) * C], in_=bview,
                                        func=AF.Exp, bias=LN8), R=[tP[PS]], W=[tEG[t_]])
                        tk.op("act", mk("activation", out=EKG[:].rearrange("p d a s -> p (d a) s")[:, :, t_ * C:(t_ + 1) * C], in_=bview,
                                        func=AF.Exp, scale=-1.0), R=[tP[PS]], W=[tEG[t_]])
                        tk.op("act", mk("activation", out=DECF[:, chl, 2:4].unsqueeze(2), in_=bview[:, 0:2, C - 1:C], func=AF.Exp),
                              R=[tP[PS]], W=[tDECF[chl]])
                        tk.op("act", mk("activation", out=DECB[:, t_, 2:4].unsqueeze(2), in_=bview[:, 2:4, 0:1], func=AF.Exp),
                              R=[tP[PS]], W=[tDECB[t_]])
                        tk.op("pool", mk("tensor_copy", DECF[:, chl, 0:2], DECR[:, 0:2]), R=[tRETT], W=[tDECF[chl]])
                        tk.op("pool", mk("tensor_copy", DECB[:, t_, 0:2], DECR[:, 2:4]), R=[tRETT], W=[tDECB[t_]])

                    def dgrp(t_, gi):
                        chl = sq["ch0"] + seg * NT + t_
                        sbi = chl % 2
                        cb, typ, oc = ((512, "v", 0), (2048, "v", 512), (1024, "z", 0), (2560, "z", 512))[gi]
                        bank = proj_group(cb, 512, rows_t=t_)
                        if typ == "v":
                            tk.op("act", mk("copy", V[t_][:, oc:oc + 512], pf(bank)), R=[tP[bank]], W=[tV[t_]])
                        else:
                            tk.op("act", mk("activation", out=SZB[sbi][:, oc:oc + 512], in_=pf(bank), func=AF.Silu),
                                  R=[tP[bank]], W=[tSZB[sbi]])
                        if gi == 3:
                            gch = job["scr0"] + chl
                            tk.dma("sp", sz_scr[gch], SZB[sbi][:], R=[tSZB[sbi]], W=[t_szscr[gch]], st=tSZB[sbi])

                    dgrp(0, 2); dgrp(0, 3); dgrp(1, 2); dgrp(1, 3)
                    if "norm" in hooks:
                        hooks["norm"]()
                    dgrp(0, 0); g1(0); dgrp(0, 1); dgrp(1, 0); g2(0); g1(1); dgrp(1, 1)
                    g2(1)

                def c_qk(sq, seg, hooks):
                    tok0 = seg * SEG
                    pend = []

                    def group(mt, which, cbase):
                        p = mt % 2
                        QRAW = QRAW2[0 if which == "k" else 1]; tQRAW = tQRAW2[0 if which == "k" else 1]
                        T1 = T1s[0 if which == "k" else 1]; tT1 = tT1s[0 if which == "k" else 1]
                        T2 = T2s[0 if which == "k" else 1]; tT2 = tT2s[0 if which == "k" else 1]
                        if mt < 2:
                            tabf = (EQR if which == "q" else EKR)[:, 0, p, :].unsqueeze(1).to_broadcast([128, NT, C])
                            tabb = (EQR if which == "q" else EKR)[:, 1, p, :].unsqueeze(1).to_broadcast([128, NT, C])
                            tabR = [tRETT]
                        else:
                            tabf = (EQG if which == "q" else EKG)[:, 0, p, :].rearrange("p (t c) -> p t c", t=NT)
                            tabb = (EQG if which == "q" else EKG)[:, 1, p, :].rearrange("p (t c) -> p t c", t=NT)
                            tabR = list(tEG)
                        chs = [sq["ch0"] + seg * NT + t_ for t_ in range(NT)]
                        if which == "q":
                            outf = QF[:, mt, (sq["ch0"] * C + tok0):(sq["ch0"] * C + tok0 + SEG)].rearrange("p (t c) -> p t c", t=NT)
                            outb = QB[:, mt, :].rearrange("p (t c) -> p t c", t=NT)
                            Wf = [tQF[c_] for c_ in chs]; Wb = [tQB[mt]]
                        else:
                            outf = KF[:, mt, :].rearrange("p (t c) -> p t c", t=NT)
                            outb = KB[:, mt, :].rearrange("p (t c) -> p t c", t=NT)
                            Wf = [tKF[mt]]; Wb = [tKB[mt]]
                        bq = proj_group(cbase, 128)
                        use_rope = rope and mt < 2
                        if use_rope:
                            tk.op("act", mk("copy", QRAW[:], pf(bq)[:, 0:SEG]), R=[tP[bq]], W=[tQRAW])

                        def stage2():
                            if use_rope:
                                br = nbank()
                                tk.op("pe", mk("matmul", pf(br)[:, 0:SEG], lhsT=PERM[:], rhs=QRAW[:], start=True, stop=True),
                                      R=[tCONST, tQRAW], W=[tP[br]])
                                tk.op("dve", mk("tensor_tensor", out=T1[:], in0=pf(br)[:, 0:SEG], in1=SIN[:, tok0:tok0 + SEG], op=ALU.mult),
                                      R=[tP[br], tROPE], W=[tT1])
                                tk.op("dve", mk("tensor_tensor", out=T2[:], in0=pf(bq)[:, 0:SEG], in1=COS[:, tok0:tok0 + SEG], op=ALU.mult),
                                      R=[tP[bq], tROPE], W=[tT2])
                                tk.op("dve", mk("tensor_tensor", out=T2[:], in0=T1[:], in1=T2[:], op=ALU.add), R=[tT1, tT2], W=[tT2])
                                src = T2[:].rearrange("p (t c) -> p t c", t=NT)
                                tk.op("dve", mk("tensor_tensor", out=outf, in0=src, in1=tabf, op=ALU.mult), R=[tT2] + tabR, W=Wf)
                                tk.op("dve", mk("tensor_tensor", out=outb, in0=src, in1=tabb, op=ALU.mult), R=[tT2] + tabR, W=Wb)
                            else:
                                src = pf(bq)[:, 0:SEG].rearrange("p (t c) -> p t c", t=NT)
                                tk.op("dve", mk("tensor_tensor", out=outf, in0=src, in1=tabf, op=ALU.mult), R=[tP[bq]] + tabR, W=Wf)
                                tk.op("dve", mk("tensor_tensor", out=outb, in0=src, in1=tabb, op=ALU.mult), R=[tP[bq]] + tabR, W=Wb)
                        return stage2

                    gi = 0
                    for mt in range(4):
                        cq, ck = col_q(mt)
                        for which, cbase in (("k", ck), ("q", cq)):
                            st2 = group(mt, which, cbase)
                            if pend:
                                pend.pop(0)()
                            pend.append(st2)
                            gi += 1
                            if gi == 2 and "t0" in hooks:
                                hooks["t0"]()
                            if gi == 5 and "t1" in hooks:
                                hooks["t1"]()
                    while pend:
                        pend.pop(0)()

                def e_pre(sq, seg, t):
                    cs = slice(t * C, (t + 1) * C)
                    for mt in range(4):
                        for d in range(2):
                            src = (KF if d == 0 else KB)[:, mt, cs]
                            tk.op("pe", mk("transpose", pb16(PK)[:, (mt * 2 + d) * 128:(mt * 2 + d + 1) * 128], src, IDN[:]),
                                  R=[tKF[mt], tKB[mt], tCONST], W=[tP[PK]], sig=(mt == 3 and d == 1))
                    tk.op("act", mk("copy", KTOK[:].rearrange("p s c -> p (s c)"), pb16(PK)), R=[tP[PK]], W=[tKTOK])

                def e_chunk(sq, seg, t):
                    chl = sq["ch0"] + seg * NT + t
                    obi = chl % 2
                    cs = slice(t * C, (t + 1) * C)
                    sbanks = [(PS, PK), (PA, PT)]

                    def scores(mt):
                        sct = SCT[mt % 2]; tsct = tSCT[mt % 2]
                        qf_ap = QF[:, mt, (chl * C):(chl + 1) * C]
                        for h in range(2):
                            hs = slice(h * 64, (h + 1) * 64)
                            sbank = sbanks[mt % 2][h]
                            for d in range(2):
                                kk = (KF if d == 0 else KB)[hs, mt, cs]
                                qq = qf_ap[hs, :] if d == 0 else QB[hs, mt, cs]
                                tk.op("pe", mk("matmul", pf(sbank)[:, d * 128:(d + 1) * 128], lhsT=kk, rhs=qq, start=True, stop=True),
                                      R=[tKF[mt], tKB[mt], tQB[mt], tQF[chl]], W=[tP[sbank]], sig=(d == 1))
                        for h in range(2):
                            sbank = sbanks[mt % 2][h]
                            tk.op("dve", mk("tensor_tensor", out=sct[:, 2 * h:2 * h + 2, :],
                                            in0=pf(sbank)[:, 0:256].rearrange("p (s c) -> p s c", s=2), in1=MASK[:, 2 * h:2 * h + 2, :],
                                            op=ALU.mult), R=[tP[sbank], tCONST], W=[tsct])

                    def kvp(mt):
                        for h in range(2):
                            hs = slice(h * 64, (h + 1) * 64)
                            hg = mt * 2 + h
                            vv = V[t][:, hg * 128:(hg + 1) * 128]
                            for d in range(2):
                                kvbank = PL if d == 0 else PB
                                tk.op("pe", mk("matmul", pf(kvbank)[hs, mt * 128:(mt + 1) * 128], lhsT=KTOK[:, mt * 2 + d, hs], rhs=vv,
                                               start=True, stop=True), R=[tKTOK, tV[t]], W=[tP[kvbank]], sig=(h == 1 and d == 1))

                    def omm(mt):
                        sct = SCT[mt % 2]; tsct = tSCT[mt % 2]
                        obank = PO0 if mt < 2 else PO1
                        for h in range(2):
                            hs = slice(h * 64, (h + 1) * 64)
                            hg = mt * 2 + h
                            oc = (hg % 4) * 128
                            vv = V[t][:, hg * 128:(hg + 1) * 128]
                            tk.op("pe", mk("matmul", pf(obank)[:, oc:oc + 128], lhsT=sct[:, h * 2, :], rhs=vv, start=True, stop=False),
                                  R=[tsct, tV[t]], W=[tP[obank]], sig=False)
                            tk.op("pe", mk("matmul", pf(obank)[:, oc:oc + 128], lhsT=sct[:, h * 2 + 1, :], rhs=vv, start=False, stop=False),
                                  R=[tsct, tV[t]], W=[tP[obank]], sig=False)
                            tk.op("pe", mk("matmul", pf(obank)[:, oc:oc + 128], lhsT=QB[hs, mt, cs], rhs=SNAP[hs, mt, 1, :], start=False, stop=True),
                                  R=[tQB[mt], tSNAP[1]], W=[tP[obank]], sig=(h == 1))

                    scores(0)
                    kvp(0)
                    for mt in range(4):
                        if mt + 1 < 4:
                            scores(mt + 1)
                            kvp(mt + 1)
                        omm(mt)
                    tk.op("act", mk("copy", OPB[obi][:, 0:512], pf(PO0)), R=[tP[PO0]], W=[tOPB[obi]])
                    tk.op("act", mk("copy", OPB[obi][:, 512:1024], pf(PO1)), R=[tP[PO1]], W=[tOPB[obi]])
                    gch = job["scr0"] + chl
                    tk.dma("sp", op_scr[gch], OPB[obi][:], R=[tOPB[obi]], W=[t_opscr[gch]], st=tOPB[obi])
                    tk.op("act", mk("copy", KVF[:, chl, :, :], pf(PL).rearrange("p (m e) -> p m e", m=4)), R=[tP[PL]], W=[tKVF[chl]])
                    tk.op("dve", mk("tensor_tensor", out=S[:, :, 1, :], in0=pf(PB).rearrange("p (m e) -> p m e", m=4), in1=S[:, :, 1, :], op=ALU.add),
                          R=[tP[PB], tS[1]], W=[tS[1]])
                    tk.op("pool", mk("tensor_tensor", out=S[:, :, 1, :], in0=S[:, :, 1, :], in1=DECB[:, t, :].unsqueeze(2).to_broadcast([128, 4, 128]),
                                     op=ALU.mult), R=[tS[1], tDECB[t]], W=[tS[1]])
                    tk.op("act", mk("copy", SNAP[:, :, 1, :], S[:, :, 1, :]), R=[tS[1]], W=[tSNAP[1]])

                xbs_next = seg_loads(0)
                a_norm(xbs_next)
                if len(segs) > 1:
                    xbs_next = seg_loads(1)
                for t_ in range(NT):
                    a_trans(t_, 0)
                for si, (sq, seg) in enumerate(segs):
                    nseg = sq["T"] // SEG
                    hcur[0] = si % 2
                    if seg == nseg - 1:
                        for mt in range(4):
                            if sq["s0"]:
                                base = st_ret if mt < 2 else st_gla
                                tk.dma("sp", S[:, mt, 1, :], state_dram(base, mt // 2, 1, mt % 2), W=[tS[1]])
                            else:
                                tk.op("pool", mk("memset", S[:, mt, 1, :], 0.0), W=[tS[1]])
                        tk.op("pool", mk("tensor_copy", SNAP[:, :, 1, :], S[:, :, 1, :]), R=[tS[1]], W=[tSNAP[1]])
                    has_next = si + 1 < len(segs)
                    hooks = {}
                    if has_next:
                        def hk_norm():
                            global_xbs = hooks["xbs"]
                            a_norm(global_xbs)
                        hooks["xbs"] = xbs_next
                        hooks["norm"] = hk_norm
                        hooks["t0"] = lambda nh=(si + 1) % 2: a_trans(0, nh)
                        hooks["t1"] = lambda nh=(si + 1) % 2: a_trans(1, nh)
                    b_gates_vz(sq, seg, hooks)
                    if has_next and si + 2 < len(segs):
                        xbs_next = seg_loads(si + 2)
                    c_qk(sq, seg, hooks)
                    e_pre(sq, seg, 1)
                    e_chunk(sq, seg, 1)
                    e_pre(sq, seg, 0)
                    e_chunk(sq, seg, 0)
                    if seg == 0 and sq["so"] is not None:
                        for mt in range(4):
                            dst = (nsr if mt < 2 else nsg)[sq["so"]]
                            tk.dma("sp", state_dram(dst, mt // 2, 1, mt % 2), S[:, mt, 1, :], R=[tS[1]], st=tS[1], is_out=True)


                stage("p1")
                ftiles = []
                for sq in job["seqs"]:
                    for cis in range(sq["T"] // C):
                        ftiles.append((sq, cis))
                nft = len(ftiles)
                PObanks = [(PO0, PO1), (PT, PL)]
                xb3 = {}

                def ld_x(i):
                    sq_, cis_ = ftiles[i]
                    xb3[i] = load_x(sq_["x"], cis_ * C)

                def ld_sz(i):
                    sq_, cis_ = ftiles[i]
                    chl_ = sq_["ch0"] + cis_
                    gch_ = job["scr0"] + chl_
                    bi_ = chl_ % 2
                    tk.dma("sp", SZB[bi_][:], sz_scr[gch_], R=[t_szscr[gch_]], W=[tSZB[bi_]], st=tSZB[bi_])

                def ld_op(i):
                    sq_, cis_ = ftiles[i]
                    chl_ = sq_["ch0"] + cis_
                    gch_ = job["scr0"] + chl_
                    bi_ = chl_ % 2
                    tk.dma("sp", OPB[bi_][:], op_scr[gch_], R=[t_opscr[gch_]], W=[tOPB[bi_]], st=tOPB[bi_])

                def s1(i):
                    sq, cis = ftiles[i]
                    chl = sq["ch0"] + cis
                    bi = chl % 2
                    pob = PObanks[i % 2]
                    if cis == 0:
                        for mt in range(4):
                            if sq["s0"]:
                                base = st_ret if mt < 2 else st_gla
                                tk.dma("sp", S[:, mt, 0, :], state_dram(base, mt // 2, 0, mt % 2), W=[tS[0]])
                            else:
                                tk.op("pool", mk("memset", S[:, mt, 0, :], 0.0), W=[tS[0]])
                    tk.op("act", mk("copy", SNAP[:, :, 0, :], S[:, :, 0, :]), R=[tS[0]], W=[tSNAP[0]])
                    for hg in range(8):
                        obank = pob[hg // 4]
                        hh = hg % 4
                        mt = hg // 2
                        hs = slice((hg % 2) * 64, (hg % 2 + 1) * 64)
                        tk.op("pe", mk("matmul", pf(obank)[:, hh * 128:(hh + 1) * 128], lhsT=IDN[:], rhs=OPB[bi][:, hg * 128:(hg + 1) * 128],
                                       start=True, stop=False), R=[tCONST, tOPB[bi]], W=[tP[obank]], sig=False)
                        tk.op("pe", mk("matmul", pf(obank)[:, hh * 128:(hh + 1) * 128], lhsT=QF[hs, mt, chl * C:(chl + 1) * C],
                                       rhs=SNAP[hs, mt, 0, :], start=False, stop=True),
                              R=[tQF[chl], tSNAP[0]], W=[tP[obank]], sig=(hh == 3))
                    tk.op("pool", mk("tensor_tensor", out=S[:, :, 0, :], in0=KVF[:, chl, :, :], in1=S[:, :, 0, :], op=ALU.add),
                          R=[tKVF[chl], tS[0]], W=[tS[0]])
                    tk.op("pool", mk("tensor_tensor", out=S[:, :, 0, :], in0=S[:, :, 0, :], in1=DECF[:, chl, :].unsqueeze(2).to_broadcast([128, 4, 128]),
                                     op=ALU.mult), R=[tS[0], tDECF[chl]], W=[tS[0]])
                    if cis == sq["T"] // C - 1 and sq["so"] is not None:
                        for mt in range(4):
                            dst = (nsr if mt < 2 else nsg)[sq["so"]]
                            tk.dma("sp", state_dram(dst, mt // 2, 0, mt % 2), S[:, mt, 0, :], R=[tS[0]], st=tS[0], is_out=True)

                def s2(i):
                    sq, cis = ftiles[i]
                    chl = sq["ch0"] + cis
                    bi = chl % 2
                    po_r, po_g = PObanks[i % 2]
                    OG = OG2[i % 2]; tOG = tOG2[i % 2]
                    for hh in range(4):
                        tk.op("dve", mk("bn_stats", out=ST[:, 4 + hh * 6:10 + hh * 6], in_=pf(po_r)[:, hh * 128:(hh + 1) * 128]),
                              R=[tP[po_r]], W=[tSTr])
                    for hh in range(4):
                        tk.op("act", mk("activation", out=ON[:, hh * 128:(hh + 1) * 128], in_=pf(po_g)[:, hh * 128:(hh + 1) * 128], func=AF.Square,
                                        scale=1.0 / math.sqrt(128.0), accum_out=ST[:, 48 + hh:49 + hh]), R=[tP[po_g]], W=[tON, tSTg])
                    for hh in range(4):
                        tk.op("dve", mk("bn_aggr", out=ST[:, 28 + hh * 2:30 + hh * 2], in_=ST[:, 4 + hh * 6:10 + hh * 6]), R=[tSTr], W=[tSTr])

                def s2b(i):
                    sq, cis = ftiles[i]
                    chl = sq["ch0"] + cis
                    bi = chl % 2
                    po_r, po_g = PObanks[i % 2]
                    OG = OG2[i % 2]; tOG = tOG2[i % 2]
                    mvv = ST[:, 28:36].rearrange("p (h t) -> p h t", t=2)
                    tk.op("act", mk("activation", out=ST[:, 36:40].unsqueeze(2), in_=mvv[:, :, 1:2], func=AF.Ln, bias=EPS), R=[tSTr], W=[tSTr])
                    tk.op("act", mk("activation", out=ST[:, 40:44], in_=ST[:, 36:40], func=AF.Exp, scale=-0.5), R=[tSTr], W=[tSTr])
                    tk.op("dve", mk("scalar_tensor_tensor", out=ST[:, 44:48].unsqueeze(2), in0=mvv[:, :, 0:1], scalar=-1.0, in1=ST[:, 40:44].unsqueeze(2),
                                    op0=ALU.mult, op1=ALU.mult), R=[tSTr], W=[tSTr])
                    tk.op("act", mk("activation", out=ST[:, 52:56], in_=ST[:, 48:52], func=AF.Ln, bias=EPS), R=[tSTg], W=[tSTg])
                    tk.op("act", mk("activation", out=ST[:, 56:60], in_=ST[:, 52:56], func=AF.Exp, scale=-0.5), R=[tSTg], W=[tSTg])
                    for hh in range(4):
                        tk.op("act", mk("activation", out=ON[:, hh * 128:(hh + 1) * 128], in_=pf(po_r)[:, hh * 128:(hh + 1) * 128], func=AF.Identity,
                                        scale=ST[:, 40 + hh:41 + hh], bias=ST[:, 44 + hh:45 + hh]), R=[tP[po_r], tSTr], W=[tON])
                    for hh in range(4):
                        tk.op("dve", mk("scalar_tensor_tensor", out=OG[:, 512 + hh * 128:512 + (hh + 1) * 128], in0=pf(po_g)[:, hh * 128:(hh + 1) * 128],
                                        scalar=ST[:, 56 + hh:57 + hh], in1=SZB[bi][:, 512 + hh * 128:512 + (hh + 1) * 128],
                                        op0=ALU.mult, op1=ALU.mult), R=[tP[po_g], tSTg, tSZB[bi]], W=[tOG])
                    tk.op("pool", mk("tensor_tensor", out=OG[:, 0:512], in0=ON[:], in1=SZB[bi][:, 0:512], op=ALU.mult),
                          R=[tON, tSZB[bi]], W=[tOG])

                def s3(i):
                    OG = OG2[i % 2]; tOG = tOG2[i % 2]
                    for k in range(8):
                        tk.op("pe", mk("transpose", pb16(PK)[:, k * 128:(k + 1) * 128], OG[:, k * 128:(k + 1) * 128], IDN[:]),
                              R=[tOG, tCONST], W=[tP[PK]], sig=(k == 7))
                    tk.op("act", mk("copy", OGT[:].rearrange("p k c -> p (k c)"), pb16(PK)), R=[tP[PK]], W=[tOGT])

                def s4mm(i):
                    for half in range(2):
                        bank = PA if half == 0 else PB
                        for k in range(8):
                            tk.op("pe", mk("matmul", pf(bank), lhsT=OGT[:, k, :], rhs=WOUT[:, k, half * 512:(half + 1) * 512],
                                           start=(k == 0), stop=(k == 7)), R=[tOGT, tWOUT], W=[tP[bank]], sig=(k == 7))

                def s4add(i):
                    xb = xb3[i]
                    for half in range(2):
                        bank = PA if half == 0 else PB
                        tk.op("dve", mk("tensor_tensor", out=Y[:, half * 512:(half + 1) * 512], in0=pf(bank),
                                        in1=XT[xb][:, half * 512:(half + 1) * 512], op=ALU.add), R=[tP[bank], tXT[xb]], W=[tY])

                def s4y_act(i):
                    tk.op("act", mk("activation", out=XN2[0][:], in_=Y[:], func=AF.Square, scale=1.0 / 32.0, accum_out=ST[:, 60:61]), R=[tY], W=[tXN2[0], tSTy])
                    tk.op("act", mk("activation", out=ST[:, 61:62], in_=ST[:, 60:61], func=AF.Ln, bias=EPS), R=[tSTy], W=[tSTy])
                    tk.op("act", mk("activation", out=ST[:, 62:63], in_=ST[:, 61:62], func=AF.Exp, scale=-0.5), R=[tSTy], W=[tSTy])

                def s4y_dve(i):
                    sq, cis = ftiles[i]
                    tk.op("dve", mk("scalar_tensor_tensor", out=YOap, in0=Y[:], scalar=ST[:, 62:63], in1=FNW[:], op0=ALU.mult, op1=ALU.mult),
                          R=[tY, tSTy, tFNW], W=[tV[0], tV[1]])
                    tk.dma("sp", sq["y"][cis * C:(cis + 1) * C, :], YOap, R=[tV[0], tV[1]], st=tV[0], is_out=True)
                    stage("p3")

                ld_op(0); ld_sz(0)
                if nft > 1:
                    ld_op(1)
                s1(0)
                for j in range(nft + 2):
                    if j + 2 < nft:
                        ld_op(j + 2)
                    if j + 1 < nft:
                        ld_sz(j + 1)
                    if j < nft:
                        ld_x(j)
                    if 0 <= j - 1 < nft:
                        s4mm(j - 1)
                    if j + 1 < nft:
                        s1(j + 1)
                    if j < nft:
                        s2(j)
                    if 0 <= j - 2 < nft:
                        s4y_dve(j - 2)
                    if j < nft:
                        s2b(j)
                    if 0 <= j - 1 < nft:
                        s4add(j - 1)
                    if j < nft:
                        s3(j)
                    if 0 <= j - 1 < nft:
                        s4y_act(j - 1)

        try:
            _main_scope()
        except _Stop:
            pass
        tk.finish("sp")
        with nc.Block() as block:
            tk.replay(block)
    nc._dbg_outs = dbg_outs
    nc._ninst = {k: v.ninst for k, v in tk.E.items()}
    return nc


def _host_consts():
    bf = ml_dtypes.bfloat16
    ii = np.arange(C)
    idn = np.eye(128, dtype=np.float32).astype(bf)
    tri = np.zeros((128, 2, 128), np.float32)
    tri[:, 0, :] = np.where(ii[:, None] <= ii[None, :], -1.0 / 16.0, 0.0)
    tri[:, 1, :] = np.where(ii[:, None] >= ii[None, :], -1.0 / 16.0, 0.0)
    mask = np.zeros((128, 4, 128), np.float32)
    mf = (ii[:, None] <= ii[None, :]).astype(np.float32)
    mb = (ii[:, None] >= ii[None, :]).astype(np.float32)
    for h in range(2):
        mask[:, h * 2 + 0, :] = mf
        mask[:, h * 2 + 1, :] = mb
    pos = np.zeros((128, 2, 128), np.float32)
    pos[:, 0, :] = (ii + 1)[None, :]
    pos[:, 1, :] = (C - ii)[None, :]
    t = np.arange(TS)
    rr = (t // 64).astype(np.float32)
    cc = (t % 64).astype(np.float32)
    inv = (10000.0 ** (-np.arange(16, dtype=np.float32) / 16)).astype(np.float32)
    ang = np.concatenate([rr[:, None] * inv, cc[:, None] * inv], axis=-1).astype(np.float32)
    cosT = np.cos(ang).T.astype(np.float32)
    sinT = np.sin(ang).T.astype(np.float32)
    cos128 = np.ascontiguousarray(np.tile(cosT, (4, 1)).astype(np.float32))
    sin128 = np.ascontiguousarray(np.tile(sinT, (4, 1)).astype(np.float32))
    i2 = np.eye(2, dtype=np.float32)
    perm = np.zeros((128, 128), np.float32)
    for m in range(128):
        if m % 64 < 32:
            perm[m + 32, m] = -1.0
        else:
            perm[m - 32, m] = 1.0
    sel = np.zeros((2, 2, 128), np.float32)
    sel[0, 0, :] = 1.0
    sel[1, 1, :] = 1.0
    return dict(c_idn=idn, c_tri=tri.astype(bf), c_mask=mask.astype(bf), c_pos=pos, c_cos=cos128, c_sin=sin128, c_perm=perm.astype(bf), c_i2=i2, c_sel=sel)


_NC_CACHE = {}


def kernel(x_prompt, x_sample, c, state_ret, state_gla, c_ctx, w_mod, b_mod, w_in,
           ret_log_decay, gla_w_alpha, gla_b_alpha, gla_norm_w, w_out, final_norm_w, _dbg=None):
    f = lambda a: np.ascontiguousarray(np.asarray(a, dtype=np.float32))
    x_prompt, x_sample, c, state_ret, state_gla, c_ctx = map(f, (x_prompt, x_sample, c, state_ret, state_gla, c_ctx))
    w_mod, b_mod, w_in, ret_log_decay = map(f, (w_mod, b_mod, w_in, ret_log_decay))
    gla_w_alpha, gla_b_alpha, gla_norm_w, w_out, final_norm_w = map(f, (gla_w_alpha, gla_b_alpha, gla_norm_w, w_out, final_norm_w))

    key = None if _dbg is None else tuple(sorted(_dbg))
    if key not in _NC_CACHE:
        _NC_CACHE[key] = build_nc(_dbg)
    nc = _NC_CACHE[key]
    consts = _host_consts()

    ld = ret_log_decay[0]
    ldcols = np.zeros((128, 4), np.float32)
    for d in range(2):
        for p in range(2):
            ldcols[0:64, d * 2 + p] = ld[d, 2 * p]
            ldcols[64:128, d * 2 + p] = ld[d, 2 * p + 1]
    wa_aug = np.zeros((33, 512), np.float32)
    wa_aug[0:16, 0:256] = gla_w_alpha[0, 0]
    wa_aug[16:32, 256:512] = gla_w_alpha[0, 1]
    wa_aug[32, 0:256] = gla_b_alpha[0, 0]
    wa_aug[32, 256:512] = gla_b_alpha[0, 1]
    rowsc = np.ones((128, 8), np.float32)
    rowsc[:, 4:8] = gla_norm_w[0][:, None]
    fnw_bc = np.ascontiguousarray(np.broadcast_to(final_norm_w[None, :], (128, D)))
    b_mod2 = np.ascontiguousarray(np.broadcast_to(b_mod[0][None, :], (2, 3 * D)))
    shared = dict(w_mod=w_mod[0], b_mod2=b_mod2, w_in=w_in[0], ldcols=ldcols, wa_aug=wa_aug, rowsc=rowsc,
                  w_out=w_out[0], fnw_bc=fnw_bc, **consts)

    in_maps = []
    for core in range(NCORES):
        cv = np.stack([c_ctx, c[core]], axis=0)
        cvecT = np.ascontiguousarray(cv.reshape(2, 8, 128).transpose(2, 1, 0).reshape(128, 16))
        m = dict(shared)
        m.update(xp=np.ascontiguousarray(x_prompt[core * NPS:(core + 1) * NPS].reshape(NPS * TP, D)),
                 xs=np.ascontiguousarray(x_sample[core]),
                 cvecT=cvecT,
                 st_ret=np.ascontiguousarray(state_ret[core, 0]),
                 st_gla=np.ascontiguousarray(state_gla[core, 0]))
        in_maps.append(m)

    res = run_bass_kernel_spmd(nc, in_maps, core_ids=list(range(NCORES)))
    rs = res.results
    y_prompt = np.concatenate([r["yp"].reshape(NPS, TP, D) for r in rs], axis=0).astype(np.float32)
    y_sample = np.stack([r["ys"] for r in rs], axis=0).astype(np.float32)
    new_ret = np.concatenate([r["nsr"].reshape(NPS, 1, 2, 4, 64, 128) for r in rs], axis=0).astype(np.float32)
    new_gla = np.concatenate([r["nsg"].reshape(NPS, 1, 2, 4, 64, 128) for r in rs], axis=0).astype(np.float32)
    if _dbg is not None:
        kernel._dbg_results = [{k: r[v] for k, v in nc._dbg_outs.items()} for r in rs]
    return (y_prompt, y_sample, new_ret, new_gla)
```

```python
import math
from contextlib import ExitStack

import numpy as np
import ml_dtypes

import concourse.bass as bass
import concourse.mybir as mybir
from concourse.bass_utils import run_bass_kernel_spmd

F32 = mybir.dt.float32
BF16 = mybir.dt.bfloat16
ALU = mybir.AluOpType
AF = mybir.ActivationFunctionType

NCORES = 8
D = 1024
DIN = 3104
TP = 256
TS = 2048
NPS = 4
C = 128
SEG = 256
NT = SEG // C
EPS = 1e-6
LN8 = math.log(0.125)


class TT:
    __slots__ = ("name", "w", "r", "psum", "dsem", "dn")

    def __init__(self, name, psum=False):
        self.name = name
        self.w = None
        self.r = {}
        self.psum = psum
        self.dsem = None
        self.dn = 0


class Eng:
    def __init__(self, name, sem):
        self.name = name
        self.sem = sem
        self.n = 0
        self.seen = {}
        self.prog = []
        self.ninst = 0


class Trk:
    def __init__(self, sems):
        self.free_sems = list(sems)
        self.E = {nm: Eng(nm, self.free_sems.pop()) for nm in ("pe", "act", "dve", "pool", "sp")}
        self.out_events = []

    def new_sem(self):
        return self.free_sems.pop()

    def _deps(self, eng, R, W):
        need = {}

        def add(ev, raw):
            if ev is None:
                return
            sem, val, en = ev
            if en == eng.name:
                if eng.name in ("pe", "sp"):
                    return
            if need.get(id(sem), (None, 0))[1] < val:
                need[id(sem)] = (sem, val, en)

        for t in R:
            add(t.w, True)
            if t.psum:
                for ev in t.r.values():
                    add(ev, False)
        for t in W:
            add(t.w, False)
            for ev in t.r.values():
                add(ev, False)
        out = []
        for sem, val, en in need.values():
            if eng.seen.get(id(sem), 0) >= val:
                continue
            if en in self.E and val > self.E[en].n:
                raise RuntimeError(f"dependency on unsignalled instruction of {en} from {eng.name}")
            out.append((sem, val))
        return out

    def _wait(self, eng, deps):
        for sem, val in deps:
            eng.prog.append(("w", sem, val))
            eng.seen[id(sem)] = val

    def _record(self, ev, R, W):
        for t in R:
            t.r[id(ev[0])] = ev
        for t in W:
            t.w = ev
            t.r = {}

    def op(self, en, fn, R=(), W=(), sig=True):
        eng = self.E[en]
        self._wait(eng, self._deps(eng, R, W))
        eng.ninst += 1
        if sig:
            eng.prog.append(("i", fn, eng.sem, 1))
            eng.n += 1
            ev = (eng.sem, eng.n, en)
        else:
            eng.prog.append(("i", fn, None, 0))
            ev = (eng.sem, eng.n + 1, en)
        self._record(ev, R, W)

    def dma(self, qn, out, in_, R=(), W=(), st=None, is_out=False):
        eng = self.E[qn]
        self._wait(eng, self._deps(eng, R, W))
        if st is None:
            st = (list(W) + list(R))[0]
        if st.dsem is None:
            st.dsem = self.new_sem()
        eng.prog.append(("i", (mk("dma_start", out=out, in_=in_)), st.dsem, 16))
        st.dn += 1
        ev = (st.dsem, 16 * st.dn, "dma")
        self._record(ev, R, W)
        if is_out:
            self.out_events.append(ev)

    def barrier(self, dma_tiles=()):
        for eng in self.E.values():
            for t in dma_tiles:
                if t.dsem is not None and eng.seen.get(id(t.dsem), 0) < 16 * t.dn:
                    eng.prog.append(("w", t.dsem, 16 * t.dn))
                    eng.seen[id(t.dsem)] = 16 * t.dn
            for o in self.E.values():
                if o is eng or o.n == 0:
                    continue
                if eng.seen.get(id(o.sem), 0) < o.n:
                    eng.prog.append(("w", o.sem, o.n))
                    eng.seen[id(o.sem)] = o.n

    def finish(self, en="sp"):
        eng = self.E[en]
        best = {}
        for sem, val, _ in self.out_events:
            if best.get(id(sem), (None, 0))[1] < val:
                best[id(sem)] = (sem, val)
        for sem, val in best.values():
            eng.prog.append(("w", sem, val))

    def replay(self, block):
        def run(eng):
            def f(q):
                for e in eng.prog:
                    if e[0] == "w":
                        q.wait_ge(e[1], e[2])
                    else:
                        inst = e[1](q)
                        if e[2] is not None:
                            inst.then_inc(e[2], e[3])
            return f
        block.tensor(run(self.E["pe"]))
        block.scalar(run(self.E["act"]))
        block.vector(run(self.E["dve"]))
        block.gpsimd(run(self.E["pool"]))
        block.sync(run(self.E["sp"]))


def mk(name, *args, **kwargs):
    return lambda q: getattr(q, name)(*args, **kwargs)


class _Stop(Exception):
    pass


STOP = None


def build_nc(dbg=None):
    nc = bass.Bass("TRN2", target_bir_lowering=False)

    stage_cnt = {}

    def stage(name):
        if STOP is None:
            return
        stage_cnt[name] = stage_cnt.get(name, 0) + 1
        nm, _, cnt = STOP.partition(":")
        if nm == name and stage_cnt[name] >= int(cnt or 1):
            raise _Stop()

    def din(name, shape, dt=F32):
        return nc.dram_tensor(name, list(shape), dt, kind="ExternalInput").ap()

    def dout(name, shape, dt=F32):
        return nc.dram_tensor(name, list(shape), dt, kind="ExternalOutput").ap()

    xp = din("xp", [NPS * TP, D])
    xs = din("xs", [TS, D])
    cvecT = din("cvecT", [128, 16])
    st_ret = din("st_ret", [2, 4, 64, 128])
    st_gla = din("st_gla", [2, 4, 64, 128])
    w_mod = din("w_mod", [D, 3 * D])
    b_mod2 = din("b_mod2", [2, 3 * D])
    w_in = din("w_in", [D, DIN])
    ldcols = din("ldcols", [128, 4])
    wa_aug = din("wa_aug", [33, 512])
    rowsc = din("rowsc", [128, 8])
    w_out = din("w_out", [D, D])
    fnw_bc = din("fnw_bc", [128, D])
    c_idn = din("c_idn", [128, 128], BF16)
    c_tri = din("c_tri", [128, 2, 128], BF16)
    c_mask = din("c_mask", [128, 4, 128], BF16)
    c_pos = din("c_pos", [128, 2, 128])
    c_cos = din("c_cos", [128, TS])
    c_sin = din("c_sin", [128, TS])
    c_perm = din("c_perm", [128, 128], BF16)
    c_i2 = din("c_i2", [2, 2])
    c_sel = din("c_sel", [2, 2, 128])

    yp = dout("yp", [NPS * TP, D])
    ys = dout("ys", [TS, D])
    nsr = dout("nsr", [NPS, 2, 4, 64, 128])
    nsg = dout("nsg", [NPS, 2, 4, 64, 128])

    NCH = (NPS * TP + TS) // C
    sz_scr = nc.dram_tensor("sz_scr", [NCH, 128, D], BF16, kind="Internal").ap()
    op_scr = nc.dram_tensor("op_scr", [NCH, 128, D], BF16, kind="Internal").ap()
    wo1_scr = nc.dram_tensor("wo1_scr", [128, 8, D], BF16, kind="Internal").ap()
    t_szscr = [TT(f"szscr{i}") for i in range(NCH)]
    t_opscr = [TT(f"opscr{i}") for i in range(NCH)]
    t_wo1scr = TT("wo1scr")

    dbg_outs = {}
    dbg_tiles = []

    with ExitStack() as es:
        sems = [es.enter_context(nc.semaphore(f"s{i}")) for i in range(100)]
        tk = Trk(sems)

        def sb(name, shape, dt, scope=es):
            return scope.enter_context(nc.sbuf_tensor(name, list(shape), dt))

        PB_ = [es.enter_context(nc.psum_tensor(f"ps{i}", [128, 512], F32)) for i in range(8)]
        tP = [TT(f"ps{i}", True) for i in range(8)]
        PA, PB, PT, PK, PL, PS, PO0, PO1 = range(8)

        def pf(i):
            return PB_[i][:]

        def pb16(i):
            return PB_[i][:].bitcast(BF16)

        WIN = sb("WIN", [128, 8, DIN], BF16); tWIN = TT("WIN")
        PERM = sb("PERM", [128, 128], BF16)
        WOUT = sb("WOUT", [128, 8, D], BF16); tWOUT = TT("WOUT")
        COS = sb("COS", [128, TS], F32); SIN = sb("SIN", [128, TS], F32); tROPE = TT("ROPE")
        IDN = sb("IDN", [128, 128], BF16); TRI = sb("TRI", [128, 2, 128], BF16)
        MASK = sb("MASK", [128, 4, 128], BF16); tCONST = TT("CONST")
        EQR = sb("EQR", [128, 2, 2, 128], BF16); EKR = sb("EKR", [128, 2, 2, 128], BF16)
        DECR = sb("DECR", [128, 4], F32); tRETT = TT("RETT")
        WA = sb("WA", [33, 512], BF16); tWA = TT("WA")
        SHT = sb("SHT", [128, 8, 2], F32); SC1T = sb("SC1T", [128, 8, 2], F32); tMOD = TT("MOD")
        FNW = sb("FNW", [128, D], F32); tFNW = TT("FNW")
        ROWSC = sb("ROWSC", [128, 8], F32)

        def dbg_dump(name, ap, shape, dt, R):
            if dbg is None or name not in dbg:
                return
            o = nc.dram_tensor("dbg_" + name, list(shape), dt, kind="ExternalOutput").ap()
            dbg_outs[name] = "dbg_" + name
            t = TT("dbg_" + name)
            dbg_tiles.append(t)
            tk.dma("sp", o, ap, R=R, W=[t], st=t, is_out=True)

        with ExitStack() as ss:
            WM = [sb(f"WM{i}", [128, DIN], F32, ss) for i in range(3)]; tWM = [TT("WM0"), TT("WM1"), TT("WM2")]
            MSB = sb("MSB", [2, 3 * D], F32, ss); tMSB = TT("MSB")
            BM2 = sb("BM2", [2, 3 * D], F32, ss); tBM2 = TT("BM2")
            GATE = sb("GATE", [128, 2, D], F32, ss); tGATE = TT("GATE")
            WOS = [sb(f"WOS{i}", [128, D], F32, ss) for i in range(2)]; tWOS = [TT("WOS0"), TT("WOS1")]
            WO1 = sb("WO1", [128, 8, D], BF16, ss); tWO1 = TT("WO1")
            CV = sb("CV", [128, 16], F32, ss); SCT0 = sb("SILC", [128, 16], F32, ss); tCV = TT("CV")
            LD = sb("LD", [128, 4], F32, ss); NLD = sb("NLD", [128, 4], F32, ss); tLD = TT("LD")
            POS = sb("POS", [128, 2, 128], F32, ss)
            WAS = sb("WAS", [33, 512], F32, ss); tWAS = TT("WAS")
            I2 = sb("I2", [2, 2], F32, ss); SEL = sb("SEL", [2, 2, 128], F32, ss)

            tk.dma("sp", CV[:], cvecT, W=[tCV])
            tk.dma("sp", WM[0][:, 0:3 * D], w_mod[0:128, :], W=[tWM[0]])
            tk.dma("sp", WM[1][:, 0:3 * D], w_mod[128:256, :], W=[tWM[1]])
            tk.dma("sp", WM[2][:, 0:3 * D], w_mod[256:384, :], W=[tWM[2]])
            tk.dma("sp", IDN[:], c_idn, W=[tCONST]); tk.dma("sp", TRI[:], c_tri, W=[tCONST], st=tCONST)
            tk.dma("sp", MASK[:], c_mask, W=[tCONST], st=tCONST)
            tk.dma("sp", PERM[:], c_perm, W=[tCONST], st=tCONST)
            tk.dma("sp", POS[:], c_pos, W=[tCONST], st=tCONST)
            tk.dma("sp", I2[:], c_i2, W=[tCONST], st=tCONST); tk.dma("sp", SEL[:], c_sel, W=[tCONST], st=tCONST)
            tk.dma("sp", ROWSC[:], rowsc, W=[tCONST], st=tCONST)
            tk.dma("sp", LD[:], ldcols, W=[tLD])
            tk.dma("sp", WAS[:], wa_aug, W=[tWAS])
            tk.dma("sp", BM2[:], b_mod2, W=[tBM2])
            tCONST.w = (tCONST.dsem, 16 * tCONST.dn, "dma")

            tk.op("act", mk("activation", out=SCT0[:], in_=CV[:], func=AF.Silu), R=[tCV], W=[tCV])
            tk.op("act", mk("copy", WA[:], WAS[:]), R=[tWAS], W=[tWA])
            tk.op("dve", mk("tensor_scalar", out=NLD[:], in0=LD[:], scalar1=-1.0, scalar2=None, op0=ALU.mult), R=[tLD], W=[tLD])
            for d in range(2):
                for p in range(2):
                    c_ = d * 2 + p
                    tk.op("act", mk("activation", out=EQR[:, d, p, :], in_=POS[:, d, :], func=AF.Exp,
                                                                          scale=LD[:, c_:c_ + 1]), R=[tLD, tCONST], W=[tRETT])
                    tk.op("act", mk("activation", out=EKR[:, d, p, :], in_=POS[:, d, :], func=AF.Exp,
                                                                          scale=NLD[:, c_:c_ + 1], bias=LN8), R=[tLD, tCONST], W=[tRETT])
            tk.op("act", mk("activation", out=DECR[:], in_=LD[:], func=AF.Exp, scale=float(C)), R=[tLD], W=[tRETT])
            mbanks = [PA, PB, PL, PS, PO0, PO1]
            for k in range(8):
                for n in range(6):
                    tk.op("pe", mk("matmul", pf(mbanks[n])[0:2, :], lhsT=SCT0[:, 2 * k:2 * k + 2],
                                   rhs=WM[k % 3][:, n * 512:(n + 1) * 512], start=(k == 0), stop=(k == 7)),
                          R=[tCV, tWM[k % 3]], W=[tP[mbanks[n]]], sig=(n == 5))
                if k + 3 < 8:
                    tk.dma("sp", WM[k % 3][:, 0:3 * D], w_mod[(k + 3) * 128:(k + 4) * 128, :], W=[tWM[k % 3]])
                else:
                    kk = k - 5
                    tk.dma("sp", WM[k % 3][:], w_in[kk * 128:(kk + 1) * 128, :], W=[tWM[k % 3]])
                if k == 5:
                    tk.dma("sp", WOS[0][:], w_out[0:128, :], W=[tWOS[0]])
                    tk.dma("sp", WOS[1][:], w_out[128:256, :], W=[tWOS[1]])
            for n in range(6):
                tk.op("dve", mk("tensor_tensor", out=MSB[:, n * 512:(n + 1) * 512], in0=pf(mbanks[n])[0:2, :],
                                                             in1=BM2[:, n * 512:(n + 1) * 512], op=ALU.add),
                      R=[tP[mbanks[n]], tBM2], W=[tMSB])
            for j in range(16):
                tk.op("pe", mk("matmul", pf(PA)[:, 2 * j:2 * j + 2], lhsT=MSB[0:2, j * 128:(j + 1) * 128],
                                                     rhs=I2[:], start=True, stop=True),
                      R=[tMSB, tCONST], W=[tP[PA]], sig=(j == 15))
            tk.op("dve", mk("tensor_copy", SHT[:].rearrange("p k c -> p (k c)"), pf(PA)[:, 0:16]), R=[tP[PA]], W=[tMOD])
            tk.op("dve", mk("tensor_scalar", out=SC1T[:].rearrange("p k c -> p (k c)"), in0=pf(PA)[:, 16:32],
                                                    scalar1=1.0, scalar2=None, op0=ALU.add), R=[tP[PA]], W=[tMOD])
            for kind in range(2):
                for n in range(2):
                    bank = PB if n == 0 else PL
                    tk.op("pe", mk("matmul",
                        pf(bank), lhsT=SEL[0:2, kind, :], rhs=MSB[0:2, 2048 + n * 512:2048 + (n + 1) * 512],
                        start=True, stop=True), R=[tMSB, tCONST], W=[tP[bank]])
                    tk.op("act", mk("copy", GATE[:, kind, n * 512:(n + 1) * 512], pf(bank)),
                          R=[tP[bank]], W=[tGATE])

            tk.dma("sp", COS[:], c_cos, W=[tROPE]); tk.dma("sp", SIN[:], c_sin, W=[tROPE], st=tROPE)
            tk.dma("sp", FNW[:], fnw_bc, W=[tFNW])
            for kk in range(8):
                b_ = (kk + 5) % 3
                tk.op("act", mk("copy", WIN[:, kk, :], WM[b_][:]), R=[tWM[b_]], W=[tWIN])
                if kk + 3 < 8:
                    tk.dma("sp", WM[b_][:], w_in[(kk + 3) * 128:(kk + 4) * 128, :], W=[tWM[b_]])
                k = kk
                tk.op("dve", mk("scalar_tensor_tensor", out=WOUT[:, k, :], in0=WOS[k % 2][:], scalar=ROWSC[:, k:k + 1],
                                in1=GATE[:, 0, :], op0=ALU.mult, op1=ALU.mult), R=[tWOS[k % 2], tGATE, tCONST], W=[tWOUT])
                tk.op("dve", mk("scalar_tensor_tensor", out=WO1[:, k, :], in0=WOS[k % 2][:], scalar=ROWSC[:, k:k + 1],
                                in1=GATE[:, 1, :], op0=ALU.mult, op1=ALU.mult), R=[tWOS[k % 2], tGATE, tCONST], W=[tWO1])
                if k + 2 < 8:
                    tk.dma("sp", WOS[k % 2][:], w_out[(k + 2) * 128:(k + 3) * 128, :], W=[tWOS[k % 2]])
            tk.dma("sp", wo1_scr, WO1[:], R=[tWO1], W=[t_wo1scr], st=tWO1)

            dbg_dump("SHT", SHT[:], [128, 8, 2], F32, [tMOD])
            dbg_dump("SC1T", SC1T[:], [128, 8, 2], F32, [tMOD])
            dbg_dump("GATE", GATE[:], [128, 2, D], F32, [tGATE])
            dbg_dump("EQR", EQR[:], [128, 2, 2, 128], BF16, [tRETT])
            tk.barrier([tWO1] + dbg_tiles)
        def _main_scope():
            KVF = sb("KVF", [128, 16, 4, 128], F32); tKVF = [TT(f"KVF{i}") for i in range(16)]
            QF = sb("QF", [128, 4, TS], BF16); tQF = [TT(f"QF{i}") for i in range(16)]
            DECF = sb("DECF", [128, 16, 4], F32); tDECF = [TT(f"DECF{i}") for i in range(16)]
            S = sb("S", [128, 4, 2, 128], F32); tS = [TT("Sf"), TT("Sb")]
            SNAP = sb("SNAP", [128, 4, 2, 128], BF16); tSNAP = [TT("SNf"), TT("SNb")]
            XT = [sb(f"XT{i}", [128, D], F32) for i in range(2)]; tXT = [TT(f"XT{i}") for i in range(2)]
            XN2 = [sb(f"XN{i}", [128, D], BF16) for i in range(NT)]; tXN2 = [TT(f"XN{i}") for i in range(NT)]
            STX = sb("STX", [128, 8], F32); tSTX = [TT("STX0"), TT("STX1")]
            HT2 = [sb(f"HT{i}", [128, 8, SEG], BF16) for i in range(2)]; tHT2 = [[TT(f"HT{i}_{t}") for t in range(NT)] for i in range(2)]
            hcur = [0]
            LRT = sb("LRT", [33, SEG], BF16); tLRT = TT("LRT")
            SPB = sb("SPB", [128, 512], BF16); tSPB = TT("SPB")
            EQG = sb("EQG", [128, 2, 2, SEG], BF16); EKG = sb("EKG", [128, 2, 2, SEG], BF16); tEG = [TT(f"EG{i}") for i in range(NT)]
            DECB = sb("DECB", [128, NT, 4], F32); tDECB = [TT(f"DECB{i}") for i in range(NT)]
            T1s = [sb(f"T1_{i}", [128, SEG], F32) for i in range(1)] * 2; T2s = [sb(f"T2_{i}", [128, SEG], F32) for i in range(1)] * 2
            tT1s = [TT("T1_0")] * 2; tT2s = [TT("T2_0")] * 2
            QRAW2 = [sb(f"QRAW{i}", [128, SEG], BF16) for i in range(2)]; tQRAW2 = [TT("QRAW0"), TT("QRAW1")]
            tT1 = TT("T1"); tT2 = TT("T2"); tRQ = TT("RQ")
            QB = sb("QB", [128, 4, SEG], BF16); KF = sb("KF", [128, 4, SEG], BF16); KB = sb("KB", [128, 4, SEG], BF16)
            tQB = [TT(f"QB{i}") for i in range(4)]; tKF = [TT(f"KF{i}") for i in range(4)]; tKB = [TT(f"KB{i}") for i in range(4)]
            VY = sb("VY", [128, NT * D], BF16)
            V = [VY[:, i * D:(i + 1) * D] for i in range(NT)]; tV = [TT(f"V{i}") for i in range(NT)]
            SZB = [sb(f"SZB{i}", [128, D], BF16) for i in range(2)]; tSZB = [TT("SZB0"), TT("SZB1")]
            OPB = [sb(f"OPB{i}", [128, D], BF16) for i in range(2)]; tOPB = [TT("OPB0"), TT("OPB1")]
            KTOK = sb("KTOK", [128, 8, 128], BF16); tKTOK = TT("KTOK")
            SCT = [sb(f"SCT{i}", [128, 4, 128], BF16) for i in range(2)]; tSCT = [TT("SCT0"), TT("SCT1")]
            ON = SPB; tON = tSPB
            OG2 = [sb("OG0", [128, D], BF16)] * 2; tOG2 = [TT("OG0")] * 2
            OGT = KTOK; tOGT = tKTOK
            Y = sb("Y", [128, D], F32); tY = TT("Y")
            YOap = VY[:].bitcast(F32)
            ETMPap = Y[:, 0:512]; tETMP = tY
            ST = sb("ST", [128, 64], F32); tST = TT("ST"); tSTr = TT("STr"); tSTg = TT("STg"); tSTy = TT("STy")
            tk.op("pool", mk("memset", LRT[32:33, :], 1.0), W=[tLRT])
            stage("setup")

            jobs = []
            seqsA = []
            for s in range(NPS):
                seqsA.append(dict(x=xp[s * TP:(s + 1) * TP, :], y=yp[s * TP:(s + 1) * TP, :], T=TP, ch0=2 * s, s0=False, so=s))
            jobs.append(dict(kind=0, rope=False, seqs=seqsA, scr0=0))
            jobs.append(dict(kind=1, rope=True, seqs=[dict(x=xs, y=ys, T=TS, ch0=0, s0=True, so=None)], scr0=8))

            def col_q(mt):
                if mt < 2:
                    return mt * 128, 256 + mt * 128
                return 1536 + (mt - 2) * 128, 1792 + (mt - 2) * 128

            def state_dram(base, typ, d, p):
                return base[d, 2 * p:2 * p + 2].rearrange("h k e -> (h k) e")

            xcount = [0]

            def load_x(xap, row0):
                i = xcount[0] % 2
                xcount[0] += 1
                tk.dma("sp", XT[i][:], xap[row0:row0 + 128, :], W=[tXT[i]])
                return i

            for job in jobs:
                kind = job["kind"]
                rope = job["rope"]
                if kind == 1:
                    tk.dma("sp", WOUT[:], wo1_scr, R=[t_wo1scr], W=[tWOUT], st=tWOUT)
                segs = []
                for sq in job["seqs"]:
                    for seg in reversed(range(sq["T"] // SEG)):
                        segs.append((sq, seg))
                rot = [PA, PB, PO0, PO1]
                rcnt = [0]

                def nbank():
                    b = rot[rcnt[0] % 4]
                    rcnt[0] += 1
                    return b

                def seg_loads(si):
                    sq_, seg_ = segs[si]
                    return [load_x(sq_["x"], (seg_ * NT + t_) * C) for t_ in range(NT)]

                def a_norm(xbs):
                    for t_ in range(NT):
                        xb = xbs[t_]
                        c0 = 4 * t_
                        tk.op("act", mk("activation", out=XN2[t_][:], in_=XT[xb][:], func=AF.Square, scale=1.0 / 32.0,
                                        accum_out=STX[:, c0:c0 + 1]), R=[tXT[xb]], W=[tXN2[t_], tSTX[t_]])
                        tk.op("act", mk("activation", out=STX[:, c0 + 1:c0 + 2], in_=STX[:, c0:c0 + 1], func=AF.Ln, bias=EPS),
                              R=[tSTX[t_]], W=[tSTX[t_]])
                        tk.op("act", mk("activation", out=STX[:, c0 + 2:c0 + 3], in_=STX[:, c0 + 1:c0 + 2], func=AF.Exp, scale=-0.5),
                              R=[tSTX[t_]], W=[tSTX[t_]])
                        tk.op("dve", mk("tensor_scalar", out=XN2[t_][:], in0=XT[xb][:], scalar1=STX[:, c0 + 2:c0 + 3], scalar2=None,
                                        op0=ALU.mult), R=[tXT[xb], tSTX[t_]], W=[tXN2[t_]])

                def a_trans(t_, hti):
                    HT = HT2[hti]; tHT = tHT2[hti]
                    for k in range(8):
                        tk.op("pe", mk("transpose", pb16(PT)[:, k * 128:(k + 1) * 128], XN2[t_][:, k * 128:(k + 1) * 128], IDN[:]),
                              R=[tXN2[t_], tCONST], W=[tP[PT]], sig=(k == 7))
                    for k in range(8):
                        if True:
                            tk.op("act", mk("activation", out=HT[:, k, t_ * C:(t_ + 1) * C], in_=pb16(PT)[:, k * 128:(k + 1) * 128],
                                            func=AF.Identity, scale=SC1T[:, k, kind:kind + 1], bias=SHT[:, k, kind:kind + 1]),
                                  R=[tP[PT], tMOD], W=[tHT[t_]])
                        else:
                            tk.op("dve", mk("tensor_scalar", out=HT[:, k, t_ * C:(t_ + 1) * C], in0=pb16(PT)[:, k * 128:(k + 1) * 128],
                                            scalar1=SC1T[:, k, kind:kind + 1], scalar2=SHT[:, k, kind:kind + 1],
                                            op0=ALU.mult, op1=ALU.add), R=[tP[PT], tMOD], W=[tHT[t_]])

                def proj_group(cols, ncols, rows_t=None, wsrc=None, tw=None):
                    b = nbank()
                    HT = HT2[hcur[0]]; tHT = tHT2[hcur[0]]
                    wsrc_ = WIN if wsrc is None else wsrc
                    tw_ = tWIN if tw is None else tw
                    for k in range(8):
                        if rows_t is None:
                            tk.op("pe", mk("matmul", pf(b)[0:ncols, 0:SEG], lhsT=wsrc_[:, k, cols:cols + ncols], rhs=HT[:, k, :],
                                           start=(k == 0), stop=(k == 7)), R=[tw_] + tHT, W=[tP[b]], sig=(k == 7))
                        else:
                            tk.op("pe", mk("matmul", pf(b)[:, 0:ncols], lhsT=HT[:, k, rows_t * C:(rows_t + 1) * C],
                                           rhs=wsrc_[:, k, cols:cols + ncols], start=(k == 0), stop=(k == 7)),
                                  R=[tw_, tHT[rows_t]], W=[tP[b]], sig=(k == 7))
                    return b

                def b_gates_vz(sq, seg, hooks):
                    b = proj_group(3072, 32)
                    tk.op("act", mk("copy", LRT[0:32, :], pf(b)[0:32, 0:SEG]), R=[tP[b]], W=[tLRT])

                    def g1(t_):
                        tk.op("pe", mk("matmul", pf(PL), lhsT=LRT[0:33, t_ * C:(t_ + 1) * C], rhs=WA[0:33, :], start=True, stop=True),
                              R=[tLRT, tWA], W=[tP[PL]])
                        tk.op("act", mk("activation", out=ETMPap, in_=pf(PL), func=AF.Exp, scale=-1.0), R=[tP[PL]], W=[tETMP])
                        tk.op("act", mk("activation", out=SPB[:], in_=ETMPap, func=AF.Ln, bias=1.0), R=[tETMP], W=[tSPB])

                    def g2(t_):
                        chl = sq["ch0"] + seg * NT + t_
                        for d in range(2):
                            for p in range(2):
                                sl = d * 2 + p
                                tk.op("pe", mk("matmul", pf(PS)[:, sl * 128:(sl + 1) * 128], lhsT=SPB[:, d * 256 + p * 128:d * 256 + (p + 1) * 128],
                                               rhs=TRI[:, d, :], start=True, stop=True), R=[tSPB, tCONST], W=[tP[PS]], sig=(sl == 3))
                        bview = pf(PS).rearrange("p (s c) -> p s c", s=4)
                        tk.op("act", mk("activation", out=EQG[:].rearrange("p d a s -> p (d a) s")[:, :, t_ * C:(t_ + 1) * C], in_=bview,
                                        func=AF.Exp, bias=LN8), R=[tP[PS]], W=[tEG[t_]])
                        tk.op("act", mk("activation", out=EKG[:].rearrange("p d a s -> p (d a) s")[:, :, t_ * C:(t_ + 1) * C], in_=bview,
                                        func=AF.Exp, scale=-1.0), R=[tP[PS]], W=[tEG[t_]])
                        tk.op("act", mk("activation", out=DECF[:, chl, 2:4].unsqueeze(2), in_=bview[:, 0:2, C - 1:C], func=AF.Exp),
                              R=[tP[PS]], W=[tDECF[chl]])
                        tk.op("act", mk("activation", out=DECB[:, t_, 2:4].unsqueeze(2), in_=bview[:, 2:4, 0:1], func=AF.Exp),
                              R=[tP[PS]], W=[tDECB[t_]])
                        tk.op("pool", mk("tensor_copy", DECF[:, chl, 0:2], DECR[:, 0:2]), R=[tRETT], W=[tDECF[chl]])
                        tk.op("pool", mk("tensor_copy", DECB[:, t_, 0:2], DECR[:, 2:4]), R=[tRETT], W=[tDECB[t_]])

                    def dgrp(t_, gi):
                        chl = sq["ch0"] + seg * NT + t_
                        sbi = chl % 2
                        cb, typ, oc = ((512, "v", 0), (2048, "v", 512), (1024, "z", 0), (2560, "z", 512))[gi]
                        bank = proj_group(cb, 512, rows_t=t_)
                        if typ == "v":
                            tk.op("act", mk("copy", V[t_][:, oc:oc + 512], pf(bank)), R=[tP[bank]], W=[tV[t_]])
                        else:
                            tk.op("act", mk("activation", out=SZB[sbi][:, oc:oc + 512], in_=pf(bank), func=AF.Silu),
                                  R=[tP[bank]], W=[tSZB[sbi]])
                        if gi == 3:
                            gch = job["scr0"] + chl
                            tk.dma("sp", sz_scr[gch], SZB[sbi][:], R=[tSZB[sbi]], W=[t_szscr[gch]], st=tSZB[sbi])

                    dgrp(0, 2); dgrp(0, 3); dgrp(1, 2); dgrp(1, 3)
                    if "norm" in hooks:
                        hooks["norm"]()
                    dgrp(0, 0); g1(0); dgrp(0, 1); dgrp(1, 0); g2(0); g1(1); dgrp(1, 1)
                    g2(1)

                def c_qk(sq, seg, hooks):
                    tok0 = seg * SEG
                    pend = []

                    def group(mt, which, cbase):
                        p = mt % 2
                        QRAW = QRAW2[0 if which == "k" else 1]; tQRAW = tQRAW2[0 if which == "k" else 1]
                        T1 = T1s[0 if which == "k" else 1]; tT1 = tT1s[0 if which == "k" else 1]
                        T2 = T2s[0 if which == "k" else 1]; tT2 = tT2s[0 if which == "k" else 1]
                        if mt < 2:
                            tabf = (EQR if which == "q" else EKR)[:, 0, p, :].unsqueeze(1).to_broadcast([128, NT, C])
                            tabb = (EQR if which == "q" else EKR)[:, 1, p, :].unsqueeze(1).to_broadcast([128, NT, C])
                            tabR = [tRETT]
                        else:
                            tabf = (EQG if which == "q" else EKG)[:, 0, p, :].rearrange("p (t c) -> p t c", t=NT)
                            tabb = (EQG if which == "q" else EKG)[:, 1, p, :].rearrange("p (t c) -> p t c", t=NT)
                            tabR = list(tEG)
                        chs = [sq["ch0"] + seg * NT + t_ for t_ in range(NT)]
                        if which == "q":
                            outf = QF[:, mt, (sq["ch0"] * C + tok0):(sq["ch0"] * C + tok0 + SEG)].rearrange("p (t c) -> p t c", t=NT)
                            outb = QB[:, mt, :].rearrange("p (t c) -> p t c", t=NT)
                            Wf = [tQF[c_] for c_ in chs]; Wb = [tQB[mt]]
                        else:
                            outf = KF[:, mt, :].rearrange("p (t c) -> p t c", t=NT)
                            outb = KB[:, mt, :].rearrange("p (t c) -> p t c", t=NT)
                            Wf = [tKF[mt]]; Wb = [tKB[mt]]
                        bq = proj_group(cbase, 128)
                        use_rope = rope and mt < 2
                        if use_rope:
                            tk.op("act", mk("copy", QRAW[:], pf(bq)[:, 0:SEG]), R=[tP[bq]], W=[tQRAW])

                        def stage2():
                            if use_rope:
                                br = nbank()
                                tk.op("pe", mk("matmul", pf(br)[:, 0:SEG], lhsT=PERM[:], rhs=QRAW[:], start=True, stop=True),
                                      R=[tCONST, tQRAW], W=[tP[br]])
                                tk.op("dve", mk("tensor_tensor", out=T1[:], in0=pf(br)[:, 0:SEG], in1=SIN[:, tok0:tok0 + SEG], op=ALU.mult),
                                      R=[tP[br], tROPE], W=[tT1])
                                tk.op("dve", mk("tensor_tensor", out=T2[:], in0=pf(bq)[:, 0:SEG], in1=COS[:, tok0:tok0 + SEG], op=ALU.mult),
                                      R=[tP[bq], tROPE], W=[tT2])
                                tk.op("dve", mk("tensor_tensor", out=T2[:], in0=T1[:], in1=T2[:], op=ALU.add), R=[tT1, tT2], W=[tT2])
                                src = T2[:].rearrange("p (t c) -> p t c", t=NT)
                                tk.op("dve", mk("tensor_tensor", out=outf, in0=src, in1=tabf, op=ALU.mult), R=[tT2] + tabR, W=Wf)
                                tk.op("dve", mk("tensor_tensor", out=outb, in0=src, in1=tabb, op=ALU.mult), R=[tT2] + tabR, W=Wb)
                            else:
                                src = pf(bq)[:, 0:SEG].rearrange("p (t c) -> p t c", t=NT)
                                tk.op("dve", mk("tensor_tensor", out=outf, in0=src, in1=tabf, op=ALU.mult), R=[tP[bq]] + tabR, W=Wf)
                                tk.op("dve", mk("tensor_tensor", out=outb, in0=src, in1=tabb, op=ALU.mult), R=[tP[bq]] + tabR, W=Wb)
                        return stage2

                    gi = 0
                    for mt in range(4):
                        cq, ck = col_q(mt)
                        for which, cbase in (("k", ck), ("q", cq)):
                            st2 = group(mt, which, cbase)
                            if pend:
                                pend.pop(0)()
                            pend.append(st2)
                            gi += 1
                            if gi == 2 and "t0" in hooks:
                                hooks["t0"]()
                            if gi == 5 and "t1" in hooks:
                                hooks["t1"]()
                    while pend:
                        pend.pop(0)()

                def e_pre(sq, seg, t):
                    cs = slice(t * C, (t + 1) * C)
                    for mt in range(4):
                        for d in range(2):
                            src = (KF if d == 0 else KB)[:, mt, cs]
                            tk.op("pe", mk("transpose", pb16(PK)[:, (mt * 2 + d) * 128:(mt * 2 + d + 1) * 128], src, IDN[:]),
                                  R=[tKF[mt], tKB[mt], tCONST], W=[tP[PK]], sig=(mt == 3 and d == 1))
                    tk.op("act", mk("copy", KTOK[:].rearrange("p s c -> p (s c)"), pb16(PK)), R=[tP[PK]], W=[tKTOK])

                def e_chunk(sq, seg, t):
                    chl = sq["ch0"] + seg * NT + t
                    obi = chl % 2
                    cs = slice(t * C, (t + 1) * C)
                    sbanks = [(PS, PK), (PA, PT)]

                    def scores(mt):
                        sct = SCT[mt % 2]; tsct = tSCT[mt % 2]
                        qf_ap = QF[:, mt, (chl * C):(chl + 1) * C]
                        for h in range(2):
                            hs = slice(h * 64, (h + 1) * 64)
                            sbank = sbanks[mt % 2][h]
                            for d in range(2):
                                kk = (KF if d == 0 else KB)[hs, mt, cs]
                                qq = qf_ap[hs, :] if d == 0 else QB[hs, mt, cs]
                                tk.op("pe", mk("matmul", pf(sbank)[:, d * 128:(d + 1) * 128], lhsT=kk, rhs=qq, start=True, stop=True),
                                      R=[tKF[mt], tKB[mt], tQB[mt], tQF[chl]], W=[tP[sbank]], sig=(d == 1))
                        for h in range(2):
                            sbank = sbanks[mt % 2][h]
                            tk.op("dve", mk("tensor_tensor", out=sct[:, 2 * h:2 * h + 2, :],
                                            in0=pf(sbank)[:, 0:256].rearrange("p (s c) -> p s c", s=2), in1=MASK[:, 2 * h:2 * h + 2, :],
                                            op=ALU.mult), R=[tP[sbank], tCONST], W=[tsct])

                    def kvp(mt):
                        for h in range(2):
                            hs = slice(h * 64, (h + 1) * 64)
                            hg = mt * 2 + h
                            vv = V[t][:, hg * 128:(hg + 1) * 128]
                            for d in range(2):
                                kvbank = PL if d == 0 else PB
                                tk.op("pe", mk("matmul", pf(kvbank)[hs, mt * 128:(mt + 1) * 128], lhsT=KTOK[:, mt * 2 + d, hs], rhs=vv,
                                               start=True, stop=True), R=[tKTOK, tV[t]], W=[tP[kvbank]], sig=(h == 1 and d == 1))

                    def omm(mt):
                        sct = SCT[mt % 2]; tsct = tSCT[mt % 2]
                        obank = PO0 if mt < 2 else PO1
                        for h in range(2):
                            hs = slice(h * 64, (h + 1) * 64)
                            hg = mt * 2 + h
                            oc = (hg % 4) * 128
                            vv = V[t][:, hg * 128:(hg + 1) * 128]
                            tk.op("pe", mk("matmul", pf(obank)[:, oc:oc + 128], lhsT=sct[:, h * 2, :], rhs=vv, start=True, stop=False),
                                  R=[tsct, tV[t]], W=[tP[obank]], sig=False)
                            zero_state = (not sq["s0"]) and seg == sq["T"] // SEG - 1 and t == NT - 1
                            tk.op("pe", mk("matmul", pf(obank)[:, oc:oc + 128], lhsT=sct[:, h * 2 + 1, :], rhs=vv, start=False, stop=zero_state),
                                  R=[tsct, tV[t]], W=[tP[obank]], sig=zero_state)
                            if not zero_state:
                                tk.op("pe", mk("matmul", pf(obank)[:, oc:oc + 128], lhsT=QB[hs, mt, cs], rhs=SNAP[hs, mt, 1, :], start=False, stop=True),
                                      R=[tQB[mt], tSNAP[1]], W=[tP[obank]], sig=True)

                    scores(0)
                    kvp(0)
                    for mt in range(4):
                        if mt + 1 < 4:
                            scores(mt + 1)
                            kvp(mt + 1)
                        omm(mt)
                    tk.op("act", mk("copy", OPB[obi][:, 0:512], pf(PO0)), R=[tP[PO0]], W=[tOPB[obi]])
                    tk.op("act", mk("copy", OPB[obi][:, 512:1024], pf(PO1)), R=[tP[PO1]], W=[tOPB[obi]])
                    gch = job["scr0"] + chl
                    tk.dma("sp", op_scr[gch], OPB[obi][:], R=[tOPB[obi]], W=[t_opscr[gch]], st=tOPB[obi])
                    tk.op("act", mk("copy", KVF[:, chl, :, :], pf(PL).rearrange("p (m e) -> p m e", m=4)), R=[tP[PL]], W=[tKVF[chl]])
                    tk.op("dve", mk("tensor_tensor", out=S[:, :, 1, :], in0=pf(PB).rearrange("p (m e) -> p m e", m=4), in1=S[:, :, 1, :], op=ALU.add),
                          R=[tP[PB], tS[1]], W=[tS[1]])
                    tk.op("pool", mk("tensor_tensor", out=S[:, :, 1, :], in0=S[:, :, 1, :], in1=DECB[:, t, :].unsqueeze(2).to_broadcast([128, 4, 128]),
                                     op=ALU.mult), R=[tS[1], tDECB[t]], W=[tS[1]])
                    tk.op("act", mk("copy", SNAP[:, :, 1, :], S[:, :, 1, :]), R=[tS[1]], W=[tSNAP[1]])

                xbs_next = seg_loads(0)
                a_norm(xbs_next)
                if len(segs) > 1:
                    xbs_next = seg_loads(1)
                for t_ in range(NT):
                    a_trans(t_, 0)
                for si, (sq, seg) in enumerate(segs):
                    nseg = sq["T"] // SEG
                    hcur[0] = si % 2
                    if seg == nseg - 1:
                        for mt in range(4):
                            if sq["s0"]:
                                base = st_ret if mt < 2 else st_gla
                                tk.dma("sp", S[:, mt, 1, :], state_dram(base, mt // 2, 1, mt % 2), W=[tS[1]])
                            else:
                                tk.op("pool", mk("memset", S[:, mt, 1, :], 0.0), W=[tS[1]])
                        tk.op("pool", mk("tensor_copy", SNAP[:, :, 1, :], S[:, :, 1, :]), R=[tS[1]], W=[tSNAP[1]])
                    has_next = si + 1 < len(segs)
                    hooks = {}
                    if has_next:
                        def hk_norm():
                            global_xbs = hooks["xbs"]
                            a_norm(global_xbs)
                        hooks["xbs"] = xbs_next
                        hooks["norm"] = hk_norm
                        hooks["t0"] = lambda nh=(si + 1) % 2: a_trans(0, nh)
                        hooks["t1"] = lambda nh=(si + 1) % 2: a_trans(1, nh)
                    b_gates_vz(sq, seg, hooks)
                    if has_next and si + 2 < len(segs):
                        xbs_next = seg_loads(si + 2)
                    c_qk(sq, seg, hooks)
                    e_pre(sq, seg, 1)
                    e_chunk(sq, seg, 1)
                    e_pre(sq, seg, 0)
                    e_chunk(sq, seg, 0)
                    if seg == 0 and sq["so"] is not None:
                        for mt in range(4):
                            dst = (nsr if mt < 2 else nsg)[sq["so"]]
                            tk.dma("sp", state_dram(dst, mt // 2, 1, mt % 2), S[:, mt, 1, :], R=[tS[1]], st=tS[1], is_out=True)


                stage("p1")
                ftiles = []
                for sq in job["seqs"]:
                    for cis in range(sq["T"] // C):
                        ftiles.append((sq, cis))
                nft = len(ftiles)
                PObanks = [(PO0, PO1), (PT, PL)]
                xb3 = {}

                def ld_x(i):
                    sq_, cis_ = ftiles[i]
                    xb3[i] = load_x(sq_["x"], cis_ * C)

                def ld_sz(i):
                    sq_, cis_ = ftiles[i]
                    chl_ = sq_["ch0"] + cis_
                    gch_ = job["scr0"] + chl_
                    bi_ = chl_ % 2
                    tk.dma("sp", SZB[bi_][:], sz_scr[gch_], R=[t_szscr[gch_]], W=[tSZB[bi_]], st=tSZB[bi_])

                def ld_op(i):
                    sq_, cis_ = ftiles[i]
                    chl_ = sq_["ch0"] + cis_
                    gch_ = job["scr0"] + chl_
                    bi_ = chl_ % 2
                    tk.dma("sp", OPB[bi_][:], op_scr[gch_], R=[t_opscr[gch_]], W=[tOPB[bi_]], st=tOPB[bi_])

                def s1(i):
                    sq, cis = ftiles[i]
                    chl = sq["ch0"] + cis
                    bi = chl % 2
                    pob = PObanks[i % 2]
                    if cis == 0:
                        for mt in range(4):
                            if sq["s0"]:
                                base = st_ret if mt < 2 else st_gla
                                tk.dma("sp", S[:, mt, 0, :], state_dram(base, mt // 2, 0, mt % 2), W=[tS[0]])
                            else:
                                tk.op("pool", mk("memset", S[:, mt, 0, :], 0.0), W=[tS[0]])
                    tk.op("act", mk("copy", SNAP[:, :, 0, :], S[:, :, 0, :]), R=[tS[0]], W=[tSNAP[0]])
                    for hg in range(8):
                        obank = pob[hg // 4]
                        hh = hg % 4
                        mt = hg // 2
                        hs = slice((hg % 2) * 64, (hg % 2 + 1) * 64)
                        zero_f = (cis == 0) and not sq["s0"]
                        tk.op("pe", mk("matmul", pf(obank)[:, hh * 128:(hh + 1) * 128], lhsT=IDN[:], rhs=OPB[bi][:, hg * 128:(hg + 1) * 128],
                                       start=True, stop=zero_f), R=[tCONST, tOPB[bi]], W=[tP[obank]], sig=(zero_f and hh == 3))
                        if not zero_f:
                            tk.op("pe", mk("matmul", pf(obank)[:, hh * 128:(hh + 1) * 128], lhsT=QF[hs, mt, chl * C:(chl + 1) * C],
                                           rhs=SNAP[hs, mt, 0, :], start=False, stop=True),
                                  R=[tQF[chl], tSNAP[0]], W=[tP[obank]], sig=(hh == 3))
                    tk.op("pool", mk("tensor_tensor", out=S[:, :, 0, :], in0=KVF[:, chl, :, :], in1=S[:, :, 0, :], op=ALU.add),
                          R=[tKVF[chl], tS[0]], W=[tS[0]])
                    tk.op("pool", mk("tensor_tensor", out=S[:, :, 0, :], in0=S[:, :, 0, :], in1=DECF[:, chl, :].unsqueeze(2).to_broadcast([128, 4, 128]),
                                     op=ALU.mult), R=[tS[0], tDECF[chl]], W=[tS[0]])
                    if cis == sq["T"] // C - 1 and sq["so"] is not None:
                        for mt in range(4):
                            dst = (nsr if mt < 2 else nsg)[sq["so"]]
                            tk.dma("sp", state_dram(dst, mt // 2, 0, mt % 2), S[:, mt, 0, :], R=[tS[0]], st=tS[0], is_out=True)

                def s2(i):
                    sq, cis = ftiles[i]
                    chl = sq["ch0"] + cis
                    bi = chl % 2
                    po_r, po_g = PObanks[i % 2]
                    OG = OG2[i % 2]; tOG = tOG2[i % 2]
                    for hh in range(4):
                        tk.op("dve", mk("bn_stats", out=ST[:, 4 + hh * 6:10 + hh * 6], in_=pf(po_r)[:, hh * 128:(hh + 1) * 128]),
                              R=[tP[po_r]], W=[tSTr])
                    for hh in range(4):
                        tk.op("act", mk("activation", out=ON[:, hh * 128:(hh + 1) * 128], in_=pf(po_g)[:, hh * 128:(hh + 1) * 128], func=AF.Square,
                                        scale=1.0 / math.sqrt(128.0), accum_out=ST[:, 48 + hh:49 + hh]), R=[tP[po_g]], W=[tON, tSTg])
                    for hh in range(4):
                        tk.op("dve", mk("bn_aggr", out=ST[:, 28 + hh * 2:30 + hh * 2], in_=ST[:, 4 + hh * 6:10 + hh * 6]), R=[tSTr], W=[tSTr])

                def s2b(i):
                    sq, cis = ftiles[i]
                    chl = sq["ch0"] + cis
                    bi = chl % 2
                    po_r, po_g = PObanks[i % 2]
                    OG = OG2[i % 2]; tOG = tOG2[i % 2]
                    mvv = ST[:, 28:36].rearrange("p (h t) -> p h t", t=2)
                    tk.op("act", mk("activation", out=ST[:, 36:40].unsqueeze(2), in_=mvv[:, :, 1:2], func=AF.Ln, bias=EPS), R=[tSTr], W=[tSTr])
                    tk.op("act", mk("activation", out=ST[:, 40:44], in_=ST[:, 36:40], func=AF.Exp, scale=-0.5), R=[tSTr], W=[tSTr])
                    tk.op("dve", mk("scalar_tensor_tensor", out=ST[:, 44:48].unsqueeze(2), in0=mvv[:, :, 0:1], scalar=-1.0, in1=ST[:, 40:44].unsqueeze(2),
                                    op0=ALU.mult, op1=ALU.mult), R=[tSTr], W=[tSTr])
                    tk.op("act", mk("activation", out=ST[:, 52:56], in_=ST[:, 48:52], func=AF.Ln, bias=EPS), R=[tSTg], W=[tSTg])
                    tk.op("act", mk("activation", out=ST[:, 56:60], in_=ST[:, 52:56], func=AF.Exp, scale=-0.5), R=[tSTg], W=[tSTg])
                    for hh in range(4):
                        tk.op("act", mk("activation", out=ON[:, hh * 128:(hh + 1) * 128], in_=pf(po_r)[:, hh * 128:(hh + 1) * 128], func=AF.Identity,
                                        scale=ST[:, 40 + hh:41 + hh], bias=ST[:, 44 + hh:45 + hh]), R=[tP[po_r], tSTr], W=[tON])
                    for hh in range(4):
                        tk.op("dve", mk("scalar_tensor_tensor", out=OG[:, 512 + hh * 128:512 + (hh + 1) * 128], in0=pf(po_g)[:, hh * 128:(hh + 1) * 128],
                                        scalar=ST[:, 56 + hh:57 + hh], in1=SZB[bi][:, 512 + hh * 128:512 + (hh + 1) * 128],
                                        op0=ALU.mult, op1=ALU.mult), R=[tP[po_g], tSTg, tSZB[bi]], W=[tOG])
                    tk.op("pool", mk("tensor_tensor", out=OG[:, 0:512], in0=ON[:], in1=SZB[bi][:, 0:512], op=ALU.mult),
                          R=[tON, tSZB[bi]], W=[tOG])

                def s3(i):
                    OG = OG2[i % 2]; tOG = tOG2[i % 2]
                    for k in range(8):
                        tk.op("pe", mk("transpose", pb16(PK)[:, k * 128:(k + 1) * 128], OG[:, k * 128:(k + 1) * 128], IDN[:]),
                              R=[tOG, tCONST], W=[tP[PK]], sig=(k == 7))
                    tk.op("act", mk("copy", OGT[:].rearrange("p k c -> p (k c)"), pb16(PK)), R=[tP[PK]], W=[tOGT])

                def s4mm(i):
                    for half in range(2):
                        bank = PA if half == 0 else PB
                        for k in range(8):
                            tk.op("pe", mk("matmul", pf(bank), lhsT=OGT[:, k, :], rhs=WOUT[:, k, half * 512:(half + 1) * 512],
                                           start=(k == 0), stop=(k == 7)), R=[tOGT, tWOUT], W=[tP[bank]], sig=(k == 7))

                def s4add(i):
                    xb = xb3[i]
                    for half in range(2):
                        bank = PA if half == 0 else PB
                        tk.op("dve", mk("tensor_tensor", out=Y[:, half * 512:(half + 1) * 512], in0=pf(bank),
                                        in1=XT[xb][:, half * 512:(half + 1) * 512], op=ALU.add), R=[tP[bank], tXT[xb]], W=[tY])

                def s4y_act(i):
                    tk.op("act", mk("activation", out=XN2[0][:], in_=Y[:], func=AF.Square, scale=1.0 / 32.0, accum_out=ST[:, 60:61]), R=[tY], W=[tXN2[0], tSTy])
                    tk.op("act", mk("activation", out=ST[:, 61:62], in_=ST[:, 60:61], func=AF.Ln, bias=EPS), R=[tSTy], W=[tSTy])
                    tk.op("act", mk("activation", out=ST[:, 62:63], in_=ST[:, 61:62], func=AF.Exp, scale=-0.5), R=[tSTy], W=[tSTy])

                def s4y_dve(i):
                    sq, cis = ftiles[i]
                    tk.op("dve", mk("scalar_tensor_tensor", out=YOap, in0=Y[:], scalar=ST[:, 62:63], in1=FNW[:], op0=ALU.mult, op1=ALU.mult),
                          R=[tY, tSTy, tFNW], W=[tV[0], tV[1]])
                    tk.dma("sp", sq["y"][cis * C:(cis + 1) * C, :], YOap, R=[tV[0], tV[1]], st=tV[0], is_out=True)
                    stage("p3")

                ld_op(0); ld_sz(0)
                if nft > 1:
                    ld_op(1)
                s1(0)
                for j in range(nft + 2):
                    if j + 2 < nft:
                        ld_op(j + 2)
                    if j + 1 < nft:
                        ld_sz(j + 1)
                    if j < nft:
                        ld_x(j)
                    if 0 <= j - 1 < nft:
                        s4mm(j - 1)
                    if j + 1 < nft:
                        s1(j + 1)
                    if j < nft:
                        s2(j)
                    if 0 <= j - 2 < nft:
                        s4y_dve(j - 2)
                    if j < nft:
                        s2b(j)
                    if 0 <= j - 1 < nft:
                        s4add(j - 1)
                    if j < nft:
                        s3(j)
                    if 0 <= j - 1 < nft:
                        s4y_act(j - 1)

        try:
            _main_scope()
        except _Stop:
            pass
        tk.finish("sp")
        with nc.Block() as block:
            tk.replay(block)
    nc._dbg_outs = dbg_outs
    nc._ninst = {k: v.ninst for k, v in tk.E.items()}
    return nc


def _host_consts():
    bf = ml_dtypes.bfloat16
    ii = np.arange(C)
    idn = np.eye(128, dtype=np.float32).astype(bf)
    tri = np.zeros((128, 2, 128), np.float32)
    tri[:, 0, :] = np.where(ii[:, None] <= ii[None, :], -1.0 / 16.0, 0.0)
    tri[:, 1, :] = np.where(ii[:, None] >= ii[None, :], -1.0 / 16.0, 0.0)
    mask = np.zeros((128, 4, 128), np.float32)
    mf = (ii[:, None] <= ii[None, :]).astype(np.float32)
    mb = (ii[:, None] >= ii[None, :]).astype(np.float32)
    for h in range(2):
        mask[:, h * 2 + 0, :] = mf
        mask[:, h * 2 + 1, :] = mb
    pos = np.zeros((128, 2, 128), np.float32)
    pos[:, 0, :] = (ii + 1)[None, :]
    pos[:, 1, :] = (C - ii)[None, :]
    t = np.arange(TS)
    rr = (t // 64).astype(np.float32)
    cc = (t % 64).astype(np.float32)
    inv = (10000.0 ** (-np.arange(16, dtype=np.float32) / 16)).astype(np.float32)
    ang = np.concatenate([rr[:, None] * inv, cc[:, None] * inv], axis=-1).astype(np.float32)
    cosT = np.cos(ang).T.astype(np.float32)
    sinT = np.sin(ang).T.astype(np.float32)
    cos128 = np.ascontiguousarray(np.tile(cosT, (4, 1)).astype(np.float32))
    sin128 = np.ascontiguousarray(np.tile(sinT, (4, 1)).astype(np.float32))
    i2 = np.eye(2, dtype=np.float32)
    perm = np.zeros((128, 128), np.float32)
    for m in range(128):
        if m % 64 < 32:
            perm[m + 32, m] = -1.0
        else:
            perm[m - 32, m] = 1.0
    sel = np.zeros((2, 2, 128), np.float32)
    sel[0, 0, :] = 1.0
    sel[1, 1, :] = 1.0
    return dict(c_idn=idn, c_tri=tri.astype(bf), c_mask=mask.astype(bf), c_pos=pos, c_cos=cos128, c_sin=sin128, c_perm=perm.astype(bf), c_i2=i2, c_sel=sel)


_NC_CACHE = {}


def kernel(x_prompt, x_sample, c, state_ret, state_gla, c_ctx, w_mod, b_mod, w_in,
           ret_log_decay, gla_w_alpha, gla_b_alpha, gla_norm_w, w_out, final_norm_w, _dbg=None):
    f = lambda a: np.ascontiguousarray(np.asarray(a, dtype=np.float32))
    x_prompt, x_sample, c, state_ret, state_gla, c_ctx = map(f, (x_prompt, x_sample, c, state_ret, state_gla, c_ctx))
    w_mod, b_mod, w_in, ret_log_decay = map(f, (w_mod, b_mod, w_in, ret_log_decay))
    gla_w_alpha, gla_b_alpha, gla_norm_w, w_out, final_norm_w = map(f, (gla_w_alpha, gla_b_alpha, gla_norm_w, w_out, final_norm_w))

    key = None if _dbg is None else tuple(sorted(_dbg))
    if key not in _NC_CACHE:
        _NC_CACHE[key] = build_nc(_dbg)
    nc = _NC_CACHE[key]
    consts = _host_consts()

    ld = ret_log_decay[0]
    ldcols = np.zeros((128, 4), np.float32)
    for d in range(2):
        for p in range(2):
            ldcols[0:64, d * 2 + p] = ld[d, 2 * p]
            ldcols[64:128, d * 2 + p] = ld[d, 2 * p + 1]
    wa_aug = np.zeros((33, 512), np.float32)
    wa_aug[0:16, 0:256] = gla_w_alpha[0, 0]
    wa_aug[16:32, 256:512] = gla_w_alpha[0, 1]
    wa_aug[32, 0:256] = gla_b_alpha[0, 0]
    wa_aug[32, 256:512] = gla_b_alpha[0, 1]
    rowsc = np.ones((128, 8), np.float32)
    rowsc[:, 4:8] = gla_norm_w[0][:, None]
    fnw_bc = np.ascontiguousarray(np.broadcast_to(final_norm_w[None, :], (128, D)))
    b_mod2 = np.ascontiguousarray(np.broadcast_to(b_mod[0][None, :], (2, 3 * D)))
    shared = dict(w_mod=w_mod[0], b_mod2=b_mod2, w_in=w_in[0], ldcols=ldcols, wa_aug=wa_aug, rowsc=rowsc,
                  w_out=w_out[0], fnw_bc=fnw_bc, **consts)

    in_maps = []
    for core in range(NCORES):
        cv = np.stack([c_ctx, c[core]], axis=0)
        cvecT = np.ascontiguousarray(cv.reshape(2, 8, 128).transpose(2, 1, 0).reshape(128, 16))
        m = dict(shared)
        m.update(xp=np.ascontiguousarray(x_prompt[core * NPS:(core + 1) * NPS].reshape(NPS * TP, D)),
                 xs=np.ascontiguousarray(x_sample[core]),
                 cvecT=cvecT,
                 st_ret=np.ascontiguousarray(state_ret[core, 0]),
                 st_gla=np.ascontiguousarray(state_gla[core, 0]))
        in_maps.append(m)

    res = run_bass_kernel_spmd(nc, in_maps, core_ids=list(range(NCORES)))
    rs = res.results
    y_prompt = np.concatenate([r["yp"].reshape(NPS, TP, D) for r in rs], axis=0).astype(np.float32)
    y_sample = np.stack([r["ys"] for r in rs], axis=0).astype(np.float32)
    new_ret = np.concatenate([r["nsr"].reshape(NPS, 1, 2, 4, 64, 128) for r in rs], axis=0).astype(np.float32)
    new_gla = np.concatenate([r["nsg"].reshape(NPS, 1, 2, 4, 64, 128) for r in rs], axis=0).astype(np.float32)
    if _dbg is not None:
        kernel._dbg_results = [{k: r[v] for k, v in nc._dbg_outs.items()} for r in rs]
    return (y_prompt, y_sample, new_ret, new_gla)
```

```python
import math
from contextlib import ExitStack

import numpy as np
import ml_dtypes

import concourse.bass as bass
import concourse.mybir as mybir
from concourse.bass_utils import run_bass_kernel_spmd

F32 = mybir.dt.float32
BF16 = mybir.dt.bfloat16
ALU = mybir.AluOpType
AF = mybir.ActivationFunctionType

NCORES = 8
D = 1024
DIN = 3104
TP = 256
TS = 2048
NPS = 4
C = 128
SEG = 256
NT = SEG // C
EPS = 1e-6
LN8 = math.log(0.125)


class TT:
    __slots__ = ("name", "w", "r", "psum", "dsem", "dn")

    def __init__(self, name, psum=False):
        self.name = name
        self.w = None
        self.r = {}
        self.psum = psum
        self.dsem = None
        self.dn = 0


class Eng:
    def __init__(self, name, sem):
        self.name = name
        self.sem = sem
        self.n = 0
        self.seen = {}
        self.prog = []
        self.ninst = 0


class Trk:
    def __init__(self, sems):
        self.free_sems = list(sems)
        self.E = {nm: Eng(nm, self.free_sems.pop()) for nm in ("pe", "act", "dve", "pool", "sp")}
        self.out_events = []

    def new_sem(self):
        return self.free_sems.pop()

    def _deps(self, eng, R, W):
        need = {}

        def add(ev, raw):
            if ev is None:
                return
            sem, val, en = ev
            if en == eng.name:
                if eng.name in ("pe", "sp"):
                    return
            if need.get(id(sem), (None, 0))[1] < val:
                need[id(sem)] = (sem, val, en)

        for t in R:
            add(t.w, True)
            if t.psum:
                for ev in t.r.values():
                    add(ev, False)
        for t in W:
            add(t.w, False)
            for ev in t.r.values():
                add(ev, False)
        out = []
        for sem, val, en in need.values():
            if eng.seen.get(id(sem), 0) >= val:
                continue
            if en in self.E and val > self.E[en].n:
                raise RuntimeError(f"dependency on unsignalled instruction of {en} from {eng.name}")
            out.append((sem, val))
        return out

    def _wait(self, eng, deps):
        for sem, val in deps:
            eng.prog.append(("w", sem, val))
            eng.seen[id(sem)] = val

    def _record(self, ev, R, W):
        for t in R:
            t.r[id(ev[0])] = ev
        for t in W:
            t.w = ev
            t.r = {}

    def op(self, en, fn, R=(), W=(), sig=True):
        eng = self.E[en]
        self._wait(eng, self._deps(eng, R, W))
        eng.ninst += 1
        if sig:
            eng.prog.append(("i", fn, eng.sem, 1))
            eng.n += 1
            ev = (eng.sem, eng.n, en)
        else:
            eng.prog.append(("i", fn, None, 0))
            ev = (eng.sem, eng.n + 1, en)
        self._record(ev, R, W)

    def dma(self, qn, out, in_, R=(), W=(), st=None, is_out=False):
        eng = self.E[qn]
        self._wait(eng, self._deps(eng, R, W))
        if st is None:
            st = (list(W) + list(R))[0]
        if st.dsem is None:
            st.dsem = self.new_sem()
        eng.prog.append(("i", (mk("dma_start", out=out, in_=in_)), st.dsem, 16))
        st.dn += 1
        ev = (st.dsem, 16 * st.dn, "dma")
        self._record(ev, R, W)
        if is_out:
            self.out_events.append(ev)

    def barrier(self, dma_tiles=()):
        for eng in self.E.values():
            for t in dma_tiles:
                if t.dsem is not None and eng.seen.get(id(t.dsem), 0) < 16 * t.dn:
                    eng.prog.append(("w", t.dsem, 16 * t.dn))
                    eng.seen[id(t.dsem)] = 16 * t.dn
            for o in self.E.values():
                if o is eng or o.n == 0:
                    continue
                if eng.seen.get(id(o.sem), 0) < o.n:
                    eng.prog.append(("w", o.sem, o.n))
                    eng.seen[id(o.sem)] = o.n

    def finish(self, en="sp"):
        eng = self.E[en]
        best = {}
        for sem, val, _ in self.out_events:
            if best.get(id(sem), (None, 0))[1] < val:
                best[id(sem)] = (sem, val)
        for sem, val in best.values():
            eng.prog.append(("w", sem, val))

    def replay(self, block):
        def run(eng):
            def f(q):
                for e in eng.prog:
                    if e[0] == "w":
                        q.wait_ge(e[1], e[2])
                    else:
                        inst = e[1](q)
                        if e[2] is not None:
                            inst.then_inc(e[2], e[3])
            return f
        block.tensor(run(self.E["pe"]))
        block.scalar(run(self.E["act"]))
        block.vector(run(self.E["dve"]))
        block.gpsimd(run(self.E["pool"]))
        block.sync(run(self.E["sp"]))


def mk(name, *args, **kwargs):
    return lambda q: getattr(q, name)(*args, **kwargs)


class _Stop(Exception):
    pass


STOP = None


def build_nc(dbg=None):
    nc = bass.Bass("TRN2", target_bir_lowering=False)

    stage_cnt = {}

    def stage(name):
        if STOP is None:
            return
        stage_cnt[name] = stage_cnt.get(name, 0) + 1
        nm, _, cnt = STOP.partition(":")
        if nm == name and stage_cnt[name] >= int(cnt or 1):
            raise _Stop()

    def din(name, shape, dt=F32):
        return nc.dram_tensor(name, list(shape), dt, kind="ExternalInput").ap()

    def dout(name, shape, dt=F32):
        return nc.dram_tensor(name, list(shape), dt, kind="ExternalOutput").ap()

    xp = din("xp", [NPS * TP, D])
    xs = din("xs", [TS, D])
    cvecT = din("cvecT", [128, 16])
    st_ret = din("st_ret", [2, 4, 64, 128])
    st_gla = din("st_gla", [2, 4, 64, 128])
    w_mod = din("w_mod", [D, 3 * D])
    b_mod2 = din("b_mod2", [2, 3 * D])
    w_in = din("w_in", [D, DIN])
    ldcols = din("ldcols", [128, 4])
    wa_aug = din("wa_aug", [33, 512])
    rowsc = din("rowsc", [128, 8])
    w_out = din("w_out", [D, D])
    fnw_bc = din("fnw_bc", [128, D])
    c_idn = din("c_idn", [128, 128], BF16)
    c_tri = din("c_tri", [128, 2, 128], BF16)
    c_mask = din("c_mask", [128, 4, 128], BF16)
    c_pos = din("c_pos", [128, 2, 128])
    c_cos = din("c_cos", [128, TS])
    c_sin = din("c_sin", [128, TS])
    c_perm = din("c_perm", [128, 128], BF16)
    c_i2 = din("c_i2", [2, 2])
    c_sel = din("c_sel", [2, 2, 128])

    yp = dout("yp", [NPS * TP, D])
    ys = dout("ys", [TS, D])
    nsr = dout("nsr", [NPS, 2, 4, 64, 128])
    nsg = dout("nsg", [NPS, 2, 4, 64, 128])

    NCH = (NPS * TP + TS) // C
    sz_scr = nc.dram_tensor("sz_scr", [NCH, 128, D], BF16, kind="Internal").ap()
    op_scr = nc.dram_tensor("op_scr", [NCH, 128, D], BF16, kind="Internal").ap()
    wo1_scr = nc.dram_tensor("wo1_scr", [128, 8, D], BF16, kind="Internal").ap()
    t_szscr = [TT(f"szscr{i}") for i in range(NCH)]
    t_opscr = [TT(f"opscr{i}") for i in range(NCH)]
    t_wo1scr = TT("wo1scr")

    dbg_outs = {}
    dbg_tiles = []

    with ExitStack() as es:
        sems = [es.enter_context(nc.semaphore(f"s{i}")) for i in range(100)]
        tk = Trk(sems)

        def sb(name, shape, dt, scope=es):
            return scope.enter_context(nc.sbuf_tensor(name, list(shape), dt))

        PB_ = [es.enter_context(nc.psum_tensor(f"ps{i}", [128, 512], F32)) for i in range(8)]
        tP = [TT(f"ps{i}", True) for i in range(8)]
        PA, PB, PT, PK, PL, PS, PO0, PO1 = range(8)

        def pf(i):
            return PB_[i][:]

        def pb16(i):
            return PB_[i][:].bitcast(BF16)

        WIN = sb("WIN", [128, 8, DIN], BF16); tWIN = TT("WIN")
        PERM = sb("PERM", [128, 128], BF16)
        WOUT = sb("WOUT", [128, 8, D], BF16); tWOUT = TT("WOUT")
        COS = sb("COS", [128, TS], F32); SIN = sb("SIN", [128, TS], F32); tROPE = TT("ROPE")
        IDN = sb("IDN", [128, 128], BF16); TRI = sb("TRI", [128, 2, 128], BF16)
        MASK = sb("MASK", [128, 4, 128], BF16); tCONST = TT("CONST")
        EQR = sb("EQR", [128, 2, 2, 128], BF16); EKR = sb("EKR", [128, 2, 2, 128], BF16)
        DECR = sb("DECR", [128, 4], F32); tRETT = TT("RETT")
        WA = sb("WA", [33, 512], BF16); tWA = TT("WA")
        SHT = sb("SHT", [128, 8, 2], F32); SC1T = sb("SC1T", [128, 8, 2], F32); tMOD = TT("MOD")
        FNW = sb("FNW", [128, D], F32); tFNW = TT("FNW")
        ROWSC = sb("ROWSC", [128, 8], F32)

        def dbg_dump(name, ap, shape, dt, R):
            if dbg is None or name not in dbg:
                return
            o = nc.dram_tensor("dbg_" + name, list(shape), dt, kind="ExternalOutput").ap()
            dbg_outs[name] = "dbg_" + name
            t = TT("dbg_" + name)
            dbg_tiles.append(t)
            tk.dma("sp", o, ap, R=R, W=[t], st=t, is_out=True)

        with ExitStack() as ss:
            WM = [sb(f"WM{i}", [128, DIN], F32, ss) for i in range(3)]; tWM = [TT("WM0"), TT("WM1"), TT("WM2")]
            MSB = sb("MSB", [2, 3 * D], F32, ss); tMSB = TT("MSB")
            BM2 = sb("BM2", [2, 3 * D], F32, ss); tBM2 = TT("BM2")
            GATE = sb("GATE", [128, 2, D], F32, ss); tGATE = TT("GATE")
            WOS = [sb(f"WOS{i}", [128, D], F32, ss) for i in range(2)]; tWOS = [TT("WOS0"), TT("WOS1")]
            WO1 = sb("WO1", [128, 8, D], BF16, ss); tWO1 = TT("WO1")
            CV = sb("CV", [128, 16], F32, ss); SCT0 = sb("SILC", [128, 16], F32, ss); tCV = TT("CV")
            LD = sb("LD", [128, 4], F32, ss); NLD = sb("NLD", [128, 4], F32, ss); tLD = TT("LD")
            POS = sb("POS", [128, 2, 128], F32, ss)
            WAS = sb("WAS", [33, 512], F32, ss); tWAS = TT("WAS")
            I2 = sb("I2", [2, 2], F32, ss); SEL = sb("SEL", [2, 2, 128], F32, ss)

            tk.dma("sp", CV[:], cvecT, W=[tCV])
            tk.dma("sp", WM[0][:, 0:3 * D], w_mod[0:128, :], W=[tWM[0]])
            tk.dma("sp", WM[1][:, 0:3 * D], w_mod[128:256, :], W=[tWM[1]])
            tk.dma("sp", WM[2][:, 0:3 * D], w_mod[256:384, :], W=[tWM[2]])
            tk.dma("sp", IDN[:], c_idn, W=[tCONST]); tk.dma("sp", TRI[:], c_tri, W=[tCONST], st=tCONST)
            tk.dma("sp", MASK[:], c_mask, W=[tCONST], st=tCONST)
            tk.dma("sp", PERM[:], c_perm, W=[tCONST], st=tCONST)
            tk.dma("sp", POS[:], c_pos, W=[tCONST], st=tCONST)
            tk.dma("sp", I2[:], c_i2, W=[tCONST], st=tCONST); tk.dma("sp", SEL[:], c_sel, W=[tCONST], st=tCONST)
            tk.dma("sp", ROWSC[:], rowsc, W=[tCONST], st=tCONST)
            tk.dma("sp", LD[:], ldcols, W=[tLD])
            tk.dma("sp", WAS[:], wa_aug, W=[tWAS])
            tk.dma("sp", BM2[:], b_mod2, W=[tBM2])
            tCONST.w = (tCONST.dsem, 16 * tCONST.dn, "dma")

            tk.op("act", mk("activation", out=SCT0[:], in_=CV[:], func=AF.Silu), R=[tCV], W=[tCV])
            tk.op("act", mk("copy", WA[:], WAS[:]), R=[tWAS], W=[tWA])
            tk.op("dve", mk("tensor_scalar", out=NLD[:], in0=LD[:], scalar1=-1.0, scalar2=None, op0=ALU.mult), R=[tLD], W=[tLD])
            for d in range(2):
                for p in range(2):
                    c_ = d * 2 + p
                    tk.op("act", mk("activation", out=EQR[:, d, p, :], in_=POS[:, d, :], func=AF.Exp,
                                                                          scale=LD[:, c_:c_ + 1]), R=[tLD, tCONST], W=[tRETT])
                    tk.op("act", mk("activation", out=EKR[:, d, p, :], in_=POS[:, d, :], func=AF.Exp,
                                                                          scale=NLD[:, c_:c_ + 1], bias=LN8), R=[tLD, tCONST], W=[tRETT])
            tk.op("act", mk("activation", out=DECR[:], in_=LD[:], func=AF.Exp, scale=float(C)), R=[tLD], W=[tRETT])
            mbanks = [PA, PB, PL, PS, PO0, PO1]
            for k in range(8):
                for n in range(6):
                    tk.op("pe", mk("matmul", pf(mbanks[n])[0:2, :], lhsT=SCT0[:, 2 * k:2 * k + 2],
                                   rhs=WM[k % 3][:, n * 512:(n + 1) * 512], start=(k == 0), stop=(k == 7)),
                          R=[tCV, tWM[k % 3]], W=[tP[mbanks[n]]], sig=(n == 5))
                if k + 3 < 8:
                    tk.dma("sp", WM[k % 3][:, 0:3 * D], w_mod[(k + 3) * 128:(k + 4) * 128, :], W=[tWM[k % 3]])
                else:
                    kk = k - 5
                    tk.dma("sp", WM[k % 3][:], w_in[kk * 128:(kk + 1) * 128, :], W=[tWM[k % 3]])
                if k == 5:
                    tk.dma("sp", WOS[0][:], w_out[0:128, :], W=[tWOS[0]])
                    tk.dma("sp", WOS[1][:], w_out[128:256, :], W=[tWOS[1]])
            for n in range(6):
                tk.op("dve", mk("tensor_tensor", out=MSB[:, n * 512:(n + 1) * 512], in0=pf(mbanks[n])[0:2, :],
                                                             in1=BM2[:, n * 512:(n + 1) * 512], op=ALU.add),
                      R=[tP[mbanks[n]], tBM2], W=[tMSB])
            for j in range(16):
                tk.op("pe", mk("matmul", pf(PA)[:, 2 * j:2 * j + 2], lhsT=MSB[0:2, j * 128:(j + 1) * 128],
                                                     rhs=I2[:], start=True, stop=True),
                      R=[tMSB, tCONST], W=[tP[PA]], sig=(j == 15))
            tk.op("dve", mk("tensor_copy", SHT[:].rearrange("p k c -> p (k c)"), pf(PA)[:, 0:16]), R=[tP[PA]], W=[tMOD])
            tk.op("dve", mk("tensor_scalar", out=SC1T[:].rearrange("p k c -> p (k c)"), in0=pf(PA)[:, 16:32],
                                                    scalar1=1.0, scalar2=None, op0=ALU.add), R=[tP[PA]], W=[tMOD])
            for kind in range(2):
                for n in range(2):
                    bank = PB if n == 0 else PL
                    tk.op("pe", mk("matmul",
                        pf(bank), lhsT=SEL[0:2, kind, :], rhs=MSB[0:2, 2048 + n * 512:2048 + (n + 1) * 512],
                        start=True, stop=True), R=[tMSB, tCONST], W=[tP[bank]])
                    tk.op("act", mk("copy", GATE[:, kind, n * 512:(n + 1) * 512], pf(bank)),
                          R=[tP[bank]], W=[tGATE])

            tk.dma("sp", COS[:], c_cos, W=[tROPE]); tk.dma("sp", SIN[:], c_sin, W=[tROPE], st=tROPE)
            tk.dma("sp", FNW[:], fnw_bc, W=[tFNW])
            for kk in range(8):
                b_ = (kk + 5) % 3
                tk.op("act", mk("copy", WIN[:, kk, :], WM[b_][:]), R=[tWM[b_]], W=[tWIN])
                if kk + 3 < 8:
                    tk.dma("sp", WM[b_][:], w_in[(kk + 3) * 128:(kk + 4) * 128, :], W=[tWM[b_]])
                k = kk
                tk.op("dve", mk("scalar_tensor_tensor", out=WOUT[:, k, :], in0=WOS[k % 2][:], scalar=ROWSC[:, k:k + 1],
                                in1=GATE[:, 0, :], op0=ALU.mult, op1=ALU.mult), R=[tWOS[k % 2], tGATE, tCONST], W=[tWOUT])
                tk.op("dve", mk("scalar_tensor_tensor", out=WO1[:, k, :], in0=WOS[k % 2][:], scalar=ROWSC[:, k:k + 1],
                                in1=GATE[:, 1, :], op0=ALU.mult, op1=ALU.mult), R=[tWOS[k % 2], tGATE, tCONST], W=[tWO1])
                if k + 2 < 8:
                    tk.dma("sp", WOS[k % 2][:], w_out[(k + 2) * 128:(k + 3) * 128, :], W=[tWOS[k % 2]])
            tk.dma("sp", wo1_scr, WO1[:], R=[tWO1], W=[t_wo1scr], st=tWO1)

            dbg_dump("SHT", SHT[:], [128, 8, 2], F32, [tMOD])
            dbg_dump("SC1T", SC1T[:], [128, 8, 2], F32, [tMOD])
            dbg_dump("GATE", GATE[:], [128, 2, D], F32, [tGATE])
            dbg_dump("EQR", EQR[:], [128, 2, 2, 128], BF16, [tRETT])
            tk.barrier([tWO1] + dbg_tiles)
        def _main_scope():
            KVF = sb("KVF", [128, 16, 4, 128], F32); tKVF = [TT(f"KVF{i}") for i in range(16)]
            QF = sb("QF", [128, 4, TS], BF16); tQF = [TT(f"QF{i}") for i in range(16)]
            DECF = sb("DECF", [128, 16, 4], F32); tDECF = [TT(f"DECF{i}") for i in range(16)]
            S = sb("S", [128, 4, 2, 128], F32); tS = [TT("Sf"), TT("Sb")]
            SNAP = sb("SNAP", [128, 4, 2, 128], BF16); tSNAP = [TT("SNf"), TT("SNb")]
            XT = [sb(f"XT{i}", [128, D], F32) for i in range(2)]; tXT = [TT(f"XT{i}") for i in range(2)]
            XN2 = [sb(f"XN{i}", [128, D], BF16) for i in range(NT)]; tXN2 = [TT(f"XN{i}") for i in range(NT)]
            STX = sb("STX", [128, 8], F32); tSTX = [TT("STX0"), TT("STX1")]
            HT2 = [sb(f"HT{i}", [128, 8, SEG], BF16) for i in range(2)]; tHT2 = [[TT(f"HT{i}_{t}") for t in range(NT)] for i in range(2)]
            hcur = [0]
            LRT = sb("LRT", [33, SEG], BF16); tLRT = TT("LRT")
            SPB = sb("SPB", [128, 512], BF16); tSPB = TT("SPB")
            EQG = sb("EQG", [128, 2, 2, SEG], BF16); EKG = sb("EKG", [128, 2, 2, SEG], BF16); tEG = [TT(f"EG{i}") for i in range(NT)]
            DECB = sb("DECB", [128, NT, 4], F32); tDECB = [TT(f"DECB{i}") for i in range(NT)]
            T1s = [sb(f"T1_{i}", [128, SEG], F32) for i in range(1)] * 2; T2s = [sb(f"T2_{i}", [128, SEG], F32) for i in range(1)] * 2
            tT1s = [TT("T1_0")] * 2; tT2s = [TT("T2_0")] * 2
            QRAW2 = [sb(f"QRAW{i}", [128, SEG], BF16) for i in range(2)]; tQRAW2 = [TT("QRAW0"), TT("QRAW1")]
            tT1 = TT("T1"); tT2 = TT("T2"); tRQ = TT("RQ")
            QB = sb("QB", [128, 4, SEG], BF16); KF = sb("KF", [128, 4, SEG], BF16); KB = sb("KB", [128, 4, SEG], BF16)
            tQB = [TT(f"QB{i}") for i in range(4)]; tKF = [TT(f"KF{i}") for i in range(4)]; tKB = [TT(f"KB{i}") for i in range(4)]
            VY = sb("VY", [128, NT * D], BF16)
            V = [VY[:, i * D:(i + 1) * D] for i in range(NT)]; tV = [TT(f"V{i}") for i in range(NT)]
            SZB = [sb(f"SZB{i}", [128, D], BF16) for i in range(2)]; tSZB = [TT("SZB0"), TT("SZB1")]
            OPB = [sb(f"OPB{i}", [128, D], BF16) for i in range(2)]; tOPB = [TT("OPB0"), TT("OPB1")]
            KTOK = sb("KTOK", [128, 8, 128], BF16); tKTOK = TT("KTOK")
            SCT = [sb(f"SCT{i}", [128, 4, 128], BF16) for i in range(2)]; tSCT = [TT("SCT0"), TT("SCT1")]
            ON = SPB; tON = tSPB
            OG2 = [sb("OG0", [128, D], BF16)] * 2; tOG2 = [TT("OG0")] * 2
            OGT = KTOK; tOGT = tKTOK
            Y = sb("Y", [128, D], F32); tY = TT("Y")
            YOap = VY[:].bitcast(F32)
            ETMPap = Y[:, 0:512]; tETMP = tY
            ST = sb("ST", [128, 64], F32); tST = TT("ST"); tSTr = TT("STr"); tSTg = TT("STg"); tSTy = TT("STy")
            tk.op("pool", mk("memset", LRT[32:33, :], 1.0), W=[tLRT])
            stage("setup")

            jobs = []
            seqsA = []
            for s in range(NPS):
                seqsA.append(dict(x=xp[s * TP:(s + 1) * TP, :], y=yp[s * TP:(s + 1) * TP, :], T=TP, ch0=2 * s, s0=False, so=s))
            jobs.append(dict(kind=0, rope=False, seqs=seqsA, scr0=0))
            jobs.append(dict(kind=1, rope=True, seqs=[dict(x=xs, y=ys, T=TS, ch0=0, s0=True, so=None)], scr0=8))

            def col_q(mt):
                if mt < 2:
                    return mt * 128, 256 + mt * 128
                return 1536 + (mt - 2) * 128, 1792 + (mt - 2) * 128

            def state_dram(base, typ, d, p):
                return base[d, 2 * p:2 * p + 2].rearrange("h k e -> (h k) e")

            xcount = [0]

            def load_x(xap, row0):
                i = xcount[0] % 2
                xcount[0] += 1
                tk.dma("sp", XT[i][:], xap[row0:row0 + 128, :], W=[tXT[i]])
                return i

            for job in jobs:
                kind = job["kind"]
                rope = job["rope"]
                if kind == 1:
                    tk.dma("sp", WOUT[:], wo1_scr, R=[t_wo1scr], W=[tWOUT], st=tWOUT)
                segs = []
                for sq in job["seqs"]:
                    for seg in reversed(range(sq["T"] // SEG)):
                        segs.append((sq, seg))
                rot = [PA, PB, PO0, PO1]
                rcnt = [0]

                def nbank():
                    b = rot[rcnt[0] % 4]
                    rcnt[0] += 1
                    return b

                def seg_loads(si):
                    sq_, seg_ = segs[si]
                    return [load_x(sq_["x"], (seg_ * NT + t_) * C) for t_ in range(NT)]

                def a_norm(xbs):
                    for t_ in range(NT):
                        xb = xbs[t_]
                        c0 = 4 * t_
                        tk.op("act", mk("activation", out=XN2[t_][:], in_=XT[xb][:], func=AF.Square, scale=1.0 / 32.0,
                                        accum_out=STX[:, c0:c0 + 1]), R=[tXT[xb]], W=[tXN2[t_], tSTX[t_]])
                        tk.op("act", mk("activation", out=STX[:, c0 + 1:c0 + 2], in_=STX[:, c0:c0 + 1], func=AF.Ln, bias=EPS),
                              R=[tSTX[t_]], W=[tSTX[t_]])
                        tk.op("act", mk("activation", out=STX[:, c0 + 2:c0 + 3], in_=STX[:, c0 + 1:c0 + 2], func=AF.Exp, scale=-0.5),
                              R=[tSTX[t_]], W=[tSTX[t_]])
                        tk.op("dve", mk("tensor_scalar", out=XN2[t_][:], in0=XT[xb][:], scalar1=STX[:, c0 + 2:c0 + 3], scalar2=None,
                                        op0=ALU.mult), R=[tXT[xb], tSTX[t_]], W=[tXN2[t_]])

                def a_trans(t_, hti):
                    HT = HT2[hti]; tHT = tHT2[hti]
                    for k in range(8):
                        tk.op("pe", mk("transpose", pb16(PT)[:, k * 128:(k + 1) * 128], XN2[t_][:, k * 128:(k + 1) * 128], IDN[:]),
                              R=[tXN2[t_], tCONST], W=[tP[PT]], sig=(k == 7))
                    for k in range(8):
                        if True:
                            tk.op("act", mk("activation", out=HT[:, k, t_ * C:(t_ + 1) * C], in_=pb16(PT)[:, k * 128:(k + 1) * 128],
                                            func=AF.Identity, scale=SC1T[:, k, kind:kind + 1], bias=SHT[:, k, kind:kind + 1]),
                                  R=[tP[PT], tMOD], W=[tHT[t_]])
                        else:
                            tk.op("dve", mk("tensor_scalar", out=HT[:, k, t_ * C:(t_ + 1) * C], in0=pb16(PT)[:, k * 128:(k + 1) * 128],
                                            scalar1=SC1T[:, k, kind:kind + 1], scalar2=SHT[:, k, kind:kind + 1],
                                            op0=ALU.mult, op1=ALU.add), R=[tP[PT], tMOD], W=[tHT[t_]])

                def proj_group(cols, ncols, rows_t=None, wsrc=None, tw=None):
                    b = nbank()
                    HT = HT2[hcur[0]]; tHT = tHT2[hcur[0]]
                    wsrc_ = WIN if wsrc is None else wsrc
                    tw_ = tWIN if tw is None else tw
                    for k in range(8):
                        if rows_t is None:
                            tk.op("pe", mk("matmul", pf(b)[0:ncols, 0:SEG], lhsT=wsrc_[:, k, cols:cols + ncols], rhs=HT[:, k, :],
                                           start=(k == 0), stop=(k == 7)), R=[tw_] + tHT, W=[tP[b]], sig=(k == 7))
                        else:
                            tk.op("pe", mk("matmul", pf(b)[:, 0:ncols], lhsT=HT[:, k, rows_t * C:(rows_t + 1) * C],
                                           rhs=wsrc_[:, k, cols:cols + ncols], start=(k == 0), stop=(k == 7)),
                                  R=[tw_, tHT[rows_t]], W=[tP[b]], sig=(k == 7))
                    return b

                def b_gates_vz(sq, seg, hooks):
                    b = proj_group(3072, 32)
                    tk.op("act", mk("copy", LRT[0:32, :], pf(b)[0:32, 0:SEG]), R=[tP[b]], W=[tLRT])

                    def g1(t_):
                        tk.op("pe", mk("matmul", pf(PL), lhsT=LRT[0:33, t_ * C:(t_ + 1) * C], rhs=WA[0:33, :], start=True, stop=True),
                              R=[tLRT, tWA], W=[tP[PL]])
                        tk.op("act", mk("activation", out=ETMPap, in_=pf(PL), func=AF.Exp, scale=-1.0), R=[tP[PL]], W=[tETMP])
                        tk.op("act", mk("activation", out=SPB[:], in_=ETMPap, func=AF.Ln, bias=1.0), R=[tETMP], W=[tSPB])

                    def g2(t_):
                        chl = sq["ch0"] + seg * NT + t_
                        for d in range(2):
                            for p in range(2):
                                sl = d * 2 + p
                                tk.op("pe", mk("matmul", pf(PS)[:, sl * 128:(sl + 1) * 128], lhsT=SPB[:, d * 256 + p * 128:d * 256 + (p + 1) * 128],
                                               rhs=TRI[:, d, :], start=True, stop=True), R=[tSPB, tCONST], W=[tP[PS]], sig=(sl == 3))
                        bview = pf(PS).rearrange("p (s c) -> p s c", s=4)
                        tk.op("act", mk("activation", out=EQG[:].rearrange("p d a s -> p (d a) s")[:, :, t_ * C:(t_ + 1) * C], in_=bview,
                                        func=AF.Exp, bias=LN8), R=[tP[PS]], W=[tEG[t_]])
                        tk.op("act", mk("activation", out=EKG[:].rearrange("p d a s -> p (d a) s")[:, :, t_ * C:(t_ + 1) * C], in_=bview,
                                        func=AF.Exp, scale=-1.0), R=[tP[PS]], W=[tEG[t_]])
                        tk.op("act", mk("activation", out=DECF[:, chl, 2:4].unsqueeze(2), in_=bview[:, 0:2, C - 1:C], func=AF.Exp),
                              R=[tP[PS]], W=[tDECF[chl]])
                        tk.op("act", mk("activation", out=DECB[:, t_, 2:4].unsqueeze(2), in_=bview[:, 2:4, 0:1], func=AF.Exp),
                              R=[tP[PS]], W=[tDECB[t_]])
                        tk.op("pool", mk("tensor_copy", DECF[:, chl, 0:2], DECR[:, 0:2]), R=[tRETT], W=[tDECF[chl]])
                        tk.op("pool", mk("tensor_copy", DECB[:, t_, 0:2], DECR[:, 2:4]), R=[tRETT], W=[tDECB[t_]])

                    def dgrp(t_, gi):
                        chl = sq["ch0"] + seg * NT + t_
                        sbi = chl % 2
                        cb, typ, oc = ((512, "v", 0), (2048, "v", 512), (1024, "z", 0), (2560, "z", 512))[gi]
                        bank = proj_group(cb, 512, rows_t=t_)
                        if typ == "v":
                            tk.op("act", mk("copy", V[t_][:, oc:oc + 512], pf(bank)), R=[tP[bank]], W=[tV[t_]])
                        else:
                            tk.op("act", mk("activation", out=SZB[sbi][:, oc:oc + 512], in_=pf(bank), func=AF.Silu),
                                  R=[tP[bank]], W=[tSZB[sbi]])
                        if gi == 3:
                            gch = job["scr0"] + chl
                            tk.dma("sp", sz_scr[gch], SZB[sbi][:], R=[tSZB[sbi]], W=[t_szscr[gch]], st=tSZB[sbi])

                    dgrp(0, 2); dgrp(0, 3); dgrp(1, 2); dgrp(1, 3)
                    if "norm" in hooks:
                        hooks["norm"]()
                    dgrp(0, 0); g1(0); dgrp(0, 1); dgrp(1, 0); g2(0); g1(1); dgrp(1, 1)
                    g2(1)

                def c_qk(sq, seg, hooks):
                    tok0 = seg * SEG
                    pend = []

                    def group(mt, which, cbase):
                        p = mt % 2
                        QRAW = QRAW2[0 if which == "k" else 1]; tQRAW = tQRAW2[0 if which == "k" else 1]
                        T1 = T1s[0 if which == "k" else 1]; tT1 = tT1s[0 if which == "k" else 1]
                        T2 = T2s[0 if which == "k" else 1]; tT2 = tT2s[0 if which == "k" else 1]
                        if mt < 2:
                            tabf = (EQR if which == "q" else EKR)[:, 0, p, :].unsqueeze(1).to_broadcast([128, NT, C])
                            tabb = (EQR if which == "q" else EKR)[:, 1, p, :].unsqueeze(1).to_broadcast([128, NT, C])
                            tabR = [tRETT]
                        else:
                            tabf = (EQG if which == "q" else EKG)[:, 0, p, :].rearrange("p (t c) -> p t c", t=NT)
                            tabb = (EQG if which == "q" else EKG)[:, 1, p, :].rearrange("p (t c) -> p t c", t=NT)
                            tabR = list(tEG)
                        chs = [sq["ch0"] + seg * NT + t_ for t_ in range(NT)]
                        if which == "q":
                            outf = QF[:, mt, (sq["ch0"] * C + tok0):(sq["ch0"] * C + tok0 + SEG)].rearrange("p (t c) -> p t c", t=NT)
                            outb = QB[:, mt, :].rearrange("p (t c) -> p t c", t=NT)
                            Wf = [tQF[c_] for c_ in chs]; Wb = [tQB[mt]]
                        else:
                            outf = KF[:, mt, :].rearrange("p (t c) -> p t c", t=NT)
                            outb = KB[:, mt, :].rearrange("p (t c) -> p t c", t=NT)
                            Wf = [tKF[mt]]; Wb = [tKB[mt]]
                        bq = proj_group(cbase, 128)
                        use_rope = rope and mt < 2
                        if use_rope:
                            tk.op("act", mk("copy", QRAW[:], pf(bq)[:, 0:SEG]), R=[tP[bq]], W=[tQRAW])

                        def stage2():
                            if use_rope:
                                br = nbank()
                                tk.op("pe", mk("matmul", pf(br)[:, 0:SEG], lhsT=PERM[:], rhs=QRAW[:], start=True, stop=True),
                                      R=[tCONST, tQRAW], W=[tP[br]])
                                tk.op("dve", mk("tensor_tensor", out=T1[:], in0=pf(br)[:, 0:SEG], in1=SIN[:, tok0:tok0 + SEG], op=ALU.mult),
                                      R=[tP[br], tROPE], W=[tT1])
                                tk.op("dve", mk("tensor_tensor", out=T2[:], in0=pf(bq)[:, 0:SEG], in1=COS[:, tok0:tok0 + SEG], op=ALU.mult),
                                      R=[tP[bq], tROPE], W=[tT2])
                                tk.op("dve", mk("tensor_tensor", out=T2[:], in0=T1[:], in1=T2[:], op=ALU.add), R=[tT1, tT2], W=[tT2])
                                src = T2[:].rearrange("p (t c) -> p t c", t=NT)
                                tk.op("dve", mk("tensor_tensor", out=outf, in0=src, in1=tabf, op=ALU.mult), R=[tT2] + tabR, W=Wf)
                                tk.op("dve", mk("tensor_tensor", out=outb, in0=src, in1=tabb, op=ALU.mult), R=[tT2] + tabR, W=Wb)
                            else:
                                src = pf(bq)[:, 0:SEG].rearrange("p (t c) -> p t c", t=NT)
                                tk.op("dve", mk("tensor_tensor", out=outf, in0=src, in1=tabf, op=ALU.mult), R=[tP[bq]] + tabR, W=Wf)
                                tk.op("dve", mk("tensor_tensor", out=outb, in0=src, in1=tabb, op=ALU.mult), R=[tP[bq]] + tabR, W=Wb)
                        return stage2

                    gi = 0
                    for mt in range(4):
                        cq, ck = col_q(mt)
                        for which, cbase in (("k", ck), ("q", cq)):
                            st2 = group(mt, which, cbase)
                            if pend:
                                pend.pop(0)()
                            pend.append(st2)
                            gi += 1
                            if gi == 2 and "t0" in hooks:
                                hooks["t0"]()
                            if gi == 5 and "t1" in hooks:
                                hooks["t1"]()
                    while pend:
                        pend.pop(0)()

                def e_pre(sq, seg, t):
                    cs = slice(t * C, (t + 1) * C)
                    for mt in range(4):
                        for d in range(2):
                            src = (KF if d == 0 else KB)[:, mt, cs]
                            tk.op("pe", mk("transpose", pb16(PK)[:, (mt * 2 + d) * 128:(mt * 2 + d + 1) * 128], src, IDN[:]),
                                  R=[tKF[mt], tKB[mt], tCONST], W=[tP[PK]], sig=(mt == 3 and d == 1))
                    tk.op("act", mk("copy", KTOK[:].rearrange("p s c -> p (s c)"), pb16(PK)), R=[tP[PK]], W=[tKTOK])

                def e_chunk(sq, seg, t):
                    chl = sq["ch0"] + seg * NT + t
                    obi = chl % 2
                    cs = slice(t * C, (t + 1) * C)
                    sbanks = [(PS, PK), (PA, PT)]

                    def scores(mt):
                        sct = SCT[mt % 2]; tsct = tSCT[mt % 2]
                        qf_ap = QF[:, mt, (chl * C):(chl + 1) * C]
                        for h in range(2):
                            hs = slice(h * 64, (h + 1) * 64)
                            sbank = sbanks[mt % 2][h]
                            for d in range(2):
                                kk = (KF if d == 0 else KB)[hs, mt, cs]
                                qq = qf_ap[hs, :] if d == 0 else QB[hs, mt, cs]
                                tk.op("pe", mk("matmul", pf(sbank)[:, d * 128:(d + 1) * 128], lhsT=kk, rhs=qq, start=True, stop=True),
                                      R=[tKF[mt], tKB[mt], tQB[mt], tQF[chl]], W=[tP[sbank]], sig=(d == 1))
                        for h in range(2):
                            sbank = sbanks[mt % 2][h]
                            tk.op("dve", mk("tensor_tensor", out=sct[:, 2 * h:2 * h + 2, :],
                                            in0=pf(sbank)[:, 0:256].rearrange("p (s c) -> p s c", s=2), in1=MASK[:, 2 * h:2 * h + 2, :],
                                            op=ALU.mult), R=[tP[sbank], tCONST], W=[tsct])

                    def kvp(mt):
                        for h in range(2):
                            hs = slice(h * 64, (h + 1) * 64)
                            hg = mt * 2 + h
                            vv = V[t][:, hg * 128:(hg + 1) * 128]
                            for d in range(2):
                                kvbank = PL if d == 0 else PB
                                tk.op("pe", mk("matmul", pf(kvbank)[hs, mt * 128:(mt + 1) * 128], lhsT=KTOK[:, mt * 2 + d, hs], rhs=vv,
                                               start=True, stop=True), R=[tKTOK, tV[t]], W=[tP[kvbank]], sig=(h == 1 and d == 1))

                    def omm(mt):
                        sct = SCT[mt % 2]; tsct = tSCT[mt % 2]
                        obank = PO0 if mt < 2 else PO1
                        for h in range(2):
                            hs = slice(h * 64, (h + 1) * 64)
                            hg = mt * 2 + h
                            oc = (hg % 4) * 128
                            vv = V[t][:, hg * 128:(hg + 1) * 128]
                            tk.op("pe", mk("matmul", pf(obank)[:, oc:oc + 128], lhsT=sct[:, h * 2, :], rhs=vv, start=True, stop=False),
                                  R=[tsct, tV[t]], W=[tP[obank]], sig=False)
                            zero_state = (not sq["s0"]) and seg == sq["T"] // SEG - 1 and t == NT - 1
                            tk.op("pe", mk("matmul", pf(obank)[:, oc:oc + 128], lhsT=sct[:, h * 2 + 1, :], rhs=vv, start=False, stop=zero_state),
                                  R=[tsct, tV[t]], W=[tP[obank]], sig=(zero_state and h == 1))
                            if not zero_state:
                                tk.op("pe", mk("matmul", pf(obank)[:, oc:oc + 128], lhsT=QB[hs, mt, cs], rhs=SNAP[hs, mt, 1, :], start=False, stop=True),
                                      R=[tQB[mt], tSNAP[1]], W=[tP[obank]], sig=(h == 1))

                    scores(0)
                    kvp(0)
                    for mt in range(4):
                        if mt + 1 < 4:
                            scores(mt + 1)
                            kvp(mt + 1)
                        omm(mt)
                    tk.op("act", mk("copy", OPB[obi][:, 0:512], pf(PO0)), R=[tP[PO0]], W=[tOPB[obi]])
                    tk.op("act", mk("copy", OPB[obi][:, 512:1024], pf(PO1)), R=[tP[PO1]], W=[tOPB[obi]])
                    gch = job["scr0"] + chl
                    tk.dma("sp", op_scr[gch], OPB[obi][:], R=[tOPB[obi]], W=[t_opscr[gch]], st=tOPB[obi])
                    tk.op("act", mk("copy", KVF[:, chl, :, :], pf(PL).rearrange("p (m e) -> p m e", m=4)), R=[tP[PL]], W=[tKVF[chl]])
                    tk.op("dve", mk("tensor_tensor", out=S[:, :, 1, :], in0=pf(PB).rearrange("p (m e) -> p m e", m=4), in1=S[:, :, 1, :], op=ALU.add),
                          R=[tP[PB], tS[1]], W=[tS[1]])
                    tk.op("pool", mk("tensor_tensor", out=S[:, :, 1, :], in0=S[:, :, 1, :], in1=DECB[:, t, :].unsqueeze(2).to_broadcast([128, 4, 128]),
                                     op=ALU.mult), R=[tS[1], tDECB[t]], W=[tS[1]])
                    tk.op("act", mk("copy", SNAP[:, :, 1, :], S[:, :, 1, :]), R=[tS[1]], W=[tSNAP[1]])

                xbs_next = seg_loads(0)
                a_norm(xbs_next)
                if len(segs) > 1:
                    xbs_next = seg_loads(1)
                for t_ in range(NT):
                    a_trans(t_, 0)
                for si, (sq, seg) in enumerate(segs):
                    nseg = sq["T"] // SEG
                    hcur[0] = si % 2
                    if seg == nseg - 1:
                        for mt in range(4):
                            if sq["s0"]:
                                base = st_ret if mt < 2 else st_gla
                                tk.dma("sp", S[:, mt, 1, :], state_dram(base, mt // 2, 1, mt % 2), W=[tS[1]])
                            else:
                                tk.op("pool", mk("memset", S[:, mt, 1, :], 0.0), W=[tS[1]])
                        tk.op("pool", mk("tensor_copy", SNAP[:, :, 1, :], S[:, :, 1, :]), R=[tS[1]], W=[tSNAP[1]])
                    has_next = si + 1 < len(segs)
                    hooks = {}
                    if has_next:
                        def hk_norm():
                            global_xbs = hooks["xbs"]
                            a_norm(global_xbs)
                        hooks["xbs"] = xbs_next
                        hooks["norm"] = hk_norm
                        hooks["t0"] = lambda nh=(si + 1) % 2: a_trans(0, nh)
                        hooks["t1"] = lambda nh=(si + 1) % 2: a_trans(1, nh)
                    b_gates_vz(sq, seg, hooks)
                    if has_next and si + 2 < len(segs):
                        xbs_next = seg_loads(si + 2)
                    c_qk(sq, seg, hooks)
                    e_pre(sq, seg, 1)
                    e_chunk(sq, seg, 1)
                    e_pre(sq, seg, 0)
                    e_chunk(sq, seg, 0)
                    if seg == 0 and sq["so"] is not None:
                        for mt in range(4):
                            dst = (nsr if mt < 2 else nsg)[sq["so"]]
                            tk.dma("sp", state_dram(dst, mt // 2, 1, mt % 2), S[:, mt, 1, :], R=[tS[1]], st=tS[1], is_out=True)


                stage("p1")
                ftiles = []
                for sq in job["seqs"]:
                    for cis in range(sq["T"] // C):
                        ftiles.append((sq, cis))
                nft = len(ftiles)
                PObanks = [(PO0, PO1), (PT, PL)]
                xb3 = {}

                def ld_x(i):
                    sq_, cis_ = ftiles[i]
                    xb3[i] = load_x(sq_["x"], cis_ * C)

                def ld_sz(i):
                    sq_, cis_ = ftiles[i]
                    chl_ = sq_["ch0"] + cis_
                    gch_ = job["scr0"] + chl_
                    bi_ = chl_ % 2
                    tk.dma("sp", SZB[bi_][:], sz_scr[gch_], R=[t_szscr[gch_]], W=[tSZB[bi_]], st=tSZB[bi_])

                def ld_op(i):
                    sq_, cis_ = ftiles[i]
                    chl_ = sq_["ch0"] + cis_
                    gch_ = job["scr0"] + chl_
                    bi_ = chl_ % 2
                    tk.dma("sp", OPB[bi_][:], op_scr[gch_], R=[t_opscr[gch_]], W=[tOPB[bi_]], st=tOPB[bi_])

                def s1(i):
                    sq, cis = ftiles[i]
                    chl = sq["ch0"] + cis
                    bi = chl % 2
                    pob = PObanks[i % 2]
                    if cis == 0:
                        for mt in range(4):
                            if sq["s0"]:
                                base = st_ret if mt < 2 else st_gla
                                tk.dma("sp", S[:, mt, 0, :], state_dram(base, mt // 2, 0, mt % 2), W=[tS[0]])
                            else:
                                tk.op("pool", mk("memset", S[:, mt, 0, :], 0.0), W=[tS[0]])
                    tk.op("act", mk("copy", SNAP[:, :, 0, :], S[:, :, 0, :]), R=[tS[0]], W=[tSNAP[0]])
                    for hg in range(8):
                        obank = pob[hg // 4]
                        hh = hg % 4
                        mt = hg // 2
                        hs = slice((hg % 2) * 64, (hg % 2 + 1) * 64)
                        zero_f = (cis == 0) and not sq["s0"]
                        tk.op("pe", mk("matmul", pf(obank)[:, hh * 128:(hh + 1) * 128], lhsT=IDN[:], rhs=OPB[bi][:, hg * 128:(hg + 1) * 128],
                                       start=True, stop=zero_f), R=[tCONST, tOPB[bi]], W=[tP[obank]], sig=(zero_f and hh == 3))
                        if not zero_f:
                            tk.op("pe", mk("matmul", pf(obank)[:, hh * 128:(hh + 1) * 128], lhsT=QF[hs, mt, chl * C:(chl + 1) * C],
                                           rhs=SNAP[hs, mt, 0, :], start=False, stop=True),
                                  R=[tQF[chl], tSNAP[0]], W=[tP[obank]], sig=(hh == 3))
                    tk.op("pool", mk("tensor_tensor", out=S[:, :, 0, :], in0=KVF[:, chl, :, :], in1=S[:, :, 0, :], op=ALU.add),
                          R=[tKVF[chl], tS[0]], W=[tS[0]])
                    tk.op("pool", mk("tensor_tensor", out=S[:, :, 0, :], in0=S[:, :, 0, :], in1=DECF[:, chl, :].unsqueeze(2).to_broadcast([128, 4, 128]),
                                     op=ALU.mult), R=[tS[0], tDECF[chl]], W=[tS[0]])
                    if cis == sq["T"] // C - 1 and sq["so"] is not None:
                        for mt in range(4):
                            dst = (nsr if mt < 2 else nsg)[sq["so"]]
                            tk.dma("sp", state_dram(dst, mt // 2, 0, mt % 2), S[:, mt, 0, :], R=[tS[0]], st=tS[0], is_out=True)

                def s2(i):
                    sq, cis = ftiles[i]
                    chl = sq["ch0"] + cis
                    bi = chl % 2
                    po_r, po_g = PObanks[i % 2]
                    OG = OG2[i % 2]; tOG = tOG2[i % 2]
                    for hh in range(4):
                        tk.op("dve", mk("bn_stats", out=ST[:, 4 + hh * 6:10 + hh * 6], in_=pf(po_r)[:, hh * 128:(hh + 1) * 128]),
                              R=[tP[po_r]], W=[tSTr])
                    for hh in range(4):
                        tk.op("act", mk("activation", out=ON[:, hh * 128:(hh + 1) * 128], in_=pf(po_g)[:, hh * 128:(hh + 1) * 128], func=AF.Square,
                                        scale=1.0 / math.sqrt(128.0), accum_out=ST[:, 48 + hh:49 + hh]), R=[tP[po_g]], W=[tON, tSTg])
                    for hh in range(4):
                        tk.op("dve", mk("bn_aggr", out=ST[:, 28 + hh * 2:30 + hh * 2], in_=ST[:, 4 + hh * 6:10 + hh * 6]), R=[tSTr], W=[tSTr])

                def s2b(i):
                    sq, cis = ftiles[i]
                    chl = sq["ch0"] + cis
                    bi = chl % 2
                    po_r, po_g = PObanks[i % 2]
                    OG = OG2[i % 2]; tOG = tOG2[i % 2]
                    mvv = ST[:, 28:36].rearrange("p (h t) -> p h t", t=2)
                    tk.op("act", mk("activation", out=ST[:, 36:40].unsqueeze(2), in_=mvv[:, :, 1:2], func=AF.Ln, bias=EPS), R=[tSTr], W=[tSTr])
                    tk.op("act", mk("activation", out=ST[:, 40:44], in_=ST[:, 36:40], func=AF.Exp, scale=-0.5), R=[tSTr], W=[tSTr])
                    tk.op("dve", mk("scalar_tensor_tensor", out=ST[:, 44:48].unsqueeze(2), in0=mvv[:, :, 0:1], scalar=-1.0, in1=ST[:, 40:44].unsqueeze(2),
                                    op0=ALU.mult, op1=ALU.mult), R=[tSTr], W=[tSTr])
                    tk.op("act", mk("activation", out=ST[:, 52:56], in_=ST[:, 48:52], func=AF.Ln, bias=EPS), R=[tSTg], W=[tSTg])
                    tk.op("act", mk("activation", out=ST[:, 56:60], in_=ST[:, 52:56], func=AF.Exp, scale=-0.5), R=[tSTg], W=[tSTg])
                    for hh in range(4):
                        tk.op("act", mk("activation", out=ON[:, hh * 128:(hh + 1) * 128], in_=pf(po_r)[:, hh * 128:(hh + 1) * 128], func=AF.Identity,
                                        scale=ST[:, 40 + hh:41 + hh], bias=ST[:, 44 + hh:45 + hh]), R=[tP[po_r], tSTr], W=[tON])
                    for hh in range(4):
                        tk.op("dve", mk("scalar_tensor_tensor", out=OG[:, 512 + hh * 128:512 + (hh + 1) * 128], in0=pf(po_g)[:, hh * 128:(hh + 1) * 128],
                                        scalar=ST[:, 56 + hh:57 + hh], in1=SZB[bi][:, 512 + hh * 128:512 + (hh + 1) * 128],
                                        op0=ALU.mult, op1=ALU.mult), R=[tP[po_g], tSTg, tSZB[bi]], W=[tOG])
                    tk.op("pool", mk("tensor_tensor", out=OG[:, 0:512], in0=ON[:], in1=SZB[bi][:, 0:512], op=ALU.mult),
                          R=[tON, tSZB[bi]], W=[tOG])

                def s3(i):
                    OG = OG2[i % 2]; tOG = tOG2[i % 2]
                    for k in range(8):
                        tk.op("pe", mk("transpose", pb16(PK)[:, k * 128:(k + 1) * 128], OG[:, k * 128:(k + 1) * 128], IDN[:]),
                              R=[tOG, tCONST], W=[tP[PK]], sig=(k == 7))
                    tk.op("act", mk("copy", OGT[:].rearrange("p k c -> p (k c)"), pb16(PK)), R=[tP[PK]], W=[tOGT])

                def s4mm(i):
                    for half in range(2):
                        bank = PA if half == 0 else PB
                        for k in range(8):
                            tk.op("pe", mk("matmul", pf(bank), lhsT=OGT[:, k, :], rhs=WOUT[:, k, half * 512:(half + 1) * 512],
                                           start=(k == 0), stop=(k == 7)), R=[tOGT, tWOUT], W=[tP[bank]], sig=(k == 7))

                def s4add(i):
                    xb = xb3[i]
                    for half in range(2):
                        bank = PA if half == 0 else PB
                        tk.op("dve", mk("tensor_tensor", out=Y[:, half * 512:(half + 1) * 512], in0=pf(bank),
                                        in1=XT[xb][:, half * 512:(half + 1) * 512], op=ALU.add), R=[tP[bank], tXT[xb]], W=[tY])

                def s4y_act(i):
                    tk.op("act", mk("activation", out=XN2[0][:], in_=Y[:], func=AF.Square, scale=1.0 / 32.0, accum_out=ST[:, 60:61]), R=[tY], W=[tXN2[0], tSTy])
                    tk.op("act", mk("activation", out=ST[:, 61:62], in_=ST[:, 60:61], func=AF.Ln, bias=EPS), R=[tSTy], W=[tSTy])
                    tk.op("act", mk("activation", out=ST[:, 62:63], in_=ST[:, 61:62], func=AF.Exp, scale=-0.5), R=[tSTy], W=[tSTy])

                def s4y_dve(i):
                    sq, cis = ftiles[i]
                    tk.op("dve", mk("scalar_tensor_tensor", out=YOap, in0=Y[:], scalar=ST[:, 62:63], in1=FNW[:], op0=ALU.mult, op1=ALU.mult),
                          R=[tY, tSTy, tFNW], W=[tV[0], tV[1]])
                    tk.dma("sp", sq["y"][cis * C:(cis + 1) * C, :], YOap, R=[tV[0], tV[1]], st=tV[0], is_out=True)
                    stage("p3")

                ld_op(0); ld_sz(0)
                if nft > 1:
                    ld_op(1)
                s1(0)
                for j in range(nft + 2):
                    if j + 2 < nft:
                        ld_op(j + 2)
                    if j + 1 < nft:
                        ld_sz(j + 1)
                    if j < nft:
                        ld_x(j)
                    if 0 <= j - 1 < nft:
                        s4mm(j - 1)
                    if j + 1 < nft:
                        s1(j + 1)
                    if j < nft:
                        s2(j)
                    if 0 <= j - 2 < nft:
                        s4y_dve(j - 2)
                    if j < nft:
                        s2b(j)
                    if 0 <= j - 1 < nft:
                        s4add(j - 1)
                    if j < nft:
                        s3(j)
                    if 0 <= j - 1 < nft:
                        s4y_act(j - 1)

        try:
            _main_scope()
        except _Stop:
            pass
        tk.finish("sp")
        with nc.Block() as block:
            tk.replay(block)
    nc._dbg_outs = dbg_outs
    nc._ninst = {k: v.ninst for k, v in tk.E.items()}
    return nc


def _host_consts():
    bf = ml_dtypes.bfloat16
    ii = np.arange(C)
    idn = np.eye(128, dtype=np.float32).astype(bf)
    tri = np.zeros((128, 2, 128), np.float32)
    tri[:, 0, :] = np.where(ii[:, None] <= ii[None, :], -1.0 / 16.0, 0.0)
    tri[:, 1, :] = np.where(ii[:, None] >= ii[None, :], -1.0 / 16.0, 0.0)
    mask = np.zeros((128, 4, 128), np.float32)
    mf = (ii[:, None] <= ii[None, :]).astype(np.float32)
    mb = (ii[:, None] >= ii[None, :]).astype(np.float32)
    for h in range(2):
        mask[:, h * 2 + 0, :] = mf
        mask[:, h * 2 + 1, :] = mb
    pos = np.zeros((128, 2, 128), np.float32)
    pos[:, 0, :] = (ii + 1)[None, :]
    pos[:, 1, :] = (C - ii)[None, :]
    t = np.arange(TS)
    rr = (t // 64).astype(np.float32)
    cc = (t % 64).astype(np.float32)
    inv = (10000.0 ** (-np.arange(16, dtype=np.float32) / 16)).astype(np.float32)
    ang = np.concatenate([rr[:, None] * inv, cc[:, None] * inv], axis=-1).astype(np.float32)
    cosT = np.cos(ang).T.astype(np.float32)
    sinT = np.sin(ang).T.astype(np.float32)
    cos128 = np.ascontiguousarray(np.tile(cosT, (4, 1)).astype(np.float32))
    sin128 = np.ascontiguousarray(np.tile(sinT, (4, 1)).astype(np.float32))
    i2 = np.eye(2, dtype=np.float32)
    perm = np.zeros((128, 128), np.float32)
    for m in range(128):
        if m % 64 < 32:
            perm[m + 32, m] = -1.0
        else:
            perm[m - 32, m] = 1.0
    sel = np.zeros((2, 2, 128), np.float32)
    sel[0, 0, :] = 1.0
    sel[1, 1, :] = 1.0
    return dict(c_idn=idn, c_tri=tri.astype(bf), c_mask=mask.astype(bf), c_pos=pos, c_cos=cos128, c_sin=sin128, c_perm=perm.astype(bf), c_i2=i2, c_sel=sel)


_NC_CACHE = {}


def kernel(x_prompt, x_sample, c, state_ret, state_gla, c_ctx, w_mod, b_mod, w_in,
           ret_log_decay, gla_w_alpha, gla_b_alpha, gla_norm_w, w_out, final_norm_w, _dbg=None):
    f = lambda a: np.ascontiguousarray(np.asarray(a, dtype=np.float32))
    x_prompt, x_sample, c, state_ret, state_gla, c_ctx = map(f, (x_prompt, x_sample, c, state_ret, state_gla, c_ctx))
    w_mod, b_mod, w_in, ret_log_decay = map(f, (w_mod, b_mod, w_in, ret_log_decay))
    gla_w_alpha, gla_b_alpha, gla_norm_w, w_out, final_norm_w = map(f, (gla_w_alpha, gla_b_alpha, gla_norm_w, w_out, final_norm_w))

    key = None if _dbg is None else tuple(sorted(_dbg))
    if key not in _NC_CACHE:
        _NC_CACHE[key] = build_nc(_dbg)
    nc = _NC_CACHE[key]
    consts = _host_consts()

    ld = ret_log_decay[0]
    ldcols = np.zeros((128, 4), np.float32)
    for d in range(2):
        for p in range(2):
            ldcols[0:64, d * 2 + p] = ld[d, 2 * p]
            ldcols[64:128, d * 2 + p] = ld[d, 2 * p + 1]
    wa_aug = np.zeros((33, 512), np.float32)
    wa_aug[0:16, 0:256] = gla_w_alpha[0, 0]
    wa_aug[16:32, 256:512] = gla_w_alpha[0, 1]
    wa_aug[32, 0:256] = gla_b_alpha[0, 0]
    wa_aug[32, 256:512] = gla_b_alpha[0, 1]
    rowsc = np.ones((128, 8), np.float32)
    rowsc[:, 4:8] = gla_norm_w[0][:, None]
    fnw_bc = np.ascontiguousarray(np.broadcast_to(final_norm_w[None, :], (128, D)))
    b_mod2 = np.ascontiguousarray(np.broadcast_to(b_mod[0][None, :], (2, 3 * D)))
    shared = dict(w_mod=w_mod[0], b_mod2=b_mod2, w_in=w_in[0], ldcols=ldcols, wa_aug=wa_aug, rowsc=rowsc,
                  w_out=w_out[0], fnw_bc=fnw_bc, **consts)

    in_maps = []
    for core in range(NCORES):
        cv = np.stack([c_ctx, c[core]], axis=0)
        cvecT = np.ascontiguousarray(cv.reshape(2, 8, 128).transpose(2, 1, 0).reshape(128, 16))
        m = dict(shared)
        m.update(xp=np.ascontiguousarray(x_prompt[core * NPS:(core + 1) * NPS].reshape(NPS * TP, D)),
                 xs=np.ascontiguousarray(x_sample[core]),
                 cvecT=cvecT,
                 st_ret=np.ascontiguousarray(state_ret[core, 0]),
                 st_gla=np.ascontiguousarray(state_gla[core, 0]))
        in_maps.append(m)

    res = run_bass_kernel_spmd(nc, in_maps, core_ids=list(range(NCORES)))
    rs = res.results
    y_prompt = np.concatenate([r["yp"].reshape(NPS, TP, D) for r in rs], axis=0).astype(np.float32)
    y_sample = np.stack([r["ys"] for r in rs], axis=0).astype(np.float32)
    new_ret = np.concatenate([r["nsr"].reshape(NPS, 1, 2, 4, 64, 128) for r in rs], axis=0).astype(np.float32)
    new_gla = np.concatenate([r["nsg"].reshape(NPS, 1, 2, 4, 64, 128) for r in rs], axis=0).astype(np.float32)
    if _dbg is not None:
        kernel._dbg_results = [{k: r[v] for k, v in nc._dbg_outs.items()} for r in rs]
    return (y_prompt, y_sample, new_ret, new_gla)
```
